# Optimizing a Trainium2 kernel written in Bass

```python
import jax, jax.numpy as jnp
from jax import lax
import numpy as np

D_MODEL = 4096
BATCH = 2
SEQ = 4096
DEPTH = 2

HEAD_DIM = 128
D_MIX = D_MODEL
D_ATTN = D_MIX // 2
D_SGU = D_MIX - D_ATTN
N_ATTN_HEADS = D_ATTN // HEAD_DIM
SGU_GROUP_DIM = 128
N_SGU_GROUPS = D_SGU // SGU_GROUP_DIM
SGU_CHUNK = 128
DILATED_CONFIGS = ((128, 1), (512, 4), (2048, 16))
BLOCK = 128
N_IN = 3 * D_ATTN + 2 * D_SGU
D_FF = 256 * ((8 * D_MODEL // 3 + 255) // 256)
CONV_WIDTH = 3
NORM_GROUP = 128
NORM_EPS = 1e-6

kernel_name = "hybrid_dilated_attn_gmlp_convffn"


def rmsnorm(x, g):
    xf = x.astype(jnp.float32)
    y = xf * lax.rsqrt(jnp.mean(xf * xf, axis=-1, keepdims=True) + NORM_EPS)
    return (y * g.astype(jnp.float32)).astype(x.dtype)


def group_rmsnorm(x, g, group):
    shp = x.shape
    xf = x.astype(jnp.float32).reshape(shp[:-1] + (shp[-1] // group, group))
    y = xf * lax.rsqrt(jnp.mean(xf * xf, axis=-1, keepdims=True) + NORM_EPS)
    return (y.reshape(shp) * g.astype(jnp.float32)).astype(x.dtype)


def alibi_slopes(n):
    return 2.0 ** (-8.0 * jnp.arange(1, n + 1, dtype=jnp.float32) / n)


def dilated_window_branch(q, k, v, slopes, window, dilation):
    B, H, S, E = q.shape
    d = dilation
    span = window // d
    L = S // d
    nb = -(-L // BLOCK)
    lp = nb * BLOCK

    def strided(t):
        t = t.reshape(B, H, L, d, E).transpose(0, 1, 3, 2, 4)
        t = jnp.pad(t, ((0, 0), (0, 0), (0, 0), (0, lp - L), (0, 0)))
        return t.reshape(B, H, d, nb, BLOCK, E)

    def with_prev(t):
        prev = jnp.pad(t[:, :, :, :-1], ((0, 0), (0, 0), (0, 0), (1, 0), (0, 0), (0, 0)))
        return jnp.concatenate([prev, t], axis=4)

    qb = strided(q)
    kk = with_prev(strided(k))
    vv = with_prev(strided(v))

    s = jnp.einsum('bhrnqe,bhrnke->bhrnqk', qb, kk).astype(jnp.float32) * (HEAD_DIM ** -0.5)
    qi = jnp.arange(BLOCK)[:, None]
    kj = jnp.arange(2 * BLOCK)[None, :]
    dist = qi + BLOCK - kj
    key_idx = jnp.arange(nb)[:, None, None] * BLOCK + kj[None] - BLOCK
    valid = (dist >= 0) & (dist <= span) & (key_idx >= 0)
    bias = -slopes[:, None, None, None, None] * (dist * d).astype(jnp.float32)[None, None, None]
    s = jnp.where(valid, s + bias, jnp.finfo(jnp.float32).min)
    lse = jax.nn.logsumexp(s, axis=-1)
    p = jnp.exp(s - lse[..., None])
    o = jnp.einsum('bhrnqk,bhrnke->bhrnqe', p, vv.astype(jnp.float32))

    def unstrided(t):
        rest = t.shape[5:]
        t = t.reshape((B, H, d, lp) + rest)[:, :, :, :L]
        t = jnp.swapaxes(t, 2, 3)
        return t.reshape((B, H, S) + rest)

    return unstrided(o), unstrided(lse)


def dilated_attention(q, k, v):
    slopes = alibi_slopes(q.shape[1])
    outs, lses = [], []
    for window, dilation in DILATED_CONFIGS:
        o, lse = dilated_window_branch(q, k, v, slopes, window, dilation)
        outs.append(o)
        lses.append(lse)
    wts = jax.nn.softmax(jnp.stack(lses, axis=0), axis=0)
    o = jnp.einsum('cbhs,cbhse->bhse', wts, jnp.stack(outs, axis=0))
    return o.astype(q.dtype)


def spatial_gating(z, norm_g, w_s, b_s):
    B, S, _ = z.shape
    u, v = jnp.split(z, 2, axis=-1)
    v = group_rmsnorm(v, norm_g, SGU_GROUP_DIM)
    v = v.reshape(B, S // SGU_CHUNK, SGU_CHUNK, N_SGU_GROUPS, SGU_GROUP_DIM)
    w = jnp.tril(w_s)
    mixed = jnp.einsum('gts,bnsgc->bntgc', w, v) + b_s.T[None, None, :, :, None]
    return u * mixed.reshape(B, S, D_SGU)


def mixing_sublayer(h, w_in, sgu_norm_g, w_spatial, b_spatial, mix_norm_g, w_out):
    B, S, _ = h.shape
    z = h @ w_in
    q, k, v, zg = jnp.split(z, [D_ATTN, 2 * D_ATTN, 3 * D_ATTN], axis=-1)

    def heads(t):
        return t.reshape(B, S, N_ATTN_HEADS, HEAD_DIM).transpose(0, 2, 1, 3)

    a = dilated_attention(heads(q), heads(k), heads(v))
    a = a.transpose(0, 2, 1, 3).reshape(B, S, D_ATTN)
    g = spatial_gating(jax.nn.gelu(zg, approximate=False), sgu_norm_g, w_spatial, b_spatial)
    m = jnp.concatenate([a, g], axis=-1)
    m = group_rmsnorm(m, mix_norm_g, NORM_GROUP)
    return m @ w_out


def causal_depthwise_conv(h, w, b):
    S = h.shape[1]
    hp = jnp.pad(h, ((0, 0), (CONV_WIDTH - 1, 0), (0, 0)))
    y = b + w[0] * hp[:, 0:S]
    for j in range(1, CONV_WIDTH):
        y = y + w[j] * hp[:, j:j + S]
    return y


def conv_ffn(h, w_up, conv_w, conv_b, w_down):
    up = causal_depthwise_conv(h @ w_up, conv_w, conv_b)
    gate, val = jnp.split(up, 2, axis=-1)
    return (jax.nn.silu(gate) * val) @ w_down


def setup_inputs(seed: int = 0) -> dict:
    key = jax.random.key(seed)
    ks = jax.random.split(key, 16)
    f32 = jnp.float32
    L = DEPTH
    nrm = lambda k, shp, s: jax.random.normal(k, shp, f32) * s
    return {
        "x": jax.random.normal(ks[0], (BATCH, SEQ, D_MODEL), f32),
        "attn_norm_g": 1.0 + nrm(ks[1], (L, D_MODEL), 0.02),
        "w_in": nrm(ks[2], (L, D_MODEL, N_IN), D_MODEL ** -0.5),
        "sgu_norm_g": 1.0 + nrm(ks[3], (L, D_SGU), 0.02),
        "w_spatial": nrm(ks[4], (L, N_SGU_GROUPS, SGU_CHUNK, SGU_CHUNK), SGU_CHUNK ** -0.5),
        "b_spatial": nrm(ks[5], (L, N_SGU_GROUPS, SGU_CHUNK), 0.02),
        "mix_norm_g": 1.0 + nrm(ks[6], (L, D_MIX), 0.02),
        "w_out": nrm(ks[7], (L, D_MIX, D_MODEL), D_MIX ** -0.5),
        "ffn_norm_g": 1.0 + nrm(ks[8], (L, D_MODEL), 0.02),
        "w_up": nrm(ks[9], (L, D_MODEL, 2 * D_FF), D_MODEL ** -0.5),
        "conv_w": nrm(ks[10], (L, CONV_WIDTH, 2 * D_FF), CONV_WIDTH ** -0.5),
        "conv_b": nrm(ks[11], (L, 2 * D_FF), 0.02),
        "w_down": nrm(ks[12], (L, D_FF, D_MODEL), D_FF ** -0.5),
        "final_norm_g": 1.0 + nrm(ks[13], (D_MODEL,), 0.02),
    }


def reference(x, attn_norm_g, w_in, sgu_norm_g, w_spatial, b_spatial, mix_norm_g, w_out,
              ffn_norm_g, w_up, conv_w, conv_b, w_down, final_norm_g):
    for i in range(DEPTH):
        h = rmsnorm(x, attn_norm_g[i])
        x = x + mixing_sublayer(h, w_in[i], sgu_norm_g[i], w_spatial[i], b_spatial[i],
                                mix_norm_g[i], w_out[i])
        h = rmsnorm(x, ffn_norm_g[i])
        x = x + conv_ffn(h, w_up[i], conv_w[i], conv_b[i], w_down[i])
    return rmsnorm(x, final_norm_g)
```

```python
import numpy as np
import ml_dtypes
from contextlib import ExitStack
import concourse.bass as bass
import concourse.mybir as mybir
from concourse.bass_utils import run_bass_kernel_spmd

F32 = mybir.dt.float32
BF16 = mybir.dt.bfloat16
I32 = mybir.dt.int32
AF = mybir.ActivationFunctionType
ALU = mybir.AluOpType
NPBF = ml_dtypes.bfloat16

NCORES = 8
T = 1024
D = 4096
KC = 32
DFF = 11008
FC = 86
SLABS = [(0, 22), (22, 22), (44, 21), (65, 21)]
EPS = 1e-6
SCALE = 128.0 ** -0.5


_POOL = {}
_UID = [0]


class Sem:
    def __new__(cls, nc, es, name, step=1):
        key = (id(nc), name)
        if key in _POOL:
            return _POOL[key]
        o = object.__new__(cls)
        o.h = nc.alloc_semaphore(name=name)
        o.step = step
        o.n = 0
        _POOL[key] = o
        return o

    def __init__(self, nc, es, name, step=1):
        pass


class Prog:
    ENG = ("pe", "act", "dve", "pool", "sp")

    def __init__(self):
        self.q = {k: [] for k in self.ENG}
        self.waited = {k: {} for k in self.ENG}

    def wait(self, eng, w):
        if w is None:
            return
        s, v = w
        if v <= 0:
            return
        if v > self.waited[eng].get(id(s), 0):
            self.waited[eng][id(s)] = v
            self.q[eng].append(lambda e, s=s, v=v: e.wait_ge(s.h, v))

    def op(self, eng, fn, waits=(), inc=None):
        for w in waits:
            self.wait(eng, w)
        if inc is None:
            self.q[eng].append(fn)
            return None
        inc.n += inc.step
        self.q[eng].append(lambda e, fn=fn, inc=inc: fn(e).then_inc(inc.h, inc.step))
        return (inc, inc.n)

    def run(self, nc, dyn=None):
        q = self.q
        with nc.Block() as block:
            @block.tensor
            def _(e):
                for f in q["pe"]:
                    f(e)

            @block.scalar
            def _(e):
                for f in q["act"]:
                    f(e)

            @block.vector
            def _(e):
                for f in q["dve"]:
                    f(e)

            @block.gpsimd
            def _(e):
                for f in q["pool"]:
                    f(e)

            @block.sync
            def _(e):
                for f in q["sp"]:
                    f(e)


class Ring:
    def __init__(self, bufs, sems=None):
        self.bufs = bufs
        self.free_at = [None] * len(bufs)
        self.sems = sems
        self.i = 0

    def all_done(self):
        return [(sm, sm.n) for sm in self.sems]

    def acquire(self):
        i = self.i
        self.i = (i + 1) % len(self.bufs)
        return i, self.bufs[i], self.free_at[i]

    def release(self, i, w):
        self.free_at[i] = w


def sbt(nc, es, name, shape, dt):
    return es.enter_context(nc.sbuf_tensor(f"s_{name}_{_UID[0]}", shape, dt))


def dsems(nc, es, tag, n):
    return [Sem(nc, es, f"{tag}{i}", 16) for i in range(n)]


def emit_rmsnorm(P, S, x_src, g_sb, xs_ring, sq_ring, ones, rstd, psA, psB, pe_waits, dst_fn,
                 x_ready=None):
    last_pe = None
    for c in range(KC):
        i, xb, fw = xs_ring.acquire()
        ld = P.op("sp", lambda e, xb=xb, c=c: e.dma_start(out=xb[:], in_=x_src(c)),
                  waits=[fw, x_ready], inc=xs_ring.sems[i])
        j, sqb, sfw = sq_ring.acquire()
        a = P.op("act", lambda e, xb=xb, sqb=sqb: e.activation(out=sqb[:], in_=xb[:], func=AF.Square),
                 waits=[ld, sfw], inc=S["act"])
        xs_ring.release(i, a)
        w0 = [a] + (list(pe_waits) if c == 0 else [])
        P.op("pe", lambda e, sqb=sqb, c=c: e.matmul(psA[:], lhsT=ones[:], rhs=sqb[:, 0:512],
                                                    start=(c == 0), stop=(c == KC - 1)), waits=w0)
        last_pe = P.op("pe", lambda e, sqb=sqb, c=c: e.matmul(psB[:], lhsT=ones[:], rhs=sqb[:, 512:1024],
                                                              start=(c == 0), stop=(c == KC - 1)),
                       inc=S["pe"])
        sq_ring.release(j, last_pe)
    P.op("act", lambda e: e.activation(out=rstd[:, 0:512], in_=psA[:], func=AF.Sqrt, bias=EPS, scale=1.0 / D),
         waits=[last_pe])
    a = P.op("act", lambda e: e.activation(out=rstd[:, 512:1024], in_=psB[:], func=AF.Sqrt, bias=EPS,
                                           scale=1.0 / D), inc=S["act"])
    ps_rel = a
    r = P.op("dve", lambda e: e.reciprocal(out=rstd[:], in_=rstd[:]), waits=[a], inc=S["dve"])
    last = None
    for c in range(KC):
        i, xb, fw = xs_ring.acquire()
        ld = P.op("sp", lambda e, xb=xb, c=c: e.dma_start(out=xb[:], in_=x_src(c)), waits=[fw],
                  inc=xs_ring.sems[i])
        last = P.op("dve", lambda e, xb=xb, c=c: e.scalar_tensor_tensor(
            out=dst_fn(c), in0=xb[:], scalar=g_sb[:, c:c + 1], in1=rstd[:], op0=ALU.mult, op1=ALU.mult),
            waits=[ld], inc=S["dve"])
        xs_ring.release(i, last)
    return last, ps_rel


class WStream:
    def __init__(self, P, S, bufs, sems):
        self.P, self.S = P, S
        self.ring = Ring(bufs, sems)

    def load(self, src_ap, nkc):
        i, wb, fw = self.ring.acquire()
        ld = self.P.op("pool", lambda e, wb=wb: e.dma_start(out=wb[:, 0:nkc, :], in_=src_ap),
                       waits=[fw], inc=self.ring.sems[i])
        return i, wb, ld

    def release(self, i, w):
        self.ring.release(i, w)


def mm_group(P, S, ps, n, lhsT_fn, rhs_fn, nk, waits, out_ap=None):
    o = out_ap if out_ap is not None else ps[:, 0:n]
    r = None
    for k in range(nk):
        w = waits if k == 0 else ()
        fn = lambda e, k=k: e.matmul(o, lhsT=lhsT_fn(k), rhs=rhs_fn(k), start=(k == 0), stop=(k == nk - 1))
        if k == nk - 1:
            r = P.op("pe", fn, waits=w, inc=S["pe"])
        else:
            P.op("pe", fn, waits=w)
    return r


def mk_sems(nc, es, tag):
    S = {}
    for nm, st in (("ld", 16), ("wld", 16), ("st", 16), ("pe", 1), ("act", 1), ("dve", 1), ("pool", 1),
                   ("cld", 16)):
        S[nm] = Sem(nc, es, f"{tag}_{nm}", st)
    return S


def phase_A(nc, io, dyn=None):
    P = Prog()
    _UID[0] += 1
    with ExitStack() as es:
        S = mk_sems(nc, es, "A")
        hT = sbt(nc, es, "hT", [128, KC, T], BF16)
        uT = sbt(nc, es, "uT", [128, 16, T], BF16)
        vn = sbt(nc, es, "vn", [128, 8, 2048], BF16)
        wbufs = [sbt(nc, es, f"w{i}", [128, KC, 256], BF16) for i in range(2)]
        xs = Ring([sbt(nc, es, f"xs{i}", [128, T], F32) for i in range(2)], dsems(nc, es, "A_xl", 2))
        sq = Ring([sbt(nc, es, f"sq{i}", [128, T], BF16) for i in range(2)])
        rstd = sbt(nc, es, "rstd", [128, T], F32)
        ones = sbt(nc, es, "ones", [128, 128], BF16)
        gsb = sbt(nc, es, "gsb", [128, KC], F32)
        normg = sbt(nc, es, "normg", [128, 2048], BF16)
        WT = sbt(nc, es, "WT", [128, 16, 128], BF16)
        brow = sbt(nc, es, "brow", [1, 2048], BF16)
        onesr = sbt(nc, es, "onesr", [1, 128], BF16)
        qst = Ring([sbt(nc, es, f"qst{i}", [128, T], BF16) for i in range(2)], dsems(nc, es, "A_qs", 2))
        vst = Ring([sbt(nc, es, f"vst{i}", [128, 8, 256], BF16) for i in range(2)], dsems(nc, es, "A_vs", 2))
        ss = sbt(nc, es, "ss", [128, 128], F32)
        rs = sbt(nc, es, "rs", [128, 128], F32)
        junk = sbt(nc, es, "junk", [128, 128], F32)
        ps = [es.enter_context(nc.psum_tensor(f"psA{i}_{_UID[0]}", [128, 512], F32)) for i in range(8)]
        psr = Ring(ps)

        c1 = P.op("sp", lambda e: e.dma_start(out=gsb[:], in_=io["g1"][:]), inc=S["cld"])
        if "zero_dst" in io:
            zt = sbt(nc, es, "zt", [128, 64], F32)
            zr = P.op("pool", lambda e: e.memset(zt[:], 0.0), inc=S["pool"])
            c1 = P.op("sp", lambda e: e.dma_start(out=io["zero_dst"], in_=zt[:]), waits=[zr], inc=S["cld"])
        P.op("pool", lambda e: e.dma_start(out=normg[:], in_=io["normg"][:]), inc=S["wld"])
        P.op("pool", lambda e: e.dma_start(out=WT[:], in_=io["wsT"][:]), inc=S["wld"])
        c2 = P.op("pool", lambda e: e.dma_start(out=brow[:], in_=io["brow"][:]), inc=S["wld"])
        P.op("pool", lambda e: e.memset(ones[:], 1.0))
        P.op("pool", lambda e: e.memset(onesr[:], 1.0))
        P.op("pool", lambda e: e.memset(ss[:], 0.0))
        cpool = P.op("pool", lambda e: e.affine_select(out=WT[:], in_=WT[:], pattern=[[0, 16], [1, 128]],
                                                       compare_op=ALU.is_ge, fill=0.0, base=0,
                                                       channel_multiplier=-1), waits=[c2], inc=S["pool"])

        h_done, ps_rel = emit_rmsnorm(P, S, lambda c: io["xT"][c], gsb, xs, sq, ones, rstd, ps[0], ps[1],
                                      [cpool], lambda c: hT[:, c, :])
        psr.free_at[0] = ps_rel
        psr.free_at[1] = ps_rel

        mixgA = sbt(nc, es, "mixgA", [128, 16], F32)
        cmg = P.op("sp", lambda e: e.dma_start(out=mixgA[:], in_=io["mixgA"][:]), inc=S["cld"])
        sgu_state = {}

        def emit_sgu(g):
            dv = None
            for tq in range(2):
                bi, pb, pfw = psr.acquire()
                full = None
                for i in range(4):
                    tc = tq * 4 + i
                    P.op("pe", lambda e, pb=pb, i=i, tc=tc, g=g: e.matmul(
                        pb[:, i * 128:(i + 1) * 128], lhsT=vn[:, tc, g * 128:(g + 1) * 128], rhs=WT[:, g, :],
                        start=True, stop=False), waits=[pfw, sgu_state["vn_done"], cpool] if i == 0 else ())
                    full = P.op("pe", lambda e, pb=pb, i=i, g=g: e.matmul(
                        pb[:, i * 128:(i + 1) * 128], lhsT=onesr[0:1, :], rhs=brow[0:1, g * 128:(g + 1) * 128],
                        start=False, stop=True), inc=S["pe"])
                dv = P.op("dve", lambda e, pb=pb, g=g, tq=tq: e.tensor_tensor(
                    out=uT[:, g, tq * 512:(tq + 1) * 512], in0=pb[:], in1=uT[:, g, tq * 512:(tq + 1) * 512],
                    op=ALU.mult), waits=[full, u_ready[g]], inc=S["dve"])
                psr.release(bi, dv)
            j, sqb, sfw = sq.acquire()
            a = P.op("act", lambda e, sqb=sqb, g=g: e.activation(out=sqb[:], in_=uT[:, g, :], func=AF.Square),
                     waits=[dv, sfw], inc=S["act"])
            ri, rb, rfw = xs.acquire()
            fulls, banks = [], []
            for th in range(2):
                bi, pb, pfw = psr.acquire()
                banks.append((bi, pb))
                fulls.append(P.op("pe", lambda e, pb=pb, sqb=sqb, th=th: e.matmul(
                    pb[:], lhsT=ones[:], rhs=sqb[:, th * 512:(th + 1) * 512], start=True, stop=True),
                    waits=[a, pfw], inc=S["pe"]))
            sq.release(j, fulls[1])
            a2 = None
            for th in range(2):
                a2 = P.op("act", lambda e, rb=rb, pb=banks[th][1], th=th: e.activation(
                    out=rb[:, th * 512:(th + 1) * 512], in_=pb[:], func=AF.Sqrt, bias=EPS, scale=1.0 / 128),
                    waits=[fulls[th], rfw], inc=S["act"])
                psr.release(banks[th][0], a2)
            P.op("dve", lambda e, rb=rb: e.reciprocal(out=rb[:], in_=rb[:]), waits=[a2, cmg])
            gd = P.op("dve", lambda e, rb=rb, g=g: e.scalar_tensor_tensor(
                out=uT[:, g, :], in0=uT[:, g, :], scalar=mixgA[:, g:g + 1], in1=rb[:], op0=ALU.mult, op1=ALU.mult),
                inc=S["dve"])
            xs.release(ri, gd)
            P.op("sp", lambda e, g=g: e.dma_start(out=io["gT"][g], in_=uT[:, g, :]), waits=[gd], inc=S["st"])

        ws = WStream(P, S, wbufs, dsems(nc, es, "A_wl", 2))
        NB = 40
        pending = []
        for b in range(min(2, NB)):
            pending.append(ws.load(io["w_in"][b], KC))
        u_ready = [None] * 16
        vn_done = None
        for b in range(NB):
            wi, wb, wld = pending.pop(0)
            blk_done = None
            if b < 16 or b >= 32:
                for j in range(2):
                    ch = 2 * b + j if b < 16 else 2 * (b - 32) + j
                    if b < 16:
                        si, stg, sfw = qst.acquire()
                    a = None
                    for th in range(2):
                        bi, pb, pfw = psr.acquire()
                        full = mm_group(P, S, pb, 512,
                                        lambda k, wb=wb, j=j: wb[:, k, j * 128:(j + 1) * 128],
                                        lambda k, th=th: hT[:, k, th * 512:(th + 1) * 512],
                                        KC, [wld, pfw, h_done])
                        if b < 16:
                            a = P.op("act", lambda e, stg=stg, pb=pb, th=th: e.activation(
                                out=stg[:, th * 512:(th + 1) * 512], in_=pb[:], func=AF.Copy),
                                waits=[full, sfw], inc=S["act"])
                        else:
                            a = P.op("act", lambda e, pb=pb, th=th, ch=ch: e.activation(
                                out=uT[:, ch, th * 512:(th + 1) * 512], in_=pb[:], func=AF.Gelu),
                                waits=[full], inc=S["act"])
                            u_ready[ch] = a
                        psr.release(bi, a)
                        blk_done = a
                    if b < 16:
                        hd = ch % 16
                        cho = (hd // 4) * 8 + (4 if ch >= 16 else 0) + hd % 4
                        st = P.op("sp", lambda e, stg=stg, cho=cho: e.dma_start(out=io["qk"][cho], in_=stg[:]),
                                  waits=[a, c1], inc=qst.sems[si])
                        qst.release(si, st)
            else:
                isv = b < 24
                cb = (b - 16) if isv else (b - 24)
                if isv:
                    si, stg, sfw = vst.acquire()
                a = None
                for tc in range(8):
                    bi, pb, pfw = psr.acquire()
                    full = mm_group(P, S, pb, 256,
                                    lambda k, tc=tc: hT[:, k, tc * 128:(tc + 1) * 128],
                                    lambda k, wb=wb: wb[:, k, :],
                                    KC, [wld, pfw, h_done])
                    if isv:
                        a = P.op("act", lambda e, stg=stg, pb=pb, tc=tc: e.activation(
                            out=stg[:, tc, :], in_=pb[:, 0:256], func=AF.Copy), waits=[full, sfw], inc=S["act"])
                    else:
                        a = P.op("act", lambda e, pb=pb, tc=tc, cb=cb: e.activation(
                            out=vn[:, tc, cb * 256:(cb + 1) * 256], in_=pb[:, 0:256], func=AF.Gelu),
                            waits=[full], inc=S["act"])
                        for gg in range(2):
                            g = cb * 2 + gg
                            P.op("act", lambda e, tc=tc, g=g: e.activation(
                                out=junk[:], in_=vn[:, tc, g * 128:(g + 1) * 128], func=AF.Square,
                                accum_out=ss[:, tc * 16 + g:tc * 16 + g + 1]))
                    psr.release(bi, a)
                    blk_done = a
                if isv:
                    vdst = io["v"](cb)
                    st = P.op("sp", lambda e, stg=stg, vdst=vdst: e.dma_start(out=vdst, in_=stg[:]),
                              waits=[a], inc=vst.sems[si])
                    vst.release(si, st)
            ws.release(wi, blk_done)
            if b + 2 < NB:
                pending.append(ws.load(io["w_in"][b + 2], KC))
            if b == 15 and "ag_qk" in io:
                prog_allgather(P, io["cc"], io["ag_qk"], qst.all_done())
            if b == 23 and "ag_v" in io:
                prog_allgather(P, io["cc"], io["ag_v"], vst.all_done())
            if b == 31:
                a = P.op("act", lambda e: e.activation(out=rs[:], in_=ss[:], func=AF.Sqrt, bias=EPS,
                                                       scale=1.0 / 128), inc=S["act"])
                P.op("dve", lambda e: e.reciprocal(out=rs[:], in_=rs[:]), waits=[a, c2])
                for tc in range(8):
                    for g in range(16):
                        col = tc * 16 + g
                        vn_done = P.op("dve", lambda e, tc=tc, g=g, col=col: e.scalar_tensor_tensor(
                            out=vn[:, tc, g * 128:(g + 1) * 128], in0=vn[:, tc, g * 128:(g + 1) * 128],
                            scalar=rs[:, col:col + 1], in1=normg[:, g * 128:(g + 1) * 128],
                            op0=ALU.mult, op1=ALU.mult), inc=S["dve"])
                sgu_state["vn_done"] = vn_done
            if b >= 32:
                emit_sgu(2 * (b - 32))
                emit_sgu(2 * (b - 32) + 1)

        P.wait("sp", (S["st"], S["st"].n))
        for w in qst.all_done() + vst.all_done():
            P.wait("sp", w)
        P.run(nc, dyn)


def attn_groups(d):
    out = []
    nb = 32 // d
    if d == 16:
        for r0 in range(0, 16, 2):
            out.append(([(r0, 0), (r0, 1), (r0 + 1, 0), (r0 + 1, 1)], ("pair", r0)))
    else:
        for r in range(d):
            for n0 in range(0, nb, 4):
                out.append(([(r, n0 + i) for i in range(4)], ("run", r, n0)))
    return out


def phase_B(nc, io, dyn=None):
    P = Prog()
    _UID[0] += 1
    NS = 4096
    with ExitStack() as es:
        S = mk_sems(nc, es, "B")
        S["sfull"] = Sem(nc, es, "B_sfull", 1)
        S["ofull"] = Sem(nc, es, "B_ofull", 1)
        S["fin"] = Sem(nc, es, "B_fin", 1)
        nset = 2
        qT = [sbt(nc, es, f"qT{i}", [128, NS], BF16) for i in range(nset)]
        kT = [sbt(nc, es, f"kT{i}", [128, NS], BF16) for i in range(nset)]
        vd = [{d: sbt(nc, es, f"v{d}_{i}", [128, 32, 128], BF16) for d in (1, 4, 16)} for i in range(nset)]
        inring = Ring(list(range(nset)), dsems(nc, es, "B_hl", nset))
        acc_o = sbt(nc, es, "acc_o", [128, NS], F32)
        acc_z = sbt(nc, es, "acc_z", [128, NS], F32)
        aT = sbt(nc, es, "aT", [128, NS], BF16)
        EB = sbt(nc, es, "EB", [128, 12, 4, 256], BF16)
        sqh = sbt(nc, es, "sqh", [128, NS], BF16)
        mixgB = sbt(nc, es, "mixgB", [128, 4], F32)
        pT = Ring([sbt(nc, es, f"pT{i}", [128, 1024], BF16) for i in range(2)])
        pT2 = Ring([sbt(nc, es, f"pTm{i}", [128, 1024], BF16) for i in range(2)])
        ones = sbt(nc, es, "onesB", [128, 128], BF16)
        nsl = sbt(nc, es, "nsl", [128, 12], F32)
        it = sbt(nc, es, "it", [128, 256], I32)
        dist = sbt(nc, es, "dist", [128, 256], F32)
        etmp = Ring([sbt(nc, es, f"etmp{i}", [128, 256], F32) for i in range(2)])
        ps = [es.enter_context(nc.psum_tensor(f"psB{i}_{_UID[0]}", [128, 512], F32)) for i in range(8)]
        sp_ring = Ring([0, 1])
        o_ring = Ring([0, 1])

        P.op("sp", lambda e: e.dma_start(out=mixgB[:], in_=io["mixgB"][:]), inc=S["cld"])
        c1 = P.op("sp", lambda e: e.dma_start(out=nsl[:], in_=io["nsl"][:]), inc=S["cld"])
        P.op("pool", lambda e: e.memset(ones[:], 1.0))
        P.op("pool", lambda e: e.iota(it[:], [[1, 256]], base=0, channel_multiplier=-1))
        itd = P.op("pool", lambda e: e.memset(ones[:, 0:1], 1.0), inc=S["pool"])
        P.op("dve", lambda e: e.tensor_copy(out=dist[:], in_=it[:]), waits=[itd])
        P.op("dve", lambda e: e.tensor_scalar(out=dist[:, 0:128], in0=dist[:, 0:128], scalar1=128.0,
                                              scalar2=None, op0=ALU.add))
        dready = P.op("dve", lambda e: e.tensor_scalar(out=dist[:, 128:256], in0=dist[:, 128:256],
                                                       scalar1=-128.0, scalar2=None, op0=ALU.add), inc=S["dve"])
        eb_done = None
        for idx in range(12):
            ei, eb, efw = etmp.acquire()
            a = P.op("act", lambda e, eb=eb, idx=idx: e.activation(out=eb[:], in_=dist[:], func=AF.Exp,
                                                                   scale=nsl[:, idx:idx + 1]),
                     waits=[dready, c1, efw], inc=S["act"])
            P.op("pool", lambda e, eb=eb, idx=idx: e.affine_select(
                out=EB[:, idx, 0, 0:128], in_=eb[:, 0:128], pattern=[[-1, 128]], compare_op=ALU.is_ge, fill=0.0,
                base=0, channel_multiplier=1), waits=[a])
            eb_done = P.op("pool", lambda e, eb=eb, idx=idx: e.affine_select(
                out=EB[:, idx, 0, 128:256], in_=eb[:, 128:256], pattern=[[1, 128]], compare_op=ALU.is_ge, fill=0.0,
                base=0, channel_multiplier=-1), inc=S["pool"])
            etmp.release(ei, eb_done)
            for rep in range(1, 4):
                eb_done = P.op("pool", lambda e, idx=idx, rep=rep: e.tensor_copy(
                    out=EB[:, idx, rep, :], in_=EB[:, idx, 0, :]), inc=S["pool"])

        pre_done = None
        if "cc" in io:
            P.wait("sp", (io["cc"], io["cc"].n))
        for (dst, srcf) in io.get("pre", []):
            pre_done = P.op("sp", lambda e, dst=dst, srcf=srcf: e.dma_start(out=dst, in_=srcf()), inc=S["cld"])

        def load_head(h):
            i, si, fw = inring.acquire()
            lsem = inring.sems[i]
            P.op("sp", lambda e: e.dma_start(out=qT[si][:].rearrange("p (r t) -> p r t", r=4), in_=io["q"](h)),
                 waits=[fw, pre_done], inc=lsem)
            P.op("sp", lambda e: e.dma_start(out=kT[si][:].rearrange("p (r t) -> p r t", r=4), in_=io["k"](h)),
                 inc=lsem)
            ld = None
            for d in (1, 4, 16):
                nb = 32 // d
                for r in range(d):
                    ld = P.op("sp", lambda e, d=d, r=r, nb=nb: e.dma_start(
                        out=vd[si][d][:, r * nb:(r + 1) * nb, :], in_=io["v"](h, d, r)), inc=lsem)
            return i, si, ld

        nxt = load_head(0)
        a_st = None
        fin_prev = None
        for h in range(4):
            ri, si, ld = nxt
            if h + 1 < 4:
                nxt = load_head(h + 1)
            q_, k_ = qT[si], kT[si]
            last_pe_head = None
            for di, d in enumerate((1, 4, 16)):
                nb = 32 // d
                groups = attn_groups(d)
                qv = q_[:].rearrange("p (m r) -> p r m", r=d)
                kv = k_[:].rearrange("p (m r) -> p r m", r=d)
                ebi = h * 3 + di
                state = {}

                def p1(gi):
                    blocks, sel = groups[gi]
                    _, sset, sfw = sp_ring.acquire()
                    full = None
                    first = True
                    for b, (r, n) in enumerate(blocks):
                        bank = ps[4 * sset + b // 2]
                        off = (b % 2) * 256
                        qa = qv[:, r, n * 128:(n + 1) * 128]
                        if n > 0:
                            ka = kv[:, r, (n - 1) * 128:n * 128]
                            P.op("pe", lambda e, bank=bank, off=off, ka=ka, qa=qa: e.matmul(
                                bank[:, off:off + 128], lhsT=ka, rhs=qa, start=True, stop=True),
                                waits=[sfw, ld] if first else ())
                            first = False
                        ka = kv[:, r, n * 128:(n + 1) * 128]
                        fn = lambda e, bank=bank, off=off, ka=ka, qa=qa: e.matmul(
                            bank[:, off + 128:off + 256], lhsT=ka, rhs=qa, start=True, stop=True)
                        if b == 3:
                            full = P.op("pe", fn, waits=[sfw, ld] if first else (), inc=S["sfull"])
                        else:
                            P.op("pe", fn, waits=[sfw, ld] if first else ())
                        first = False
                    pi, pbuf, pfw = pT.acquire()
                    P.op("act", lambda e, pbuf=pbuf, sset=sset: e.activation(
                        out=pbuf[:, 0:512], in_=ps[4 * sset][:], func=AF.Exp, scale=SCALE), waits=[full, pfw])
                    a = P.op("act", lambda e, pbuf=pbuf, sset=sset: e.activation(
                        out=pbuf[:, 512:1024], in_=ps[4 * sset + 1][:], func=AF.Exp, scale=SCALE), inc=S["act"])
                    sp_ring.release(sset, a)
                    mi, mbuf, mfw = pT2.acquire()
                    pl = P.op("dve", lambda e, mbuf=mbuf, pbuf=pbuf, ebi=ebi: e.tensor_tensor(
                        out=mbuf[:].rearrange("p (b c) -> p b c", b=4),
                        in0=pbuf[:].rearrange("p (b c) -> p b c", b=4), in1=EB[:, ebi, :, :], op=ALU.mult),
                        waits=[a, mfw, eb_done], inc=S["dve"])
                    pT.release(pi, pl)
                    state[gi] = (mi, mbuf, pl)

                def p2(gi):
                    blocks, sel = groups[gi]
                    mi, mbuf, pl = state.pop(gi)
                    _, oset, ofw = o_ring.acquire()
                    po, pz = ps[4 * oset + 2], ps[4 * oset + 3]
                    vt = vd[si][d]
                    full = None
                    first = True
                    for b, (r, n) in enumerate(blocks):
                        j = r * nb + n
                        for tgt, lfn in ((po, lambda jj: vt[:, jj, :]), (pz, lambda jj: ones[:])):
                            oa = tgt[:, b * 128:(b + 1) * 128]
                            if n > 0:
                                P.op("pe", lambda e, oa=oa, l=lfn(j - 1), b=b: e.matmul(
                                    oa, lhsT=l, rhs=mbuf[:, b * 256:b * 256 + 128], start=True, stop=False),
                                    waits=[pl, ofw] if first else ())
                                first = False
                            fn = lambda e, oa=oa, l=lfn(j), b=b, n=n: e.matmul(
                                oa, lhsT=l, rhs=mbuf[:, b * 256 + 128:b * 256 + 256], start=(n == 0), stop=True)
                            if b == 3 and tgt is pz:
                                full = P.op("pe", fn, waits=[pl, ofw] if first else (), inc=S["ofull"])
                            else:
                                P.op("pe", fn, waits=[pl, ofw] if first else ())
                            first = False
                    pT2.release(mi, full)
                    if sel[0] == "pair":
                        r0 = sel[1]
                        ao = acc_o[:].rearrange("p (m r) -> p r m", r=16)[:, r0:r0 + 2, :]
                        az = acc_z[:].rearrange("p (m r) -> p r m", r=16)[:, r0:r0 + 2, :]
                        pov = po[:].rearrange("p (a m) -> p a m", a=2)
                        pzv = pz[:].rearrange("p (a m) -> p a m", a=2)
                    else:
                        _, r, n0 = sel
                        ao = acc_o[:].rearrange("p (m r) -> p r m", r=d)[:, r, n0 * 128:n0 * 128 + 512]
                        az = acc_z[:].rearrange("p (m r) -> p r m", r=d)[:, r, n0 * 128:n0 * 128 + 512]
                        pov, pzv = po[:], pz[:]
                    if d == 1:
                        P.op("dve", lambda e, ao=ao, pov=pov: e.tensor_copy(out=ao, in_=pov),
                             waits=[full, fin_prev])
                        dv = P.op("dve", lambda e, az=az, pzv=pzv: e.tensor_copy(out=az, in_=pzv), inc=S["dve"])
                    else:
                        P.op("dve", lambda e, ao=ao, pov=pov: e.tensor_tensor(out=ao, in0=ao, in1=pov, op=ALU.add),
                             waits=[full])
                        dv = P.op("dve", lambda e, az=az, pzv=pzv: e.tensor_tensor(out=az, in0=az, in1=pzv,
                                                                                   op=ALU.add), inc=S["dve"])
                    o_ring.release(oset, dv)
                    return full

                ng = len(groups)
                p1(0)
                for gi in range(ng):
                    if gi + 1 < ng:
                        p1(gi + 1)
                    last_pe_head = p2(gi)
                    if di == 0 and gi == 5 and h > 0 and "ag_a" in io:
                        prog_allgather(P, io["cc"], [io["ag_a"][h - 1]], [a_st])
            inring.release(ri, last_pe_head)
            P.op("dve", lambda e: e.reciprocal(out=acc_z[:], in_=acc_z[:]))
            fin = P.op("dve", lambda e: e.tensor_tensor(out=aT[:], in0=acc_o[:], in1=acc_z[:], op=ALU.mult),
                       waits=[a_st], inc=S["fin"])
            asq = P.op("act", lambda e: e.activation(out=sqh[:], in_=aT[:], func=AF.Square), waits=[fin],
                       inc=S["act"])
            a2 = None
            for qq in range(4):
                _, oset, ofw = o_ring.acquire()
                po, pz = ps[4 * oset + 2], ps[4 * oset + 3]
                P.op("pe", lambda e, po=po, qq=qq: e.matmul(po[:], lhsT=ones[:], rhs=sqh[:, qq * 1024:qq * 1024 + 512],
                                                            start=True, stop=True), waits=[asq, ofw])
                full = P.op("pe", lambda e, pz=pz, qq=qq: e.matmul(
                    pz[:], lhsT=ones[:], rhs=sqh[:, qq * 1024 + 512:qq * 1024 + 1024], start=True, stop=True),
                    inc=S["ofull"])
                P.op("act", lambda e, po=po, qq=qq: e.activation(
                    out=acc_z[:, qq * 1024:qq * 1024 + 512], in_=po[:], func=AF.Sqrt, bias=EPS, scale=1.0 / 128),
                    waits=[full])
                a2 = P.op("act", lambda e, pz=pz, qq=qq: e.activation(
                    out=acc_z[:, qq * 1024 + 512:qq * 1024 + 1024], in_=pz[:], func=AF.Sqrt, bias=EPS,
                    scale=1.0 / 128), inc=S["act"])
                o_ring.release(oset, a2)
            P.op("dve", lambda e: e.reciprocal(out=acc_z[:], in_=acc_z[:]), waits=[a2, c1])
            fin2 = P.op("dve", lambda e, h=h: e.scalar_tensor_tensor(
                out=aT[:], in0=aT[:], scalar=mixgB[:, h:h + 1], in1=acc_z[:], op0=ALU.mult, op1=ALU.mult),
                inc=S["fin"])
            a_st = P.op("sp", lambda e, h=h: e.dma_start(out=io["a"](h), in_=aT[:].rearrange(
                "p (tb t) -> p tb t", tb=4)), waits=[fin2], inc=S["st"])
            fin_prev = None
        if "ag_a" in io:
            prog_allgather(P, io["cc"], [io["ag_a"][3]], [a_st])
        P.wait("sp", (S["st"], S["st"].n))
        P.run(nc, dyn)


class XPipe:
    def __init__(self, P, xs, x_src, order, st_hist):
        self.P, self.xs, self.x_src, self.order, self.st_hist = P, xs, x_src, order, st_hist
        self.nxt = 0
        self.loaded = {}

    def issue(self):
        if self.nxt >= len(self.order):
            return
        ch = self.order[self.nxt]
        self.nxt += 1
        i, xb, fw = self.xs.acquire()
        src = self.x_src(ch)
        xl = self.P.op("sp", lambda e, xb=xb, src=src: e.dma_start(out=xb[:], in_=src),
                       waits=[fw, self.st_hist.get(ch)], inc=self.xs.sems[i])
        self.loaded[ch] = (i, xb, xl)


def emit_resid_block(P, S, psr, xp, xo, wb, wld, b, nk, rhs_fn, x_dst, ready, st_hist, halo_dst=None):
    blk_done = None
    for j in range(2):
        ch = 2 * b + j
        xi, xb, xl = xp.loaded.pop(ch)
        oi, ob, ofw = xo.acquire()
        dv = None
        for th in range(2):
            bi, pb, pfw = psr.acquire()
            full = mm_group(P, S, pb, 512, lambda k, wb=wb, j=j: wb[:, k, j * 128:(j + 1) * 128],
                            lambda k, th=th: rhs_fn(k, th), nk, [wld, pfw] + list(ready))
            dv = P.op("dve", lambda e, ob=ob, pb=pb, xb=xb, th=th: e.tensor_tensor(
                out=ob[:, th * 512:(th + 1) * 512], in0=pb[:], in1=xb[:, th * 512:(th + 1) * 512],
                op=ALU.add), waits=[full, xl, ofw], inc=S["dve"])
            psr.release(bi, dv)
            blk_done = full
        xp.xs.release(xi, dv)
        dst = x_dst(ch)
        if halo_dst is not None:
            hd = halo_dst(ch)
            P.op("sp", lambda e, ob=ob, hd=hd: e.dma_start(out=hd, in_=ob[:, T - 2:T]), waits=[dv],
                 inc=xo.sems[oi])
        st = P.op("sp", lambda e, ob=ob, dst=dst: e.dma_start(out=dst, in_=ob[:]), waits=[dv],
                  inc=xo.sems[oi])
        st_hist[ch] = st
        xo.release(oi, st)
        xp.issue()
    return blk_done


def phase_C1(nc, io, dyn=None):
    P = Prog()
    _UID[0] += 1
    with ExitStack() as es:
        S = mk_sems(nc, es, "C1")
        S["ld2"] = Sem(nc, es, "C1_ld2", 16)
        mT = sbt(nc, es, "mT", [128, KC, T], BF16)
        wbufs = [sbt(nc, es, f"wc{i}", [128, 16, 256], BF16) for i in range(4)]
        xs = Ring([sbt(nc, es, f"xc{i}", [128, T], F32) for i in range(3)], dsems(nc, es, "C1_xl", 3))
        xo = Ring([sbt(nc, es, f"xo{i}", [128, T], F32) for i in range(2)], dsems(nc, es, "C1_xs", 2))
        ps = [es.enter_context(nc.psum_tensor(f"psC{i}_{_UID[0]}", [128, 512], F32)) for i in range(8)]
        psr = Ring(ps)

        ld_g = None
        for c in range(16):
            ld_g = P.op("sp", lambda e, c=c: e.dma_start(out=mT[:, 16 + c, :], in_=io["gT"][c]), inc=S["ld"])
        st_hist = {}
        ws = WStream(P, S, wbufs, dsems(nc, es, "C1_wl", 4))
        sched = [(0, b) for b in range(16)] + [(1, b) for b in range(16)]
        pending = [ws.load(io["w_out"][hf][b], 16) for (hf, b) in sched[:4]]
        nxt = 4
        xp = XPipe(P, xs, lambda ch: io["xT"][ch], list(range(KC)), st_hist)
        for _ in range(3):
            xp.issue()
        ld_a = None
        for (hf, b) in sched:
            if hf == 1 and b == 0:
                if "cc" in io:
                    P.wait("sp", (io["cc"], io["cc"].n))
                pre_done = None
                for (dst, srcf) in io.get("pre", []):
                    pre_done = P.op("sp", lambda e, dst=dst, srcf=srcf: e.dma_start(out=dst, in_=srcf()),
                                    inc=S["cld"])
                P.wait("sp", pre_done)
                for c in range(16):
                    ld_a = P.op("sp", lambda e, c=c: e.dma_start(out=mT[:, c, :], in_=io["aT"](c)), inc=S["ld2"])
                xp = XPipe(P, xs, lambda ch: io["xo"][ch], list(range(KC)), st_hist)
                for _ in range(3):
                    xp.issue()
            wi, wb, wld = pending.pop(0)
            if hf == 0:
                blk_done = emit_resid_block(P, S, psr, xp, xo, wb, wld, b, 16,
                                            lambda k, th: mT[:, 16 + k, th * 512:(th + 1) * 512],
                                            lambda ch: io["xo"][ch], [ld_g], st_hist)
            else:
                blk_done = emit_resid_block(P, S, psr, xp, xo, wb, wld, b, 16,
                                            lambda k, th: mT[:, k, th * 512:(th + 1) * 512],
                                            lambda ch: io["xo"][ch], [ld_a], st_hist,
                                            halo_dst=io.get("halo_dst"))
            ws.release(wi, blk_done)
            if nxt < len(sched):
                pending.append(ws.load(io["w_out"][sched[nxt][0]][sched[nxt][1]], 16))
                nxt += 1
        for w in xo.all_done():
            P.wait("sp", w)
        if "ag_xh" in io:
            prog_allgather(P, io["cc"], io["ag_xh"], xo.all_done())
        P.run(nc, dyn)


def phase_C2(nc, io, final, dyn=None):
    P = Prog()
    _UID[0] += 1
    with ExitStack() as es:
        S = mk_sems(nc, es, "C2")
        S["act2"] = Sem(nc, es, "C2_act2", 1)
        S["dve2"] = Sem(nc, es, "C2_dve2", 1)
        hT = sbt(nc, es, "h2T", [128, KC, T], BF16)
        hh = sbt(nc, es, "hh", [128, KC, 2], BF16)
        gsl = sbt(nc, es, "gsl", [128, 22, T], BF16)
        wbufs = [sbt(nc, es, f"wf{i}", [128, KC, 256], BF16) for i in range(2)]
        xs = Ring([sbt(nc, es, f"xf{i}", [128, T], F32) for i in range(2)], dsems(nc, es, "C2_xl", 2))
        xo = Ring([sbt(nc, es, f"xg{i}", [128, T], F32) for i in range(2)], dsems(nc, es, "C2_xs", 2))
        sq = Ring([sbt(nc, es, f"sqf{i}", [128, T], BF16) for i in range(2)])
        rstd = sbt(nc, es, "rstdf", [128, T], F32)
        ones = sbt(nc, es, "onesF", [128, 128], BF16)
        g2 = sbt(nc, es, "g2s", [128, KC], F32)
        gf = sbt(nc, es, "gfs", [128, KC], F32)
        cw = sbt(nc, es, "cws", [128, 172, 3], F32)
        cb = sbt(nc, es, "cbs", [128, 172], F32)
        xh = sbt(nc, es, "xhs", [128, KC, 2], F32)
        xh2 = sbt(nc, es, "xh2", [128, KC, 2], BF16)
        rh = sbt(nc, es, "rh", [128, 2], F32)
        U = Ring([(sbt(nc, es, f"Ug{i}", [128, T + 2], F32), sbt(nc, es, f"Uv{i}", [128, T + 2], F32))
                  for i in range(2)])
        cc = Ring([(sbt(nc, es, f"cg{i}", [128, 512], F32), sbt(nc, es, f"cv{i}", [128, 512], F32))
                   for i in range(2)])
        sgr = Ring([sbt(nc, es, f"sg{i}", [128, 512], F32) for i in range(2)])
        ps = [es.enter_context(nc.psum_tensor(f"psF{i}_{_UID[0]}", [128, 512], F32)) for i in range(8)]
        psr = Ring(ps[0:7])
        ph = ps[7]

        c1 = None
        for dst, src in ((g2, "g2"), (cw, "cw"), (cb, "cb"), (xh, "xh")) + (((gf, "gf"),) if final else ()):
            if callable(io[src]):
                if "cc" in io:
                    P.wait("sp", (io["cc"], io["cc"].n))
                c1 = P.op("sp", lambda e, dst=dst, src=src: e.dma_start(
                    out=dst[:].rearrange("p k t -> p (k t)"), in_=io[src]()), inc=S["cld"])
            else:
                c1 = P.op("sp", lambda e, dst=dst, src=src: e.dma_start(out=dst[:], in_=io[src][:]), inc=S["cld"])
        ones_ok = P.op("pool", lambda e: e.memset(ones[:], 1.0), inc=S["pool"])

        a = P.op("act", lambda e: e.activation(out=xh2[:], in_=xh[:], func=AF.Square), waits=[c1], inc=S["act"])
        full = mm_group(P, S, ph, 2, lambda k: ones[:], lambda k: xh2[:, k, :], KC, [a, ones_ok])
        a = P.op("act", lambda e: e.activation(out=rh[:], in_=ph[:, 0:2], func=AF.Sqrt, bias=EPS, scale=1.0 / D),
                 waits=[full], inc=S["act"])
        ph_free = a
        P.op("dve", lambda e: e.reciprocal(out=rh[:], in_=rh[:]), waits=[a])
        hh_done = None
        for c in range(KC):
            hh_done = P.op("dve", lambda e, c=c: e.scalar_tensor_tensor(
                out=hh[:, c, :], in0=xh[:, c, :], scalar=g2[:, c:c + 1], in1=rh[:], op0=ALU.mult, op1=ALU.mult),
                inc=S["dve"])

        h_done, ps_rel = emit_rmsnorm(P, S, lambda c: io["xT"][c], g2, xs, sq, ones, rstd, ps[0], ps[1],
                                      [ones_ok], lambda c: hT[:, c, :])
        psr.free_at[0] = ps_rel
        psr.free_at[1] = ps_rel

        ws = WStream(P, S, wbufs, dsems(nc, es, "C2_wl", 2))
        sched = []
        for s, (k0, nk) in enumerate(SLABS):
            for jj in range(nk):
                sched.append(("up", s, jj, io["w_up"][k0 + jj], KC))
            for b in range(16):
                sched.append(("dn", s, b, io[f"w_dn{s}"][b], nk))
        pending = [ws.load(sched[i][3], sched[i][4]) for i in range(2)]
        nxt_load = 2
        st_hist = {}
        stage_b = []
        g_done = None
        xp = None

        def emit_stage_b():
            (cgb, cvb, ci, jj, th, dvc) = stage_b.pop(0)
            gi_, sgb, gfw = sgr.acquire()
            a2 = P.op("act", lambda e, sgb=sgb, cgb=cgb: e.activation(out=sgb[:], in_=cgb[:], func=AF.Silu),
                      waits=[dvc, gfw], inc=S["act2"])
            d2 = P.op("dve", lambda e, sgb=sgb, cvb=cvb, jj=jj, th=th: e.tensor_tensor(
                out=gsl[:, jj, th * 512:(th + 1) * 512], in0=sgb[:], in1=cvb[:], op=ALU.mult),
                waits=[a2], inc=S["dve2"])
            sgr.release(gi_, d2)
            cc.release(ci, d2)
            return d2

        for it_, (kind, s, idx, src, nkc) in enumerate(sched):
            wi, wb, wld = pending.pop(0)
            k0, nk = SLABS[s]
            blk_done = None
            if kind == "up":
                jj = idx
                cg_, cv_ = k0 + jj, FC + k0 + jj
                ui, (Ug, Uv), ufw = U.acquire()
                mm_group(P, S, ph, 2, lambda k, wb=wb: wb[:, k, 0:128], lambda k: hh[:, k, :], KC,
                         [wld, ph_free, hh_done], out_ap=ph[:, 0:2])
                hfull = mm_group(P, S, ph, 2, lambda k, wb=wb: wb[:, k, 128:256], lambda k: hh[:, k, :], KC,
                                 [], out_ap=ph[:, 2:4])
                P.op("act", lambda e, Ug=Ug: e.activation(out=Ug[:, 0:2], in_=ph[:, 0:2], func=AF.Copy),
                     waits=[hfull, ufw])
                ph_free = P.op("act", lambda e, Uv=Uv: e.activation(out=Uv[:, 0:2], in_=ph[:, 2:4], func=AF.Copy),
                               inc=S["act"])
                for th in range(2):
                    big, pg, pgw = psr.acquire()
                    fg = mm_group(P, S, pg, 512, lambda k, wb=wb: wb[:, k, 0:128],
                                  lambda k, th=th: hT[:, k, th * 512:(th + 1) * 512], KC, [wld, pgw, h_done])
                    biv, pv, pvw = psr.acquire()
                    fv = mm_group(P, S, pv, 512, lambda k, wb=wb: wb[:, k, 128:256],
                                  lambda k, th=th: hT[:, k, th * 512:(th + 1) * 512], KC, [pvw])
                    blk_done = fv
                    ci, (cgb, cvb), cfw = cc.acquire()
                    lo = 2 + th * 512
                    P.op("act", lambda e, Ug=Ug, pg=pg, lo=lo: e.activation(out=Ug[:, lo:lo + 512], in_=pg[:],
                                                                            func=AF.Copy), waits=[fg, ufw])
                    a = P.op("act", lambda e, cgb=cgb, pg=pg, c=cg_: e.activation(
                        out=cgb[:], in_=pg[:], func=AF.Identity, bias=cb[:, c:c + 1], scale=cw[:, c, 2:3]),
                        waits=[cfw, c1], inc=S["act"])
                    psr.release(big, a)
                    P.op("act", lambda e, Uv=Uv, pv=pv, lo=lo: e.activation(out=Uv[:, lo:lo + 512], in_=pv[:],
                                                                            func=AF.Copy), waits=[fv])
                    a = P.op("act", lambda e, cvb=cvb, pv=pv, c=cv_: e.activation(
                        out=cvb[:], in_=pv[:], func=AF.Identity, bias=cb[:, c:c + 1], scale=cw[:, c, 2:3]),
                        inc=S["act"])
                    psr.release(biv, a)
                    dvc = None
                    for (ub, cbuf, c) in ((Ug, cgb, cg_), (Uv, cvb, cv_)):
                        P.op("dve", lambda e, ub=ub, cbuf=cbuf, c=c, lo=lo: e.scalar_tensor_tensor(
                            out=cbuf[:], in0=ub[:, lo - 1:lo + 511], scalar=cw[:, c, 1:2], in1=cbuf[:],
                            op0=ALU.mult, op1=ALU.add), waits=[a])
                        dvc = P.op("dve", lambda e, ub=ub, cbuf=cbuf, c=c, lo=lo: e.scalar_tensor_tensor(
                            out=cbuf[:], in0=ub[:, lo - 2:lo + 510], scalar=cw[:, c, 0:1], in1=cbuf[:],
                            op0=ALU.mult, op1=ALU.add), inc=S["dve"])
                    if th == 1:
                        U.release(ui, dvc)
                    stage_b.append((cgb, cvb, ci, jj, th, dvc))
                    if len(stage_b) > 1:
                        emit_stage_b()
                if jj == nk - 1:
                    while stage_b:
                        g_done = emit_stage_b()
                    xsrc = (lambda ch: io["xT"][ch]) if s == 0 else (lambda ch: io["xw"][ch])
                    xp = XPipe(P, xs, xsrc, list(range(KC)), st_hist)
                    xp.issue()
                    xp.issue()
            else:
                blk_done = emit_resid_block(P, S, psr, xp, xo, wb, wld, idx, nk,
                                            lambda k, th: gsl[:, k, th * 512:(th + 1) * 512],
                                            lambda ch: io["xw"][ch], [g_done], st_hist)
            ws.release(wi, blk_done)
            if nxt_load < len(sched):
                pending.append(ws.load(sched[nxt_load][3], sched[nxt_load][4]))
                nxt_load += 1
        if final:
            last_pe = None
            for c in range(KC):
                i, xb, fw = xs.acquire()
                ld = P.op("sp", lambda e, xb=xb, c=c: e.dma_start(out=xb[:], in_=io["xw"][c]),
                          waits=[fw] + xo.all_done(), inc=xs.sems[i])
                j, sqb, sfw = sq.acquire()
                a = P.op("act", lambda e, xb=xb, sqb=sqb: e.activation(out=sqb[:], in_=xb[:], func=AF.Square),
                         waits=[ld, sfw], inc=S["act"])
                xs.release(i, a)
                w0 = [a] + ([(S["dve"], S["dve"].n)] if c == 0 else [])
                P.op("pe", lambda e, sqb=sqb, c=c: e.matmul(ps[0][:], lhsT=ones[:], rhs=sqb[:, 0:512],
                                                            start=(c == 0), stop=(c == KC - 1)), waits=w0)
                last_pe = P.op("pe", lambda e, sqb=sqb, c=c: e.matmul(ps[1][:], lhsT=ones[:], rhs=sqb[:, 512:1024],
                                                                      start=(c == 0), stop=(c == KC - 1)),
                               inc=S["pe"])
                sq.release(j, last_pe)
            P.op("act", lambda e: e.activation(out=rstd[:, 0:512], in_=ps[0][:], func=AF.Sqrt, bias=EPS,
                                               scale=1.0 / D), waits=[last_pe])
            a = P.op("act", lambda e: e.activation(out=rstd[:, 512:1024], in_=ps[1][:], func=AF.Sqrt, bias=EPS,
                                                   scale=1.0 / D), inc=S["act"])
            P.op("dve", lambda e: e.reciprocal(out=rstd[:], in_=rstd[:]), waits=[a])
            for c in range(KC):
                i, xb, fw = xs.acquire()
                ld = P.op("sp", lambda e, xb=xb, c=c: e.dma_start(out=xb[:], in_=io["xw"][c]), waits=[fw],
                          inc=xs.sems[i])
                oi, ob, ofw = xo.acquire()
                dv = P.op("dve", lambda e, xb=xb, ob=ob, c=c: e.scalar_tensor_tensor(
                    out=ob[:], in0=xb[:], scalar=gf[:, c:c + 1], in1=rstd[:], op0=ALU.mult, op1=ALU.mult),
                    waits=[ld, ofw], inc=S["dve"])
                xs.release(i, dv)
                st = P.op("sp", lambda e, ob=ob, c=c: e.dma_start(out=io["out"][c], in_=ob[:]), waits=[dv],
                          inc=xo.sems[oi])
                xo.release(oi, st)
        for w in xo.all_done():
            P.wait("sp", w)
        P.run(nc, dyn)


GROUPS = [[0, 1, 2, 3], [4, 5, 6, 7]]


def _blocks(w, order=None):
    K, N = w.shape
    a = w.reshape(K // 128, 128, N // 256, 256).transpose(2, 1, 0, 3)
    if order is not None:
        a = a[order]
    return np.ascontiguousarray(a)


def _fm(v):
    return np.ascontiguousarray(v.reshape(-1, 128).T)


def _dram(nc, name, shape, dt, kind):
    return nc.dram_tensor(name, list(shape), dt, kind=kind).ap()


def prog_allgather(P, cc, pairs, waits):
    r = None
    for i, (src, dst) in enumerate(pairs):
        r = P.op("pool", lambda e, src=src, dst=dst: e.collective_compute(
            "AllGather", ALU.bypass, replica_groups=GROUPS, ins=[src], outs=[dst]),
            waits=waits if i == 0 else (), inc=cc)
    return r


def emit_allgather(nc, cc, pairs):
    with nc.Block() as block:
        @block.gpsimd
        def _(g):
            for (src, dst) in pairs:
                g.collective_compute("AllGather", ALU.bypass, replica_groups=GROUPS, ins=[src], outs=[dst]
                                     ).then_inc(cc.h, 1)
                cc.n += 1
            g.wait_ge(cc.h, cc.n)


def build_fused(L=2, stop=99):
    _POOL.clear()
    nc = bass.Bass("TRN2", target_bir_lowering=False, num_devices=NCORES)
    EI, IN = "ExternalInput", "Internal"
    dyn = {}
    xT = _dram(nc, "xT", [KC, 128, T], F32, EI)
    nsl = _dram(nc, "nsl", [128, 12], F32, EI)
    gf = _dram(nc, "gf", [128, KC], F32, EI)
    out = _dram(nc, "out", [KC, 128, T], F32, "ExternalOutput")
    xA = _dram(nc, "xA", [KC, 128, T], F32, IN)
    xB = _dram(nc, "xB", [KC, 128, T], F32, IN)
    qk_send = _dram(nc, "qk_send", [32, 128, T], BF16, IN)
    v_send2 = _dram(nc, "v_send2", [4, T, 512], BF16, IN)
    gT = _dram(nc, "gTd", [16, 128, T], BF16, IN)
    qk_all = _dram(nc, "qk_all", [4 * 32 * 128, T], BF16, IN)
    v_all2 = _dram(nc, "v_all2", [4, 4 * T, 512], BF16, IN)
    a_send3 = _dram(nc, "a_send3", [4, 128, 4096], BF16, IN)
    a_all3 = _dram(nc, "a_all3", [4, 4, 128, 4096], BF16, IN)
    xh_send = _dram(nc, "xh_send", [128, 2 * KC], F32, IN)
    xh_ext = _dram(nc, "xh_ext", [5 * 128, 2 * KC], F32, IN)
    my_qk2 = _dram(nc, "my_qk2", [2, 4, 4, 128, T], BF16, IN)
    my_v = _dram(nc, "my_v", [4096, 512], BF16, IN)
    my_a = _dram(nc, "my_a", [16, 128, T], BF16, IN)
    cc = Sem(nc, None, "cc", 1)

    qk_s2 = qk_send.rearrange("c p t -> (c p) t")
    ag_qkv = [(qk_s2[j * 512:(j + 1) * 512, :], qk_all[j * 2048:(j + 1) * 2048, :]) for j in range(8)]
    ag_qkv += [(v_send2[g], v_all2[g]) for g in range(4)]
    ag_a = [(a_send3[h], a_all3[h].rearrange("rr p t -> (rr p) t")) for h in range(4)]
    xh5 = xh_ext.rearrange("(b p) c -> b p c", p=128)

    with nc.Block() as block0:
        @block0.sync
        def _(sp):
            dyn["r"] = sp.snap(sp.partition_id() % 4, min_val=0, max_val=3)
            dyn["r2"] = sp.snap(dyn["r"] * 2, min_val=0, max_val=6)

    x_in = xT
    for l in range(L):
        wl = [("g1", [128, KC]), ("w_in", [40, 128, KC, 256]), ("normg", [128, 2048]), ("wsT", [128, 16, 128]),
              ("brow", [1, 2048])]
        wl += [("mixgA", [128, 16]), ("mixgB", [128, 4])]
        if stop >= 5:
            wl += [("w_out", [2, 16, 128, 16, 256])]
        if stop >= 7:
            wl += [("g2", [128, KC]), ("w_up", [FC, 128, KC, 256]), ("cw", [128, 172, 3]), ("cb", [128, 172])]
        W = {k: _dram(nc, f"{k}_{l}", shp, F32, EI) for k, shp in wl}
        if stop >= 7:
            for s_, (k0, nk) in enumerate(SLABS):
                W[f"w_dn{s_}"] = _dram(nc, f"w_dn{s_}_{l}", [16, 128, nk, 256], F32, EI)
        ioA = {"xT": x_in, "g1": W["g1"], "w_in": W["w_in"], "normg": W["normg"], "wsT": W["wsT"],
               "brow": W["brow"], "qk": qk_send, "gT": gT, "mixgA": W["mixgA"],
               "v": lambda cb: v_send2[cb // 2].rearrange("(tc p) c -> p tc c", p=128)[
                   :, :, (cb % 2) * 256:(cb % 2) * 256 + 256]}
        if l == 0:
            ioA["zero_dst"] = xh_ext[0:128, :]
        ioA.update({"cc": cc, "ag_qk": ag_qkv[0:8], "ag_v": ag_qkv[8:12]})
        phase_A(nc, ioA, dyn)
        if stop < 2:
            break
        ioB = {
            "pre": [
                (my_qk2.rearrange("jj rr c p t -> jj (rr c p) t"),
                 lambda: qk_all.rearrange("(j x) t -> j x t", j=8)[bass.ds(dyn["r2"], 2)]),
                (my_v[:], lambda: v_all2[bass.ds(dyn["r"], 1)].rearrange("a t c -> (a t) c")),
            ],
            "q": lambda h: my_qk2[0][:, h].rearrange("rr p t -> p rr t"),
            "k": lambda h: my_qk2[1][:, h].rearrange("rr p t -> p rr t"),
            "v": lambda h, d, r: my_v[:, h * 128:(h + 1) * 128].rearrange("(n p r) e -> r p n e", p=128, r=d)[r],
            "nsl": nsl, "a": lambda h: a_send3[h].rearrange("p (tb t) -> p tb t", tb=4),
            "cc": cc, "ag_a": ag_a, "mixgB": W["mixgB"],
        }
        if stop < 3:
            break
        phase_B(nc, ioB, dyn)
        if stop < 4:
            break
        ioC1 = {"xT": x_in,
                "pre": [(my_a.rearrange("h p t -> (h p) t"),
                         lambda: a_all3.rearrange("h rr p (tb t) -> tb (h rr p) t", tb=4)[
                             bass.ds(dyn["r"], 1)].rearrange("a x t -> (a x) t"))],
                "aT": lambda c: my_a[(c % 4) * 4 + c // 4],
                "cc": cc, "ag_xh": [(xh_send[:], xh_ext[128:640, :])],
                "gT": gT, "w_out": W["w_out"], "xo": xA,
                "halo_dst": lambda ch: xh_send[:, 2 * ch:2 * ch + 2]}
        if stop < 5:
            break
        phase_C1(nc, ioC1, dyn)
        if stop < 6:
            break
        if stop < 7:
            break
        final = (l == L - 1)
        ioC2 = {"xT": xA, "xh": lambda: xh5[bass.ds(dyn["r"], 1)].rearrange("a p c -> p (a c)"),
                "g2": W["g2"], "w_up": W["w_up"], "cw": W["cw"], "cb": W["cb"], "xw": xB, "cc": cc}
        for s_ in range(4):
            ioC2[f"w_dn{s_}"] = W[f"w_dn{s_}"]
        if final:
            ioC2["gf"] = gf
            ioC2["out"] = out
        phase_C2(nc, ioC2, final, dyn)
        x_in = xB
    return nc


def _to_xT(x):
    out = []
    for c in range(NCORES):
        b, r = divmod(c, 4)
        xs = x[b, r * T:(r + 1) * T, :]
        out.append(np.ascontiguousarray(xs.T.reshape(KC, 128, T)))
    return out


def kernel(x, attn_norm_g, w_in, sgu_norm_g, w_spatial, b_spatial, mix_norm_g, w_out,
           ffn_norm_g, w_up, conv_w, conv_b, w_down, final_norm_g):
    f = lambda a: np.asarray(a, dtype=np.float32)
    L = 2
    xT = _to_xT(f(x))
    slopes = 2.0 ** (-8.0 * np.arange(1, 17, dtype=np.float64) / 16.0)
    shared = {"gf": _fm(f(final_norm_g))}
    percore = {}
    order = list(range(24)) + list(range(32, 40)) + list(range(24, 32))
    for l in range(L):
        shared[f"g1_{l}"] = _fm(f(attn_norm_g[l]))
        shared[f"w_in_{l}"] = _blocks(f(w_in[l]), order)
        shared[f"normg_{l}"] = np.ascontiguousarray(np.broadcast_to(f(sgu_norm_g[l])[None, :], (128, 2048)))
        shared[f"wsT_{l}"] = np.ascontiguousarray(f(w_spatial[l]).transpose(2, 0, 1))
        shared[f"brow_{l}"] = np.ascontiguousarray(f(b_spatial[l]).reshape(1, 2048))
        mg = _fm(f(mix_norm_g[l]))
        shared[f"mixgA_{l}"] = np.ascontiguousarray(mg[:, 16:32])
        for hg in range(4):
            percore.setdefault(hg, {})[f"mixgB_{l}"] = np.ascontiguousarray(mg[:, 4 * hg:4 * hg + 4])
        wo = f(w_out[l])
        shared[f"w_out_{l}"] = np.stack([_blocks(wo[2048:]), _blocks(wo[:2048])], 0)
        shared[f"g2_{l}"] = _fm(f(ffn_norm_g[l]))
        wu = f(w_up[l])
        shared[f"w_up_{l}"] = np.ascontiguousarray(
            wu.reshape(KC, 128, 2, FC, 128).transpose(3, 1, 0, 2, 4).reshape(FC, 128, KC, 256))
        wd = f(w_down[l])
        for s_, (k0, nk) in enumerate(SLABS):
            shared[f"w_dn{s_}_{l}"] = np.ascontiguousarray(
                wd[k0 * 128:(k0 + nk) * 128, :].reshape(nk, 128, 16, 256).transpose(2, 1, 0, 3))
        shared[f"cw_{l}"] = np.ascontiguousarray(f(conv_w[l]).reshape(3, 172, 128).transpose(2, 1, 0))
        shared[f"cb_{l}"] = np.ascontiguousarray(f(conv_b[l]).reshape(172, 128).T)
    in_maps = []
    for c in range(NCORES):
        hg = c % 4
        nsl = np.zeros((128, 12), np.float32)
        for hl in range(4):
            for di, d in enumerate((1, 4, 16)):
                nsl[:, hl * 3 + di] = np.float32(-slopes[4 * hg + hl] * d)
        m = dict(shared)
        m.update(percore[hg])
        m["xT"] = xT[c]
        m["nsl"] = nsl
        in_maps.append(m)
    nc = build_fused(L)
    res = run_bass_kernel_spmd(nc, in_maps, core_ids=list(range(NCORES))).results
    y = np.empty((2, 4096, D), np.float32)
    for c in range(NCORES):
        b, r = divmod(c, 4)
        y[b, r * T:(r + 1) * T, :] = np.asarray(res[c]["out"]).reshape(D, T).T
    return y
```

```python
import numpy as np
import ml_dtypes
from contextlib import ExitStack
import concourse.bass as bass
import concourse.mybir as mybir
from concourse.bass_utils import run_bass_kernel_spmd

F32 = mybir.dt.float32
BF16 = mybir.dt.bfloat16
I32 = mybir.dt.int32
AF = mybir.ActivationFunctionType
ALU = mybir.AluOpType
NPBF = ml_dtypes.bfloat16

NCORES = 8
T = 1024
D = 4096
KC = 32
DFF = 11008
FC = 86
SLABS = [(0, 22), (22, 22), (44, 21), (65, 21)]
EPS = 1e-6
SCALE = 128.0 ** -0.5


_POOL = {}
_UID = [0]


class Sem:
    def __new__(cls, nc, es, name, step=1):
        key = (id(nc), name)
        if key in _POOL:
            return _POOL[key]
        o = object.__new__(cls)
        o.h = nc.alloc_semaphore(name=name)
        o.step = step
        o.n = 0
        _POOL[key] = o
        return o

    def __init__(self, nc, es, name, step=1):
        pass


class Prog:
    ENG = ("pe", "act", "dve", "pool", "sp")

    def __init__(self):
        self.q = {k: [] for k in self.ENG}
        self.waited = {k: {} for k in self.ENG}

    def wait(self, eng, w):
        if w is None:
            return
        s, v = w
        if v <= 0:
            return
        if v > self.waited[eng].get(id(s), 0):
            self.waited[eng][id(s)] = v
            self.q[eng].append(lambda e, s=s, v=v: e.wait_ge(s.h, v))

    def op(self, eng, fn, waits=(), inc=None):
        for w in waits:
            self.wait(eng, w)
        if inc is None:
            self.q[eng].append(fn)
            return None
        inc.n += inc.step
        self.q[eng].append(lambda e, fn=fn, inc=inc: fn(e).then_inc(inc.h, inc.step))
        return (inc, inc.n)

    def run(self, nc, dyn=None):
        q = self.q
        with nc.Block() as block:
            @block.tensor
            def _(e):
                for f in q["pe"]:
                    f(e)

            @block.scalar
            def _(e):
                for f in q["act"]:
                    f(e)

            @block.vector
            def _(e):
                for f in q["dve"]:
                    f(e)

            @block.gpsimd
            def _(e):
                for f in q["pool"]:
                    f(e)

            @block.sync
            def _(e):
                for f in q["sp"]:
                    f(e)


class Ring:
    def __init__(self, bufs, sems=None):
        self.bufs = bufs
        self.free_at = [None] * len(bufs)
        self.sems = sems
        self.i = 0

    def all_done(self):
        return [(sm, sm.n) for sm in self.sems]

    def acquire(self):
        i = self.i
        self.i = (i + 1) % len(self.bufs)
        return i, self.bufs[i], self.free_at[i]

    def release(self, i, w):
        self.free_at[i] = w


def sbt(nc, es, name, shape, dt):
    return es.enter_context(nc.sbuf_tensor(f"s_{name}_{_UID[0]}", shape, dt))


def dsems(nc, es, tag, n):
    return [Sem(nc, es, f"{tag}{i}", 16) for i in range(n)]


def emit_rmsnorm(P, S, x_src, g_sb, xs_ring, sq_ring, ones, rstd, psA, psB, pe_waits, dst_fn,
                 x_ready=None):
    last_pe = None
    for c in range(KC):
        i, xb, fw = xs_ring.acquire()
        ld = P.op("sp", lambda e, xb=xb, c=c: e.dma_start(out=xb[:], in_=x_src(c)),
                  waits=[fw, x_ready], inc=xs_ring.sems[i])
        j, sqb, sfw = sq_ring.acquire()
        a = P.op("act", lambda e, xb=xb, sqb=sqb: e.activation(out=sqb[:], in_=xb[:], func=AF.Square),
                 waits=[ld, sfw], inc=S["act"])
        xs_ring.release(i, a)
        w0 = [a] + (list(pe_waits) if c == 0 else [])
        P.op("pe", lambda e, sqb=sqb, c=c: e.matmul(psA[:], lhsT=ones[:], rhs=sqb[:, 0:512],
                                                    start=(c == 0), stop=(c == KC - 1)), waits=w0)
        last_pe = P.op("pe", lambda e, sqb=sqb, c=c: e.matmul(psB[:], lhsT=ones[:], rhs=sqb[:, 512:1024],
                                                              start=(c == 0), stop=(c == KC - 1)),
                       inc=S["pe"])
        sq_ring.release(j, last_pe)
    P.op("act", lambda e: e.activation(out=rstd[:, 0:512], in_=psA[:], func=AF.Sqrt, bias=EPS, scale=1.0 / D),
         waits=[last_pe])
    a = P.op("act", lambda e: e.activation(out=rstd[:, 512:1024], in_=psB[:], func=AF.Sqrt, bias=EPS,
                                           scale=1.0 / D), inc=S["act"])
    ps_rel = a
    r = P.op("dve", lambda e: e.reciprocal(out=rstd[:], in_=rstd[:]), waits=[a], inc=S["dve"])
    last = None
    for c in range(KC):
        i, xb, fw = xs_ring.acquire()
        ld = P.op("sp", lambda e, xb=xb, c=c: e.dma_start(out=xb[:], in_=x_src(c)), waits=[fw],
                  inc=xs_ring.sems[i])
        last = P.op("dve", lambda e, xb=xb, c=c: e.scalar_tensor_tensor(
            out=dst_fn(c), in0=xb[:], scalar=g_sb[:, c:c + 1], in1=rstd[:], op0=ALU.mult, op1=ALU.mult),
            waits=[ld], inc=S["dve"])
        xs_ring.release(i, last)
    return last, ps_rel


class WStream:
    def __init__(self, P, S, bufs, sems):
        self.P, self.S = P, S
        self.ring = Ring(bufs, sems)

    def load(self, src_ap, nkc):
        i, wb, fw = self.ring.acquire()
        ld = self.P.op("pool", lambda e, wb=wb: e.dma_start(out=wb[:, 0:nkc, :], in_=src_ap),
                       waits=[fw], inc=self.ring.sems[i])
        return i, wb, ld

    def release(self, i, w):
        self.ring.release(i, w)


def mm_group(P, S, ps, n, lhsT_fn, rhs_fn, nk, waits, out_ap=None):
    o = out_ap if out_ap is not None else ps[:, 0:n]
    r = None
    for k in range(nk):
        w = waits if k == 0 else ()
        fn = lambda e, k=k: e.matmul(o, lhsT=lhsT_fn(k), rhs=rhs_fn(k), start=(k == 0), stop=(k == nk - 1))
        if k == nk - 1:
            r = P.op("pe", fn, waits=w, inc=S["pe"])
        else:
            P.op("pe", fn, waits=w)
    return r


def mk_sems(nc, es, tag):
    S = {}
    for nm, st in (("ld", 16), ("wld", 16), ("st", 16), ("pe", 1), ("act", 1), ("dve", 1), ("pool", 1),
                   ("cld", 16)):
        S[nm] = Sem(nc, es, f"{tag}_{nm}", st)
    return S


def phase_A(nc, io, dyn=None):
    P = Prog()
    _UID[0] += 1
    with ExitStack() as es:
        S = mk_sems(nc, es, "A")
        hT = sbt(nc, es, "hT", [128, KC, T], BF16)
        uT = sbt(nc, es, "uT", [128, 16, T], BF16)
        vn = sbt(nc, es, "vn", [128, 8, 2048], BF16)
        wbufs = [sbt(nc, es, f"w{i}", [128, KC, 256], BF16) for i in range(2)]
        xs = Ring([sbt(nc, es, f"xs{i}", [128, T], F32) for i in range(2)], dsems(nc, es, "A_xl", 2))
        sq = Ring([sbt(nc, es, f"sq{i}", [128, T], BF16) for i in range(2)])
        rstd = sbt(nc, es, "rstd", [128, T], F32)
        ones = sbt(nc, es, "ones", [128, 128], BF16)
        gsb = sbt(nc, es, "gsb", [128, KC], F32)
        normg = sbt(nc, es, "normg", [128, 2048], BF16)
        WT = sbt(nc, es, "WT", [128, 16, 128], BF16)
        brow = sbt(nc, es, "brow", [1, 2048], BF16)
        onesr = sbt(nc, es, "onesr", [1, 128], BF16)
        qst = Ring([sbt(nc, es, f"qst{i}", [128, T], BF16) for i in range(2)], dsems(nc, es, "A_qs", 2))
        vst = Ring([sbt(nc, es, f"vst{i}", [128, 8, 256], BF16) for i in range(2)], dsems(nc, es, "A_vs", 2))
        ss = sbt(nc, es, "ss", [128, 128], F32)
        rs = sbt(nc, es, "rs", [128, 128], F32)
        junk = sbt(nc, es, "junk", [128, 128], F32)
        ps = [es.enter_context(nc.psum_tensor(f"psA{i}_{_UID[0]}", [128, 512], F32)) for i in range(8)]
        psr = Ring(ps)

        c1 = P.op("sp", lambda e: e.dma_start(out=gsb[:], in_=io["g1"][:]), inc=S["cld"])
        if "zero_dst" in io:
            zt = sbt(nc, es, "zt", [128, 64], F32)
            zr = P.op("pool", lambda e: e.memset(zt[:], 0.0), inc=S["pool"])
            c1 = P.op("sp", lambda e: e.dma_start(out=io["zero_dst"], in_=zt[:]), waits=[zr], inc=S["cld"])
        P.op("pool", lambda e: e.dma_start(out=normg[:], in_=io["normg"][:]), inc=S["wld"])
        P.op("pool", lambda e: e.dma_start(out=WT[:], in_=io["wsT"][:]), inc=S["wld"])
        c2 = P.op("pool", lambda e: e.dma_start(out=brow[:], in_=io["brow"][:]), inc=S["wld"])
        P.op("pool", lambda e: e.memset(ones[:], 1.0))
        P.op("pool", lambda e: e.memset(onesr[:], 1.0))
        P.op("pool", lambda e: e.memset(ss[:], 0.0))
        cpool = P.op("pool", lambda e: e.affine_select(out=WT[:], in_=WT[:], pattern=[[0, 16], [1, 128]],
                                                       compare_op=ALU.is_ge, fill=0.0, base=0,
                                                       channel_multiplier=-1), waits=[c2], inc=S["pool"])

        h_done, ps_rel = emit_rmsnorm(P, S, lambda c: io["xT"][c], gsb, xs, sq, ones, rstd, ps[0], ps[1],
                                      [cpool], lambda c: hT[:, c, :])
        psr.free_at[0] = ps_rel
        psr.free_at[1] = ps_rel

        mixgA = sbt(nc, es, "mixgA", [128, 16], F32)
        cmg = P.op("sp", lambda e: e.dma_start(out=mixgA[:], in_=io["mixgA"][:]), inc=S["cld"])
        sgu_state = {}

        def emit_sgu(g):
            dv = None
            for tq in range(2):
                bi, pb, pfw = psr.acquire()
                full = None
                for i in range(4):
                    tc = tq * 4 + i
                    P.op("pe", lambda e, pb=pb, i=i, tc=tc, g=g: e.matmul(
                        pb[:, i * 128:(i + 1) * 128], lhsT=vn[:, tc, g * 128:(g + 1) * 128], rhs=WT[:, g, :],
                        start=True, stop=False), waits=[pfw, sgu_state["vn_done"], cpool] if i == 0 else ())
                    full = P.op("pe", lambda e, pb=pb, i=i, g=g: e.matmul(
                        pb[:, i * 128:(i + 1) * 128], lhsT=onesr[0:1, :], rhs=brow[0:1, g * 128:(g + 1) * 128],
                        start=False, stop=True), inc=S["pe"])
                dv = P.op("dve", lambda e, pb=pb, g=g, tq=tq: e.tensor_tensor(
                    out=uT[:, g, tq * 512:(tq + 1) * 512], in0=pb[:], in1=uT[:, g, tq * 512:(tq + 1) * 512],
                    op=ALU.mult), waits=[full, u_ready[g]], inc=S["dve"])
                psr.release(bi, dv)
            j, sqb, sfw = sq.acquire()
            a = P.op("act", lambda e, sqb=sqb, g=g: e.activation(out=sqb[:], in_=uT[:, g, :], func=AF.Square),
                     waits=[dv, sfw], inc=S["act"])
            sgu_state.setdefault("pend", []).append((g, j, sqb, a))

        def emit_sgu2():
            g, j, sqb, a = sgu_state["pend"].pop(0)
            ri, rb, rfw = xs.acquire()
            fulls, banks = [], []
            for th in range(2):
                bi, pb, pfw = psr.acquire()
                banks.append((bi, pb))
                fulls.append(P.op("pe", lambda e, pb=pb, sqb=sqb, th=th: e.matmul(
                    pb[:], lhsT=ones[:], rhs=sqb[:, th * 512:(th + 1) * 512], start=True, stop=True),
                    waits=[a, pfw], inc=S["pe"]))
            sq.release(j, fulls[1])
            a2 = None
            for th in range(2):
                a2 = P.op("act", lambda e, rb=rb, pb=banks[th][1], th=th: e.activation(
                    out=rb[:, th * 512:(th + 1) * 512], in_=pb[:], func=AF.Sqrt, bias=EPS, scale=1.0 / 128),
                    waits=[fulls[th], rfw], inc=S["act"])
                psr.release(banks[th][0], a2)
            P.op("dve", lambda e, rb=rb: e.reciprocal(out=rb[:], in_=rb[:]), waits=[a2, cmg])
            gd = P.op("dve", lambda e, rb=rb, g=g: e.scalar_tensor_tensor(
                out=uT[:, g, :], in0=uT[:, g, :], scalar=mixgA[:, g:g + 1], in1=rb[:], op0=ALU.mult, op1=ALU.mult),
                inc=S["dve"])
            xs.release(ri, gd)
            P.op("sp", lambda e, g=g: e.dma_start(out=io["gT"][g], in_=uT[:, g, :]), waits=[gd], inc=S["st"])

        ws = WStream(P, S, wbufs, dsems(nc, es, "A_wl", 2))
        NB = 40
        pending = []
        for b in range(min(2, NB)):
            pending.append(ws.load(io["w_in"][b], KC))
        u_ready = [None] * 16
        vn_done = None
        for b in range(NB):
            wi, wb, wld = pending.pop(0)
            blk_done = None
            if b < 16 or b >= 32:
                for j in range(2):
                    ch = 2 * b + j if b < 16 else 2 * (b - 32) + j
                    if b < 16:
                        si, stg, sfw = qst.acquire()
                    a = None
                    for th in range(2):
                        bi, pb, pfw = psr.acquire()
                        full = mm_group(P, S, pb, 512,
                                        lambda k, wb=wb, j=j: wb[:, k, j * 128:(j + 1) * 128],
                                        lambda k, th=th: hT[:, k, th * 512:(th + 1) * 512],
                                        KC, [wld, pfw, h_done])
                        if b < 16:
                            a = P.op("act", lambda e, stg=stg, pb=pb, th=th: e.activation(
                                out=stg[:, th * 512:(th + 1) * 512], in_=pb[:], func=AF.Copy),
                                waits=[full, sfw], inc=S["act"])
                        else:
                            a = P.op("act", lambda e, pb=pb, th=th, ch=ch: e.activation(
                                out=uT[:, ch, th * 512:(th + 1) * 512], in_=pb[:], func=AF.Gelu),
                                waits=[full], inc=S["act"])
                            u_ready[ch] = a
                        psr.release(bi, a)
                        blk_done = a
                    if b < 16:
                        hd = ch % 16
                        cho = (hd // 4) * 8 + (4 if ch >= 16 else 0) + hd % 4
                        st = P.op("sp", lambda e, stg=stg, cho=cho: e.dma_start(out=io["qk"][cho], in_=stg[:]),
                                  waits=[a, c1], inc=qst.sems[si])
                        qst.release(si, st)
            else:
                isv = b < 24
                cb = (b - 16) if isv else (b - 24)
                if isv:
                    si, stg, sfw = vst.acquire()
                a = None
                for tc in range(8):
                    bi, pb, pfw = psr.acquire()
                    full = mm_group(P, S, pb, 256,
                                    lambda k, tc=tc: hT[:, k, tc * 128:(tc + 1) * 128],
                                    lambda k, wb=wb: wb[:, k, :],
                                    KC, [wld, pfw, h_done])
                    if isv:
                        a = P.op("act", lambda e, stg=stg, pb=pb, tc=tc: e.activation(
                            out=stg[:, tc, :], in_=pb[:, 0:256], func=AF.Copy), waits=[full, sfw], inc=S["act"])
                    else:
                        a = P.op("act", lambda e, pb=pb, tc=tc, cb=cb: e.activation(
                            out=vn[:, tc, cb * 256:(cb + 1) * 256], in_=pb[:, 0:256], func=AF.Gelu),
                            waits=[full], inc=S["act"])
                        for gg in range(2):
                            g = cb * 2 + gg
                            P.op("act", lambda e, tc=tc, g=g: e.activation(
                                out=junk[:], in_=vn[:, tc, g * 128:(g + 1) * 128], func=AF.Square,
                                accum_out=ss[:, tc * 16 + g:tc * 16 + g + 1]))
                    psr.release(bi, a)
                    blk_done = a
                if isv:
                    vdst = io["v"](cb)
                    st = P.op("sp", lambda e, stg=stg, vdst=vdst: e.dma_start(out=vdst, in_=stg[:]),
                              waits=[a], inc=vst.sems[si])
                    vst.release(si, st)
            ws.release(wi, blk_done)
            if b + 2 < NB:
                pending.append(ws.load(io["w_in"][b + 2], KC))
            if b == 15 and "ag_qk" in io:
                prog_allgather(P, io["cc"], io["ag_qk"], qst.all_done())
            if b == 23 and "ag_v" in io:
                prog_allgather(P, io["cc"], io["ag_v"], vst.all_done())
            if b == 31:
                a = P.op("act", lambda e: e.activation(out=rs[:], in_=ss[:], func=AF.Sqrt, bias=EPS,
                                                       scale=1.0 / 128), inc=S["act"])
                P.op("dve", lambda e: e.reciprocal(out=rs[:], in_=rs[:]), waits=[a, c2])
                for tc in range(8):
                    for g in range(16):
                        col = tc * 16 + g
                        vn_done = P.op("dve", lambda e, tc=tc, g=g, col=col: e.scalar_tensor_tensor(
                            out=vn[:, tc, g * 128:(g + 1) * 128], in0=vn[:, tc, g * 128:(g + 1) * 128],
                            scalar=rs[:, col:col + 1], in1=normg[:, g * 128:(g + 1) * 128],
                            op0=ALU.mult, op1=ALU.mult), inc=S["dve"])
                sgu_state["vn_done"] = vn_done
            if b >= 32:
                while sgu_state.get("pend"):
                    emit_sgu2()
                emit_sgu(2 * (b - 32))
                emit_sgu(2 * (b - 32) + 1)

        while sgu_state.get("pend"):
            emit_sgu2()
        P.wait("sp", (S["st"], S["st"].n))
        for w in qst.all_done() + vst.all_done():
            P.wait("sp", w)
        P.run(nc, dyn)


def attn_groups(d):
    out = []
    nb = 32 // d
    if d == 16:
        for r0 in range(0, 16, 2):
            out.append(([(r0, 0), (r0, 1), (r0 + 1, 0), (r0 + 1, 1)], ("pair", r0)))
    else:
        for r in range(d):
            for n0 in range(0, nb, 4):
                out.append(([(r, n0 + i) for i in range(4)], ("run", r, n0)))
    return out


def phase_B(nc, io, dyn=None):
    P = Prog()
    _UID[0] += 1
    NS = 4096
    with ExitStack() as es:
        S = mk_sems(nc, es, "B")
        S["sfull"] = Sem(nc, es, "B_sfull", 1)
        S["ofull"] = Sem(nc, es, "B_ofull", 1)
        S["fin"] = Sem(nc, es, "B_fin", 1)
        nset = 2
        qT = [sbt(nc, es, f"qT{i}", [128, NS], BF16) for i in range(nset)]
        kT = [sbt(nc, es, f"kT{i}", [128, NS], BF16) for i in range(nset)]
        vd = [{d: sbt(nc, es, f"v{d}_{i}", [128, 32, 128], BF16) for d in (1, 4, 16)} for i in range(nset)]
        inring = Ring(list(range(nset)), dsems(nc, es, "B_hl", nset))
        acc_o = sbt(nc, es, "acc_o", [128, NS], F32)
        acc_z = sbt(nc, es, "acc_z", [128, NS], F32)
        aT = sbt(nc, es, "aT", [128, NS], BF16)
        EB = sbt(nc, es, "EB", [128, 12, 4, 256], BF16)
        sqh = sbt(nc, es, "sqh", [128, NS], BF16)
        mixgB = sbt(nc, es, "mixgB", [128, 4], F32)
        pT = Ring([sbt(nc, es, f"pT{i}", [128, 1024], BF16) for i in range(2)])
        pT2 = Ring([sbt(nc, es, f"pTm{i}", [128, 1024], BF16) for i in range(2)])
        ones = sbt(nc, es, "onesB", [128, 128], BF16)
        nsl = sbt(nc, es, "nsl", [128, 12], F32)
        it = sbt(nc, es, "it", [128, 256], I32)
        dist = sbt(nc, es, "dist", [128, 256], F32)
        etmp = Ring([sbt(nc, es, f"etmp{i}", [128, 256], F32) for i in range(2)])
        ps = [es.enter_context(nc.psum_tensor(f"psB{i}_{_UID[0]}", [128, 512], F32)) for i in range(8)]
        sp_ring = Ring([0, 1])
        o_ring = Ring([0, 1])

        P.op("sp", lambda e: e.dma_start(out=mixgB[:], in_=io["mixgB"][:]), inc=S["cld"])
        c1 = P.op("sp", lambda e: e.dma_start(out=nsl[:], in_=io["nsl"][:]), inc=S["cld"])
        P.op("pool", lambda e: e.memset(ones[:], 1.0))
        P.op("pool", lambda e: e.iota(it[:], [[1, 256]], base=0, channel_multiplier=-1))
        itd = P.op("pool", lambda e: e.memset(ones[:, 0:1], 1.0), inc=S["pool"])
        P.op("dve", lambda e: e.tensor_copy(out=dist[:], in_=it[:]), waits=[itd])
        P.op("dve", lambda e: e.tensor_scalar(out=dist[:, 0:128], in0=dist[:, 0:128], scalar1=128.0,
                                              scalar2=None, op0=ALU.add))
        dready = P.op("dve", lambda e: e.tensor_scalar(out=dist[:, 128:256], in0=dist[:, 128:256],
                                                       scalar1=-128.0, scalar2=None, op0=ALU.add), inc=S["dve"])
        eb_done = None
        for idx in range(12):
            ei, eb, efw = etmp.acquire()
            a = P.op("act", lambda e, eb=eb, idx=idx: e.activation(out=eb[:], in_=dist[:], func=AF.Exp,
                                                                   scale=nsl[:, idx:idx + 1]),
                     waits=[dready, c1, efw], inc=S["act"])
            P.op("pool", lambda e, eb=eb, idx=idx: e.affine_select(
                out=EB[:, idx, 0, 0:128], in_=eb[:, 0:128], pattern=[[-1, 128]], compare_op=ALU.is_ge, fill=0.0,
                base=0, channel_multiplier=1), waits=[a])
            eb_done = P.op("pool", lambda e, eb=eb, idx=idx: e.affine_select(
                out=EB[:, idx, 0, 128:256], in_=eb[:, 128:256], pattern=[[1, 128]], compare_op=ALU.is_ge, fill=0.0,
                base=0, channel_multiplier=-1), inc=S["pool"])
            etmp.release(ei, eb_done)
            for rep in range(1, 4):
                eb_done = P.op("pool", lambda e, idx=idx, rep=rep: e.tensor_copy(
                    out=EB[:, idx, rep, :], in_=EB[:, idx, 0, :]), inc=S["pool"])

        pre_done = None
        if "cc" in io:
            P.wait("sp", (io["cc"], io["cc"].n))
        for (dst, srcf) in io.get("pre", []):
            pre_done = P.op("sp", lambda e, dst=dst, srcf=srcf: e.dma_start(out=dst, in_=srcf()), inc=S["cld"])

        def load_head(h):
            i, si, fw = inring.acquire()
            lsem = inring.sems[i]
            P.op("sp", lambda e: e.dma_start(out=qT[si][:].rearrange("p (r t) -> p r t", r=4), in_=io["q"](h)),
                 waits=[fw, pre_done], inc=lsem)
            P.op("sp", lambda e: e.dma_start(out=kT[si][:].rearrange("p (r t) -> p r t", r=4), in_=io["k"](h)),
                 inc=lsem)
            ld = None
            for d in (1, 4, 16):
                nb = 32 // d
                for r in range(d):
                    ld = P.op("sp", lambda e, d=d, r=r, nb=nb: e.dma_start(
                        out=vd[si][d][:, r * nb:(r + 1) * nb, :], in_=io["v"](h, d, r)), inc=lsem)
            return i, si, ld

        nxt = load_head(0)
        a_st = None
        fin_prev = None
        for h in range(4):
            ri, si, ld = nxt
            if h + 1 < 4:
                nxt = load_head(h + 1)
            q_, k_ = qT[si], kT[si]
            last_pe_head = None
            for di, d in enumerate((1, 4, 16)):
                nb = 32 // d
                groups = attn_groups(d)
                qv = q_[:].rearrange("p (m r) -> p r m", r=d)
                kv = k_[:].rearrange("p (m r) -> p r m", r=d)
                ebi = h * 3 + di
                state = {}

                def p1(gi):
                    blocks, sel = groups[gi]
                    _, sset, sfw = sp_ring.acquire()
                    full = None
                    first = True
                    for b, (r, n) in enumerate(blocks):
                        bank = ps[4 * sset + b // 2]
                        off = (b % 2) * 256
                        qa = qv[:, r, n * 128:(n + 1) * 128]
                        if n > 0:
                            ka = kv[:, r, (n - 1) * 128:n * 128]
                            P.op("pe", lambda e, bank=bank, off=off, ka=ka, qa=qa: e.matmul(
                                bank[:, off:off + 128], lhsT=ka, rhs=qa, start=True, stop=True),
                                waits=[sfw, ld] if first else ())
                            first = False
                        ka = kv[:, r, n * 128:(n + 1) * 128]
                        fn = lambda e, bank=bank, off=off, ka=ka, qa=qa: e.matmul(
                            bank[:, off + 128:off + 256], lhsT=ka, rhs=qa, start=True, stop=True)
                        if b == 3:
                            full = P.op("pe", fn, waits=[sfw, ld] if first else (), inc=S["sfull"])
                        else:
                            P.op("pe", fn, waits=[sfw, ld] if first else ())
                        first = False
                    pi, pbuf, pfw = pT.acquire()
                    P.op("act", lambda e, pbuf=pbuf, sset=sset: e.activation(
                        out=pbuf[:, 0:512], in_=ps[4 * sset][:], func=AF.Exp, scale=SCALE), waits=[full, pfw])
                    a = P.op("act", lambda e, pbuf=pbuf, sset=sset: e.activation(
                        out=pbuf[:, 512:1024], in_=ps[4 * sset + 1][:], func=AF.Exp, scale=SCALE), inc=S["act"])
                    sp_ring.release(sset, a)
                    mi, mbuf, mfw = pT2.acquire()
                    meng = "dve" if gi % 2 == 0 else "pool"
                    pl = P.op(meng, lambda e, mbuf=mbuf, pbuf=pbuf, ebi=ebi: e.tensor_tensor(
                        out=mbuf[:].rearrange("p (b c) -> p b c", b=4),
                        in0=pbuf[:].rearrange("p (b c) -> p b c", b=4), in1=EB[:, ebi, :, :], op=ALU.mult),
                        waits=[a, mfw, eb_done], inc=S[meng])
                    pT.release(pi, pl)
                    state[gi] = (mi, mbuf, pl)

                def p2(gi):
                    blocks, sel = groups[gi]
                    mi, mbuf, pl = state.pop(gi)
                    _, oset, ofw = o_ring.acquire()
                    po, pz = ps[4 * oset + 2], ps[4 * oset + 3]
                    vt = vd[si][d]
                    full = None
                    first = True
                    for b, (r, n) in enumerate(blocks):
                        j = r * nb + n
                        for tgt, lfn in ((po, lambda jj: vt[:, jj, :]), (pz, lambda jj: ones[:])):
                            oa = tgt[:, b * 128:(b + 1) * 128]
                            if n > 0:
                                P.op("pe", lambda e, oa=oa, l=lfn(j - 1), b=b: e.matmul(
                                    oa, lhsT=l, rhs=mbuf[:, b * 256:b * 256 + 128], start=True, stop=False),
                                    waits=[pl, ofw] if first else ())
                                first = False
                            fn = lambda e, oa=oa, l=lfn(j), b=b, n=n: e.matmul(
                                oa, lhsT=l, rhs=mbuf[:, b * 256 + 128:b * 256 + 256], start=(n == 0), stop=True)
                            if b == 3 and tgt is pz:
                                full = P.op("pe", fn, waits=[pl, ofw] if first else (), inc=S["ofull"])
                            else:
                                P.op("pe", fn, waits=[pl, ofw] if first else ())
                            first = False
                    pT2.release(mi, full)
                    if sel[0] == "pair":
                        r0 = sel[1]
                        ao = acc_o[:].rearrange("p (m r) -> p r m", r=16)[:, r0:r0 + 2, :]
                        az = acc_z[:].rearrange("p (m r) -> p r m", r=16)[:, r0:r0 + 2, :]
                        pov = po[:].rearrange("p (a m) -> p a m", a=2)
                        pzv = pz[:].rearrange("p (a m) -> p a m", a=2)
                    else:
                        _, r, n0 = sel
                        ao = acc_o[:].rearrange("p (m r) -> p r m", r=d)[:, r, n0 * 128:n0 * 128 + 512]
                        az = acc_z[:].rearrange("p (m r) -> p r m", r=d)[:, r, n0 * 128:n0 * 128 + 512]
                        pov, pzv = po[:], pz[:]
                    if d == 1:
                        P.op("dve", lambda e, ao=ao, pov=pov: e.tensor_copy(out=ao, in_=pov),
                             waits=[full, fin_prev])
                        dv = P.op("dve", lambda e, az=az, pzv=pzv: e.tensor_copy(out=az, in_=pzv), inc=S["dve"])
                    else:
                        P.op("dve", lambda e, ao=ao, pov=pov: e.tensor_tensor(out=ao, in0=ao, in1=pov, op=ALU.add),
                             waits=[full])
                        dv = P.op("dve", lambda e, az=az, pzv=pzv: e.tensor_tensor(out=az, in0=az, in1=pzv,
                                                                                   op=ALU.add), inc=S["dve"])
                    o_ring.release(oset, dv)
                    return full

                ng = len(groups)
                p1(0)
                for gi in range(ng):
                    if gi + 1 < ng:
                        p1(gi + 1)
                    last_pe_head = p2(gi)
                    if di == 0 and gi == 5 and h > 0 and "ag_a" in io:
                        prog_allgather(P, io["cc"], [io["ag_a"][h - 1]], [a_st])
            inring.release(ri, last_pe_head)
            P.op("dve", lambda e: e.reciprocal(out=acc_z[:], in_=acc_z[:]))
            fin = P.op("dve", lambda e: e.tensor_tensor(out=aT[:], in0=acc_o[:], in1=acc_z[:], op=ALU.mult),
                       waits=[a_st], inc=S["fin"])
            asq = P.op("act", lambda e: e.activation(out=sqh[:], in_=aT[:], func=AF.Square), waits=[fin],
                       inc=S["act"])
            a2 = None
            for qq in range(4):
                _, oset, ofw = o_ring.acquire()
                po, pz = ps[4 * oset + 2], ps[4 * oset + 3]
                P.op("pe", lambda e, po=po, qq=qq: e.matmul(po[:], lhsT=ones[:], rhs=sqh[:, qq * 1024:qq * 1024 + 512],
                                                            start=True, stop=True), waits=[asq, ofw])
                full = P.op("pe", lambda e, pz=pz, qq=qq: e.matmul(
                    pz[:], lhsT=ones[:], rhs=sqh[:, qq * 1024 + 512:qq * 1024 + 1024], start=True, stop=True),
                    inc=S["ofull"])
                P.op("act", lambda e, po=po, qq=qq: e.activation(
                    out=acc_z[:, qq * 1024:qq * 1024 + 512], in_=po[:], func=AF.Sqrt, bias=EPS, scale=1.0 / 128),
                    waits=[full])
                a2 = P.op("act", lambda e, pz=pz, qq=qq: e.activation(
                    out=acc_z[:, qq * 1024 + 512:qq * 1024 + 1024], in_=pz[:], func=AF.Sqrt, bias=EPS,
                    scale=1.0 / 128), inc=S["act"])
                o_ring.release(oset, a2)
            P.op("dve", lambda e: e.reciprocal(out=acc_z[:], in_=acc_z[:]), waits=[a2, c1])
            fin2 = P.op("dve", lambda e, h=h: e.scalar_tensor_tensor(
                out=aT[:], in0=aT[:], scalar=mixgB[:, h:h + 1], in1=acc_z[:], op0=ALU.mult, op1=ALU.mult),
                inc=S["fin"])
            a_st = P.op("sp", lambda e, h=h: e.dma_start(out=io["a"](h), in_=aT[:].rearrange(
                "p (tb t) -> p tb t", tb=4)), waits=[fin2], inc=S["st"])
            fin_prev = None
        if "ag_a" in io:
            prog_allgather(P, io["cc"], [io["ag_a"][3]], [a_st])
        P.wait("sp", (S["st"], S["st"].n))
        P.run(nc, dyn)


class XPipe:
    def __init__(self, P, xs, x_src, order, st_hist):
        self.P, self.xs, self.x_src, self.order, self.st_hist = P, xs, x_src, order, st_hist
        self.nxt = 0
        self.loaded = {}

    def issue(self):
        if self.nxt >= len(self.order):
            return
        ch = self.order[self.nxt]
        self.nxt += 1
        i, xb, fw = self.xs.acquire()
        src = self.x_src(ch)
        xl = self.P.op("sp", lambda e, xb=xb, src=src: e.dma_start(out=xb[:], in_=src),
                       waits=[fw, self.st_hist.get(ch)], inc=self.xs.sems[i])
        self.loaded[ch] = (i, xb, xl)


def emit_resid_block(P, S, psr, xp, xo, wb, wld, b, nk, rhs_fn, x_dst, ready, st_hist, halo_dst=None):
    blk_done = None
    for j in range(2):
        ch = 2 * b + j
        xi, xb, xl = xp.loaded.pop(ch)
        oi, ob, ofw = xo.acquire()
        dv = None
        for th in range(2):
            bi, pb, pfw = psr.acquire()
            full = mm_group(P, S, pb, 512, lambda k, wb=wb, j=j: wb[:, k, j * 128:(j + 1) * 128],
                            lambda k, th=th: rhs_fn(k, th), nk, [wld, pfw] + list(ready))
            dv = P.op("dve", lambda e, ob=ob, pb=pb, xb=xb, th=th: e.tensor_tensor(
                out=ob[:, th * 512:(th + 1) * 512], in0=pb[:], in1=xb[:, th * 512:(th + 1) * 512],
                op=ALU.add), waits=[full, xl, ofw], inc=S["dve"])
            psr.release(bi, dv)
            blk_done = full
        xp.xs.release(xi, dv)
        dst = x_dst(ch)
        if halo_dst is not None:
            hd = halo_dst(ch)
            P.op("sp", lambda e, ob=ob, hd=hd: e.dma_start(out=hd, in_=ob[:, T - 2:T]), waits=[dv],
                 inc=xo.sems[oi])
        st = P.op("sp", lambda e, ob=ob, dst=dst: e.dma_start(out=dst, in_=ob[:]), waits=[dv],
                  inc=xo.sems[oi])
        st_hist[ch] = st
        xo.release(oi, st)
        xp.issue()
    return blk_done


def phase_C1(nc, io, dyn=None):
    P = Prog()
    _UID[0] += 1
    with ExitStack() as es:
        S = mk_sems(nc, es, "C1")
        S["ld2"] = Sem(nc, es, "C1_ld2", 16)
        mT = sbt(nc, es, "mT", [128, KC, T], BF16)
        wbufs = [sbt(nc, es, f"wc{i}", [128, 16, 256], BF16) for i in range(4)]
        xs = Ring([sbt(nc, es, f"xc{i}", [128, T], F32) for i in range(3)], dsems(nc, es, "C1_xl", 3))
        xo = Ring([sbt(nc, es, f"xo{i}", [128, T], F32) for i in range(2)], dsems(nc, es, "C1_xs", 2))
        ps = [es.enter_context(nc.psum_tensor(f"psC{i}_{_UID[0]}", [128, 512], F32)) for i in range(8)]
        psr = Ring(ps)

        ld_g = None
        for c in range(16):
            ld_g = P.op("sp", lambda e, c=c: e.dma_start(out=mT[:, 16 + c, :], in_=io["gT"][c]), inc=S["ld"])
        st_hist = {}
        ws = WStream(P, S, wbufs, dsems(nc, es, "C1_wl", 4))
        sched = [(0, b) for b in range(16)] + [(1, b) for b in range(16)]
        pending = [ws.load(io["w_out"][hf][b], 16) for (hf, b) in sched[:4]]
        nxt = 4
        xp = XPipe(P, xs, lambda ch: io["xT"][ch], list(range(KC)), st_hist)
        for _ in range(3):
            xp.issue()
        ld_a = None
        for (hf, b) in sched:
            if hf == 1 and b == 0:
                if "cc" in io:
                    P.wait("sp", (io["cc"], io["cc"].n))
                pre_done = None
                for (dst, srcf) in io.get("pre", []):
                    pre_done = P.op("sp", lambda e, dst=dst, srcf=srcf: e.dma_start(out=dst, in_=srcf()),
                                    inc=S["cld"])
                P.wait("sp", pre_done)
                for c in range(16):
                    ld_a = P.op("sp", lambda e, c=c: e.dma_start(out=mT[:, c, :], in_=io["aT"](c)), inc=S["ld2"])
                xp = XPipe(P, xs, lambda ch: io["xo"][ch], list(range(KC)), st_hist)
                for _ in range(3):
                    xp.issue()
            wi, wb, wld = pending.pop(0)
            if hf == 0:
                blk_done = emit_resid_block(P, S, psr, xp, xo, wb, wld, b, 16,
                                            lambda k, th: mT[:, 16 + k, th * 512:(th + 1) * 512],
                                            lambda ch: io["xo"][ch], [ld_g], st_hist)
            else:
                blk_done = emit_resid_block(P, S, psr, xp, xo, wb, wld, b, 16,
                                            lambda k, th: mT[:, k, th * 512:(th + 1) * 512],
                                            lambda ch: io["xo"][ch], [ld_a], st_hist,
                                            halo_dst=io.get("halo_dst"))
            ws.release(wi, blk_done)
            if nxt < len(sched):
                pending.append(ws.load(io["w_out"][sched[nxt][0]][sched[nxt][1]], 16))
                nxt += 1
        for w in xo.all_done():
            P.wait("sp", w)
        if "ag_xh" in io:
            prog_allgather(P, io["cc"], io["ag_xh"], xo.all_done())
        P.run(nc, dyn)


def phase_C2(nc, io, final, dyn=None):
    P = Prog()
    _UID[0] += 1
    with ExitStack() as es:
        S = mk_sems(nc, es, "C2")
        S["act2"] = Sem(nc, es, "C2_act2", 1)
        S["dve2"] = Sem(nc, es, "C2_dve2", 1)
        hT = sbt(nc, es, "h2T", [128, KC, T], BF16)
        hh = sbt(nc, es, "hh", [128, KC, 2], BF16)
        gsl = sbt(nc, es, "gsl", [128, 22, T], BF16)
        wbufs = [sbt(nc, es, f"wf{i}", [128, KC, 256], BF16) for i in range(2)]
        xs = Ring([sbt(nc, es, f"xf{i}", [128, T], F32) for i in range(2)], dsems(nc, es, "C2_xl", 2))
        xo = Ring([sbt(nc, es, f"xg{i}", [128, T], F32) for i in range(2)], dsems(nc, es, "C2_xs", 2))
        sq = Ring([sbt(nc, es, f"sqf{i}", [128, T], BF16) for i in range(2)])
        rstd = sbt(nc, es, "rstdf", [128, T], F32)
        ones = sbt(nc, es, "onesF", [128, 128], BF16)
        g2 = sbt(nc, es, "g2s", [128, KC], F32)
        gf = sbt(nc, es, "gfs", [128, KC], F32)
        cw = sbt(nc, es, "cws", [128, 172, 3], F32)
        cb = sbt(nc, es, "cbs", [128, 172], F32)
        xh = sbt(nc, es, "xhs", [128, KC, 2], F32)
        xh2 = sbt(nc, es, "xh2", [128, KC, 2], BF16)
        rh = sbt(nc, es, "rh", [128, 2], F32)
        U = Ring([(sbt(nc, es, f"Ug{i}", [128, T + 2], F32), sbt(nc, es, f"Uv{i}", [128, T + 2], F32))
                  for i in range(2)])
        cc = Ring([(sbt(nc, es, f"cg{i}", [128, 512], F32), sbt(nc, es, f"cv{i}", [128, 512], F32))
                   for i in range(2)])
        sgr = Ring([sbt(nc, es, f"sg{i}", [128, 512], F32) for i in range(2)])
        ps = [es.enter_context(nc.psum_tensor(f"psF{i}_{_UID[0]}", [128, 512], F32)) for i in range(8)]
        psr = Ring(ps[0:7])
        ph = ps[7]

        c1 = None
        for dst, src in ((g2, "g2"), (cw, "cw"), (cb, "cb"), (xh, "xh")) + (((gf, "gf"),) if final else ()):
            if callable(io[src]):
                if "cc" in io:
                    P.wait("sp", (io["cc"], io["cc"].n))
                c1 = P.op("sp", lambda e, dst=dst, src=src: e.dma_start(
                    out=dst[:].rearrange("p k t -> p (k t)"), in_=io[src]()), inc=S["cld"])
            else:
                c1 = P.op("sp", lambda e, dst=dst, src=src: e.dma_start(out=dst[:], in_=io[src][:]), inc=S["cld"])
        ones_ok = P.op("pool", lambda e: e.memset(ones[:], 1.0), inc=S["pool"])

        a = P.op("act", lambda e: e.activation(out=xh2[:], in_=xh[:], func=AF.Square), waits=[c1], inc=S["act"])
        full = mm_group(P, S, ph, 2, lambda k: ones[:], lambda k: xh2[:, k, :], KC, [a, ones_ok])
        a = P.op("act", lambda e: e.activation(out=rh[:], in_=ph[:, 0:2], func=AF.Sqrt, bias=EPS, scale=1.0 / D),
                 waits=[full], inc=S["act"])
        ph_free = a
        P.op("dve", lambda e: e.reciprocal(out=rh[:], in_=rh[:]), waits=[a])
        hh_done = None
        for c in range(KC):
            hh_done = P.op("dve", lambda e, c=c: e.scalar_tensor_tensor(
                out=hh[:, c, :], in0=xh[:, c, :], scalar=g2[:, c:c + 1], in1=rh[:], op0=ALU.mult, op1=ALU.mult),
                inc=S["dve"])

        h_done, ps_rel = emit_rmsnorm(P, S, lambda c: io["xT"][c], g2, xs, sq, ones, rstd, ps[0], ps[1],
                                      [ones_ok], lambda c: hT[:, c, :])
        psr.free_at[0] = ps_rel
        psr.free_at[1] = ps_rel

        ws = WStream(P, S, wbufs, dsems(nc, es, "C2_wl", 2))
        sched = []
        for s, (k0, nk) in enumerate(SLABS):
            for jj in range(nk):
                sched.append(("up", s, jj, io["w_up"][k0 + jj], KC))
            for b in range(16):
                sched.append(("dn", s, b, io[f"w_dn{s}"][b], nk))
        pending = [ws.load(sched[i][3], sched[i][4]) for i in range(2)]
        nxt_load = 2
        st_hist = {}
        stage_b = []
        g_done = None
        xp = None

        def emit_stage_b():
            (cgb, cvb, ci, jj, th, dvc) = stage_b.pop(0)
            gi_, sgb, gfw = sgr.acquire()
            a2 = P.op("act", lambda e, sgb=sgb, cgb=cgb: e.activation(out=sgb[:], in_=cgb[:], func=AF.Silu),
                      waits=[dvc, gfw], inc=S["act2"])
            d2 = P.op("dve", lambda e, sgb=sgb, cvb=cvb, jj=jj, th=th: e.tensor_tensor(
                out=gsl[:, jj, th * 512:(th + 1) * 512], in0=sgb[:], in1=cvb[:], op=ALU.mult),
                waits=[a2], inc=S["dve2"])
            sgr.release(gi_, d2)
            cc.release(ci, d2)
            return d2

        for it_, (kind, s, idx, src, nkc) in enumerate(sched):
            wi, wb, wld = pending.pop(0)
            k0, nk = SLABS[s]
            blk_done = None
            if kind == "up":
                jj = idx
                cg_, cv_ = k0 + jj, FC + k0 + jj
                ui, (Ug, Uv), ufw = U.acquire()
                mm_group(P, S, ph, 2, lambda k, wb=wb: wb[:, k, 0:128], lambda k: hh[:, k, :], KC,
                         [wld, ph_free, hh_done], out_ap=ph[:, 0:2])
                hfull = mm_group(P, S, ph, 2, lambda k, wb=wb: wb[:, k, 128:256], lambda k: hh[:, k, :], KC,
                                 [], out_ap=ph[:, 2:4])
                P.op("act", lambda e, Ug=Ug: e.activation(out=Ug[:, 0:2], in_=ph[:, 0:2], func=AF.Copy),
                     waits=[hfull, ufw])
                ph_free = P.op("act", lambda e, Uv=Uv: e.activation(out=Uv[:, 0:2], in_=ph[:, 2:4], func=AF.Copy),
                               inc=S["act"])
                for th in range(2):
                    big, pg, pgw = psr.acquire()
                    fg = mm_group(P, S, pg, 512, lambda k, wb=wb: wb[:, k, 0:128],
                                  lambda k, th=th: hT[:, k, th * 512:(th + 1) * 512], KC, [wld, pgw, h_done])
                    biv, pv, pvw = psr.acquire()
                    fv = mm_group(P, S, pv, 512, lambda k, wb=wb: wb[:, k, 128:256],
                                  lambda k, th=th: hT[:, k, th * 512:(th + 1) * 512], KC, [pvw])
                    blk_done = fv
                    ci, (cgb, cvb), cfw = cc.acquire()
                    lo = 2 + th * 512
                    P.op("act", lambda e, Ug=Ug, pg=pg, lo=lo: e.activation(out=Ug[:, lo:lo + 512], in_=pg[:],
                                                                            func=AF.Copy), waits=[fg, ufw])
                    a = P.op("act", lambda e, cgb=cgb, pg=pg, c=cg_: e.activation(
                        out=cgb[:], in_=pg[:], func=AF.Identity, bias=cb[:, c:c + 1], scale=cw[:, c, 2:3]),
                        waits=[cfw, c1], inc=S["act"])
                    psr.release(big, a)
                    P.op("act", lambda e, Uv=Uv, pv=pv, lo=lo: e.activation(out=Uv[:, lo:lo + 512], in_=pv[:],
                                                                            func=AF.Copy), waits=[fv])
                    a = P.op("act", lambda e, cvb=cvb, pv=pv, c=cv_: e.activation(
                        out=cvb[:], in_=pv[:], func=AF.Identity, bias=cb[:, c:c + 1], scale=cw[:, c, 2:3]),
                        inc=S["act"])
                    psr.release(biv, a)
                    dvc = None
                    for (ub, cbuf, c) in ((Ug, cgb, cg_), (Uv, cvb, cv_)):
                        P.op("dve", lambda e, ub=ub, cbuf=cbuf, c=c, lo=lo: e.scalar_tensor_tensor(
                            out=cbuf[:], in0=ub[:, lo - 1:lo + 511], scalar=cw[:, c, 1:2], in1=cbuf[:],
                            op0=ALU.mult, op1=ALU.add), waits=[a])
                        dvc = P.op("dve", lambda e, ub=ub, cbuf=cbuf, c=c, lo=lo: e.scalar_tensor_tensor(
                            out=cbuf[:], in0=ub[:, lo - 2:lo + 510], scalar=cw[:, c, 0:1], in1=cbuf[:],
                            op0=ALU.mult, op1=ALU.add), inc=S["dve"])
                    if th == 1:
                        U.release(ui, dvc)
                    stage_b.append((cgb, cvb, ci, jj, th, dvc))
                    if len(stage_b) > 1:
                        emit_stage_b()
                if jj == nk - 1:
                    while stage_b:
                        g_done = emit_stage_b()
                    xsrc = (lambda ch: io["xT"][ch]) if s == 0 else (lambda ch: io["xw"][ch])
                    xp = XPipe(P, xs, xsrc, list(range(KC)), st_hist)
                    xp.issue()
                    xp.issue()
            else:
                blk_done = emit_resid_block(P, S, psr, xp, xo, wb, wld, idx, nk,
                                            lambda k, th: gsl[:, k, th * 512:(th + 1) * 512],
                                            lambda ch: io["xw"][ch], [g_done], st_hist)
            ws.release(wi, blk_done)
            if nxt_load < len(sched):
                pending.append(ws.load(sched[nxt_load][3], sched[nxt_load][4]))
                nxt_load += 1
        if final:
            last_pe = None
            for c in range(KC):
                i, xb, fw = xs.acquire()
                ld = P.op("sp", lambda e, xb=xb, c=c: e.dma_start(out=xb[:], in_=io["xw"][c]),
                          waits=[fw] + xo.all_done(), inc=xs.sems[i])
                j, sqb, sfw = sq.acquire()
                a = P.op("act", lambda e, xb=xb, sqb=sqb: e.activation(out=sqb[:], in_=xb[:], func=AF.Square),
                         waits=[ld, sfw], inc=S["act"])
                xs.release(i, a)
                w0 = [a] + ([(S["dve"], S["dve"].n)] if c == 0 else [])
                P.op("pe", lambda e, sqb=sqb, c=c: e.matmul(ps[0][:], lhsT=ones[:], rhs=sqb[:, 0:512],
                                                            start=(c == 0), stop=(c == KC - 1)), waits=w0)
                last_pe = P.op("pe", lambda e, sqb=sqb, c=c: e.matmul(ps[1][:], lhsT=ones[:], rhs=sqb[:, 512:1024],
                                                                      start=(c == 0), stop=(c == KC - 1)),
                               inc=S["pe"])
                sq.release(j, last_pe)
            P.op("act", lambda e: e.activation(out=rstd[:, 0:512], in_=ps[0][:], func=AF.Sqrt, bias=EPS,
                                               scale=1.0 / D), waits=[last_pe])
            a = P.op("act", lambda e: e.activation(out=rstd[:, 512:1024], in_=ps[1][:], func=AF.Sqrt, bias=EPS,
                                                   scale=1.0 / D), inc=S["act"])
            P.op("dve", lambda e: e.reciprocal(out=rstd[:], in_=rstd[:]), waits=[a])
            for c in range(KC):
                i, xb, fw = xs.acquire()
                ld = P.op("sp", lambda e, xb=xb, c=c: e.dma_start(out=xb[:], in_=io["xw"][c]), waits=[fw],
                          inc=xs.sems[i])
                oi, ob, ofw = xo.acquire()
                dv = P.op("dve", lambda e, xb=xb, ob=ob, c=c: e.scalar_tensor_tensor(
                    out=ob[:], in0=xb[:], scalar=gf[:, c:c + 1], in1=rstd[:], op0=ALU.mult, op1=ALU.mult),
                    waits=[ld, ofw], inc=S["dve"])
                xs.release(i, dv)
                st = P.op("sp", lambda e, ob=ob, c=c: e.dma_start(out=io["out"][c], in_=ob[:]), waits=[dv],
                          inc=xo.sems[oi])
                xo.release(oi, st)
        for w in xo.all_done():
            P.wait("sp", w)
        P.run(nc, dyn)


GROUPS = [[0, 1, 2, 3], [4, 5, 6, 7]]


def _blocks(w, order=None):
    K, N = w.shape
    a = w.reshape(K // 128, 128, N // 256, 256).transpose(2, 1, 0, 3)
    if order is not None:
        a = a[order]
    return np.ascontiguousarray(a)


def _fm(v):
    return np.ascontiguousarray(v.reshape(-1, 128).T)


def _dram(nc, name, shape, dt, kind):
    return nc.dram_tensor(name, list(shape), dt, kind=kind).ap()


def prog_allgather(P, cc, pairs, waits):
    r = None
    for i, (src, dst) in enumerate(pairs):
        r = P.op("pool", lambda e, src=src, dst=dst: e.collective_compute(
            "AllGather", ALU.bypass, replica_groups=GROUPS, ins=[src], outs=[dst]),
            waits=waits if i == 0 else (), inc=cc)
    return r


def emit_allgather(nc, cc, pairs):
    with nc.Block() as block:
        @block.gpsimd
        def _(g):
            for (src, dst) in pairs:
                g.collective_compute("AllGather", ALU.bypass, replica_groups=GROUPS, ins=[src], outs=[dst]
                                     ).then_inc(cc.h, 1)
                cc.n += 1
            g.wait_ge(cc.h, cc.n)


def build_fused(L=2, stop=99):
    _POOL.clear()
    nc = bass.Bass("TRN2", target_bir_lowering=False, num_devices=NCORES)
    EI, IN = "ExternalInput", "Internal"
    dyn = {}
    xT = _dram(nc, "xT", [KC, 128, T], F32, EI)
    nsl = _dram(nc, "nsl", [128, 12], F32, EI)
    gf = _dram(nc, "gf", [128, KC], F32, EI)
    out = _dram(nc, "out", [KC, 128, T], F32, "ExternalOutput")
    xA = _dram(nc, "xA", [KC, 128, T], F32, IN)
    xB = _dram(nc, "xB", [KC, 128, T], F32, IN)
    qk_send = _dram(nc, "qk_send", [32, 128, T], BF16, IN)
    v_send2 = _dram(nc, "v_send2", [4, T, 512], BF16, IN)
    gT = _dram(nc, "gTd", [16, 128, T], BF16, IN)
    qk_all = _dram(nc, "qk_all", [4 * 32 * 128, T], BF16, IN)
    v_all2 = _dram(nc, "v_all2", [4, 4 * T, 512], BF16, IN)
    a_send3 = _dram(nc, "a_send3", [4, 128, 4096], BF16, IN)
    a_all3 = _dram(nc, "a_all3", [4, 4, 128, 4096], BF16, IN)
    xh_send = _dram(nc, "xh_send", [128, 2 * KC], F32, IN)
    xh_ext = _dram(nc, "xh_ext", [5 * 128, 2 * KC], F32, IN)
    my_qk2 = _dram(nc, "my_qk2", [2, 4, 4, 128, T], BF16, IN)
    my_v = _dram(nc, "my_v", [4096, 512], BF16, IN)
    my_a = _dram(nc, "my_a", [16, 128, T], BF16, IN)
    cc = Sem(nc, None, "cc", 1)

    qk_s2 = qk_send.rearrange("c p t -> (c p) t")
    ag_qkv = [(qk_s2[j * 512:(j + 1) * 512, :], qk_all[j * 2048:(j + 1) * 2048, :]) for j in range(8)]
    ag_qkv += [(v_send2[g], v_all2[g]) for g in range(4)]
    ag_a = [(a_send3[h], a_all3[h].rearrange("rr p t -> (rr p) t")) for h in range(4)]
    xh5 = xh_ext.rearrange("(b p) c -> b p c", p=128)

    with nc.Block() as block0:
        @block0.sync
        def _(sp):
            dyn["r"] = sp.snap(sp.partition_id() % 4, min_val=0, max_val=3)
            dyn["r2"] = sp.snap(dyn["r"] * 2, min_val=0, max_val=6)

    x_in = xT
    for l in range(L):
        wl = [("g1", [128, KC]), ("w_in", [40, 128, KC, 256]), ("normg", [128, 2048]), ("wsT", [128, 16, 128]),
              ("brow", [1, 2048])]
        wl += [("mixgA", [128, 16]), ("mixgB", [128, 4])]
        if stop >= 5:
            wl += [("w_out", [2, 16, 128, 16, 256])]
        if stop >= 7:
            wl += [("g2", [128, KC]), ("w_up", [FC, 128, KC, 256]), ("cw", [128, 172, 3]), ("cb", [128, 172])]
        W = {k: _dram(nc, f"{k}_{l}", shp, F32, EI) for k, shp in wl}
        if stop >= 7:
            for s_, (k0, nk) in enumerate(SLABS):
                W[f"w_dn{s_}"] = _dram(nc, f"w_dn{s_}_{l}", [16, 128, nk, 256], F32, EI)
        ioA = {"xT": x_in, "g1": W["g1"], "w_in": W["w_in"], "normg": W["normg"], "wsT": W["wsT"],
               "brow": W["brow"], "qk": qk_send, "gT": gT, "mixgA": W["mixgA"],
               "v": lambda cb: v_send2[cb // 2].rearrange("(tc p) c -> p tc c", p=128)[
                   :, :, (cb % 2) * 256:(cb % 2) * 256 + 256]}
        if l == 0:
            ioA["zero_dst"] = xh_ext[0:128, :]
        ioA.update({"cc": cc, "ag_qk": ag_qkv[0:8], "ag_v": ag_qkv[8:12]})
        phase_A(nc, ioA, dyn)
        if stop < 2:
            break
        ioB = {
            "pre": [
                (my_qk2.rearrange("jj rr c p t -> jj (rr c p) t"),
                 lambda: qk_all.rearrange("(j x) t -> j x t", j=8)[bass.ds(dyn["r2"], 2)]),
                (my_v[:], lambda: v_all2[bass.ds(dyn["r"], 1)].rearrange("a t c -> (a t) c")),
            ],
            "q": lambda h: my_qk2[0][:, h].rearrange("rr p t -> p rr t"),
            "k": lambda h: my_qk2[1][:, h].rearrange("rr p t -> p rr t"),
            "v": lambda h, d, r: my_v[:, h * 128:(h + 1) * 128].rearrange("(n p r) e -> r p n e", p=128, r=d)[r],
            "nsl": nsl, "a": lambda h: a_send3[h].rearrange("p (tb t) -> p tb t", tb=4),
            "cc": cc, "ag_a": ag_a, "mixgB": W["mixgB"],
        }
        if stop < 3:
            break
        phase_B(nc, ioB, dyn)
        if stop < 4:
            break
        ioC1 = {"xT": x_in,
                "pre": [(my_a.rearrange("h p t -> (h p) t"),
                         lambda: a_all3.rearrange("h rr p (tb t) -> tb (h rr p) t", tb=4)[
                             bass.ds(dyn["r"], 1)].rearrange("a x t -> (a x) t"))],
                "aT": lambda c: my_a[(c % 4) * 4 + c // 4],
                "cc": cc, "ag_xh": [(xh_send[:], xh_ext[128:640, :])],
                "gT": gT, "w_out": W["w_out"], "xo": xA,
                "halo_dst": lambda ch: xh_send[:, 2 * ch:2 * ch + 2]}
        if stop < 5:
            break
        phase_C1(nc, ioC1, dyn)
        if stop < 6:
            break
        if stop < 7:
            break
        final = (l == L - 1)
        ioC2 = {"xT": xA, "xh": lambda: xh5[bass.ds(dyn["r"], 1)].rearrange("a p c -> p (a c)"),
                "g2": W["g2"], "w_up": W["w_up"], "cw": W["cw"], "cb": W["cb"], "xw": xB, "cc": cc}
        for s_ in range(4):
            ioC2[f"w_dn{s_}"] = W[f"w_dn{s_}"]
        if final:
            ioC2["gf"] = gf
            ioC2["out"] = out
        phase_C2(nc, ioC2, final, dyn)
        x_in = xB
    return nc


def _to_xT(x):
    out = []
    for c in range(NCORES):
        b, r = divmod(c, 4)
        xs = x[b, r * T:(r + 1) * T, :]
        out.append(np.ascontiguousarray(xs.T.reshape(KC, 128, T)))
    return out


def kernel(x, attn_norm_g, w_in, sgu_norm_g, w_spatial, b_spatial, mix_norm_g, w_out,
           ffn_norm_g, w_up, conv_w, conv_b, w_down, final_norm_g):
    f = lambda a: np.asarray(a, dtype=np.float32)
    L = 2
    xT = _to_xT(f(x))
    slopes = 2.0 ** (-8.0 * np.arange(1, 17, dtype=np.float64) / 16.0)
    shared = {"gf": _fm(f(final_norm_g))}
    percore = {}
    order = list(range(24)) + list(range(32, 40)) + list(range(24, 32))
    for l in range(L):
        shared[f"g1_{l}"] = _fm(f(attn_norm_g[l]))
        shared[f"w_in_{l}"] = _blocks(f(w_in[l]), order)
        shared[f"normg_{l}"] = np.ascontiguousarray(np.broadcast_to(f(sgu_norm_g[l])[None, :], (128, 2048)))
        shared[f"wsT_{l}"] = np.ascontiguousarray(f(w_spatial[l]).transpose(2, 0, 1))
        shared[f"brow_{l}"] = np.ascontiguousarray(f(b_spatial[l]).reshape(1, 2048))
        mg = _fm(f(mix_norm_g[l]))
        shared[f"mixgA_{l}"] = np.ascontiguousarray(mg[:, 16:32])
        for hg in range(4):
            percore.setdefault(hg, {})[f"mixgB_{l}"] = np.ascontiguousarray(mg[:, 4 * hg:4 * hg + 4])
        wo = f(w_out[l])
        shared[f"w_out_{l}"] = np.stack([_blocks(wo[2048:]), _blocks(wo[:2048])], 0)
        shared[f"g2_{l}"] = _fm(f(ffn_norm_g[l]))
        wu = f(w_up[l])
        shared[f"w_up_{l}"] = np.ascontiguousarray(
            wu.reshape(KC, 128, 2, FC, 128).transpose(3, 1, 0, 2, 4).reshape(FC, 128, KC, 256))
        wd = f(w_down[l])
        for s_, (k0, nk) in enumerate(SLABS):
            shared[f"w_dn{s_}_{l}"] = np.ascontiguousarray(
                wd[k0 * 128:(k0 + nk) * 128, :].reshape(nk, 128, 16, 256).transpose(2, 1, 0, 3))
        shared[f"cw_{l}"] = np.ascontiguousarray(f(conv_w[l]).reshape(3, 172, 128).transpose(2, 1, 0))
        shared[f"cb_{l}"] = np.ascontiguousarray(f(conv_b[l]).reshape(172, 128).T)
    in_maps = []
    for c in range(NCORES):
        hg = c % 4
        nsl = np.zeros((128, 12), np.float32)
        for hl in range(4):
            for di, d in enumerate((1, 4, 16)):
                nsl[:, hl * 3 + di] = np.float32(-slopes[4 * hg + hl] * d)
        m = dict(shared)
        m.update(percore[hg])
        m["xT"] = xT[c]
        m["nsl"] = nsl
        in_maps.append(m)
    nc = build_fused(L)
    res = run_bass_kernel_spmd(nc, in_maps, core_ids=list(range(NCORES))).results
    y = np.empty((2, 4096, D), np.float32)
    for c in range(NCORES):
        b, r = divmod(c, 4)
        y[b, r * T:(r + 1) * T, :] = np.asarray(res[c]["out"]).reshape(D, T).T
    return y
```

```python
import numpy as np
import ml_dtypes
from contextlib import ExitStack
import concourse.bass as bass
import concourse.mybir as mybir
from concourse.bass_utils import run_bass_kernel_spmd

F32 = mybir.dt.float32
BF16 = mybir.dt.bfloat16
I32 = mybir.dt.int32
AF = mybir.ActivationFunctionType
ALU = mybir.AluOpType
NPBF = ml_dtypes.bfloat16

NCORES = 8
T = 1024
D = 4096
KC = 32
DFF = 11008
FC = 86
SLABS = [(0, 22), (22, 22), (44, 21), (65, 21)]
EPS = 1e-6
SCALE = 128.0 ** -0.5


_POOL = {}
_UID = [0]


class Sem:
    def __new__(cls, nc, es, name, step=1):
        key = (id(nc), name)
        if key in _POOL:
            return _POOL[key]
        o = object.__new__(cls)
        o.h = nc.alloc_semaphore(name=name)
        o.step = step
        o.n = 0
        _POOL[key] = o
        return o

    def __init__(self, nc, es, name, step=1):
        pass


class Prog:
    ENG = ("pe", "act", "dve", "pool", "sp")

    def __init__(self):
        self.q = {k: [] for k in self.ENG}
        self.waited = {k: {} for k in self.ENG}

    def wait(self, eng, w):
        if w is None:
            return
        s, v = w
        if v <= 0:
            return
        if v > self.waited[eng].get(id(s), 0):
            self.waited[eng][id(s)] = v
            self.q[eng].append(lambda e, s=s, v=v: e.wait_ge(s.h, v))

    def op(self, eng, fn, waits=(), inc=None):
        for w in waits:
            self.wait(eng, w)
        if inc is None:
            self.q[eng].append(fn)
            return None
        inc.n += inc.step
        self.q[eng].append(lambda e, fn=fn, inc=inc: fn(e).then_inc(inc.h, inc.step))
        return (inc, inc.n)

    def run(self, nc, dyn=None):
        q = self.q
        with nc.Block() as block:
            @block.tensor
            def _(e):
                for f in q["pe"]:
                    f(e)

            @block.scalar
            def _(e):
                for f in q["act"]:
                    f(e)

            @block.vector
            def _(e):
                for f in q["dve"]:
                    f(e)

            @block.gpsimd
            def _(e):
                for f in q["pool"]:
                    f(e)

            @block.sync
            def _(e):
                for f in q["sp"]:
                    f(e)


class Ring:
    def __init__(self, bufs, sems=None):
        self.bufs = bufs
        self.free_at = [None] * len(bufs)
        self.sems = sems
        self.i = 0

    def all_done(self):
        return [(sm, sm.n) for sm in self.sems]

    def acquire(self):
        i = self.i
        self.i = (i + 1) % len(self.bufs)
        return i, self.bufs[i], self.free_at[i]

    def release(self, i, w):
        self.free_at[i] = w


def sbt(nc, es, name, shape, dt):
    return es.enter_context(nc.sbuf_tensor(f"s_{name}_{_UID[0]}", shape, dt))


def dsems(nc, es, tag, n):
    return [Sem(nc, es, f"{tag}{i}", 16) for i in range(n)]


def emit_rmsnorm(P, S, x_src, g_sb, xs_ring, sq_ring, ones, rstd, psA, psB, pe_waits, dst_fn,
                 x_ready=None, rstd_ready=None):
    last_pe = None
    for c in range(KC if rstd_ready is None else 0):
        i, xb, fw = xs_ring.acquire()
        ld = P.op("sp", lambda e, xb=xb, c=c: e.dma_start(out=xb[:], in_=x_src(c)),
                  waits=[fw, x_ready], inc=xs_ring.sems[i])
        j, sqb, sfw = sq_ring.acquire()
        a = P.op("act", lambda e, xb=xb, sqb=sqb: e.activation(out=sqb[:], in_=xb[:], func=AF.Square),
                 waits=[ld, sfw], inc=S["act"])
        xs_ring.release(i, a)
        w0 = [a] + (list(pe_waits) if c == 0 else [])
        P.op("pe", lambda e, sqb=sqb, c=c: e.matmul(psA[:], lhsT=ones[:], rhs=sqb[:, 0:512],
                                                    start=(c == 0), stop=(c == KC - 1)), waits=w0)
        last_pe = P.op("pe", lambda e, sqb=sqb, c=c: e.matmul(psB[:], lhsT=ones[:], rhs=sqb[:, 512:1024],
                                                              start=(c == 0), stop=(c == KC - 1)),
                       inc=S["pe"])
        sq_ring.release(j, last_pe)
    ps_rel = None
    if rstd_ready is None:
        P.op("act", lambda e: e.activation(out=rstd[:, 0:512], in_=psA[:], func=AF.Sqrt, bias=EPS, scale=1.0 / D),
             waits=[last_pe])
        a = P.op("act", lambda e: e.activation(out=rstd[:, 512:1024], in_=psB[:], func=AF.Sqrt, bias=EPS,
                                               scale=1.0 / D), inc=S["act"])
        ps_rel = a
        rstd_ready = P.op("dve", lambda e: e.reciprocal(out=rstd[:], in_=rstd[:]), waits=[a], inc=S["dve"])
    last = None
    for c in range(KC):
        i, xb, fw = xs_ring.acquire()
        ld = P.op("sp", lambda e, xb=xb, c=c: e.dma_start(out=xb[:], in_=x_src(c)), waits=[fw, x_ready],
                  inc=xs_ring.sems[i])
        last = P.op("dve", lambda e, xb=xb, c=c: e.scalar_tensor_tensor(
            out=dst_fn(c), in0=xb[:], scalar=g_sb[:, c:c + 1], in1=rstd[:], op0=ALU.mult, op1=ALU.mult),
            waits=[ld, rstd_ready], inc=S["dve"])
        xs_ring.release(i, last)
    return last, ps_rel


class WStream:
    def __init__(self, P, S, bufs, sems):
        self.P, self.S = P, S
        self.ring = Ring(bufs, sems)

    def load(self, src_ap, nkc):
        i, wb, fw = self.ring.acquire()
        ld = self.P.op("pool", lambda e, wb=wb: e.dma_start(out=wb[:, 0:nkc, :], in_=src_ap),
                       waits=[fw], inc=self.ring.sems[i])
        return i, wb, ld

    def release(self, i, w):
        self.ring.release(i, w)


def mm_group(P, S, ps, n, lhsT_fn, rhs_fn, nk, waits, out_ap=None):
    o = out_ap if out_ap is not None else ps[:, 0:n]
    r = None
    for k in range(nk):
        w = waits if k == 0 else ()
        fn = lambda e, k=k: e.matmul(o, lhsT=lhsT_fn(k), rhs=rhs_fn(k), start=(k == 0), stop=(k == nk - 1))
        if k == nk - 1:
            r = P.op("pe", fn, waits=w, inc=S["pe"])
        else:
            P.op("pe", fn, waits=w)
    return r


def mk_sems(nc, es, tag):
    S = {}
    for nm, st in (("ld", 16), ("wld", 16), ("st", 16), ("pe", 1), ("act", 1), ("dve", 1), ("pool", 1),
                   ("cld", 16)):
        S[nm] = Sem(nc, es, f"{tag}_{nm}", st)
    return S


def phase_A(nc, io, dyn=None):
    P = Prog()
    _UID[0] += 1
    with ExitStack() as es:
        S = mk_sems(nc, es, "A")
        hT = sbt(nc, es, "hT", [128, KC, T], BF16)
        uT = sbt(nc, es, "uT", [128, 16, T], BF16)
        vn = sbt(nc, es, "vn", [128, 8, 2048], BF16)
        wbufs = [sbt(nc, es, f"w{i}", [128, KC, 256], BF16) for i in range(2)]
        xs = Ring([sbt(nc, es, f"xs{i}", [128, T], F32) for i in range(2)], dsems(nc, es, "A_xl", 2))
        sq = Ring([sbt(nc, es, f"sq{i}", [128, T], BF16) for i in range(2)])
        rstd = sbt(nc, es, "rstd", [128, T], F32)
        ones = sbt(nc, es, "ones", [128, 128], BF16)
        gsb = sbt(nc, es, "gsb", [128, KC], F32)
        normg = sbt(nc, es, "normg", [128, 2048], BF16)
        WT = sbt(nc, es, "WT", [128, 16, 128], BF16)
        brow = sbt(nc, es, "brow", [1, 2048], BF16)
        onesr = sbt(nc, es, "onesr", [1, 128], BF16)
        qst = Ring([sbt(nc, es, f"qst{i}", [128, T], BF16) for i in range(2)], dsems(nc, es, "A_qs", 2))
        vst = Ring([sbt(nc, es, f"vst{i}", [128, 8, 256], BF16) for i in range(2)], dsems(nc, es, "A_vs", 2))
        ss = sbt(nc, es, "ss", [128, 128], F32)
        rs = sbt(nc, es, "rs", [128, 128], F32)
        junk = sbt(nc, es, "junk", [128, 128], F32)
        ps = [es.enter_context(nc.psum_tensor(f"psA{i}_{_UID[0]}", [128, 512], F32)) for i in range(8)]
        psr = Ring(ps)

        c1 = P.op("sp", lambda e: e.dma_start(out=gsb[:], in_=io["g1"][:]), inc=S["cld"])
        if "zero_dst" in io:
            zt = sbt(nc, es, "zt", [128, 64], F32)
            zr = P.op("pool", lambda e: e.memset(zt[:], 0.0), inc=S["pool"])
            c1 = P.op("sp", lambda e: e.dma_start(out=io["zero_dst"], in_=zt[:]), waits=[zr], inc=S["cld"])
        P.op("pool", lambda e: e.dma_start(out=normg[:], in_=io["normg"][:]), inc=S["wld"])
        P.op("pool", lambda e: e.dma_start(out=WT[:], in_=io["wsT"][:]), inc=S["wld"])
        c2 = P.op("pool", lambda e: e.dma_start(out=brow[:], in_=io["brow"][:]), inc=S["wld"])
        P.op("pool", lambda e: e.memset(ones[:], 1.0))
        P.op("pool", lambda e: e.memset(onesr[:], 1.0))
        P.op("pool", lambda e: e.memset(ss[:], 0.0))
        cpool = P.op("pool", lambda e: e.affine_select(out=WT[:], in_=WT[:], pattern=[[0, 16], [1, 128]],
                                                       compare_op=ALU.is_ge, fill=0.0, base=0,
                                                       channel_multiplier=-1), waits=[c2], inc=S["pool"])

        rr_ = None
        if "rstd_in" in io:
            rr_ = P.op("sp", lambda e: e.dma_start(out=rstd[:], in_=io["rstd_in"][:]), inc=S["ld"])
        h_done, ps_rel = emit_rmsnorm(P, S, lambda c: io["xT"][c], gsb, xs, sq, ones, rstd, ps[0], ps[1],
                                      [cpool], lambda c: hT[:, c, :], rstd_ready=rr_)
        psr.free_at[0] = ps_rel
        psr.free_at[1] = ps_rel

        mixgA = sbt(nc, es, "mixgA", [128, 16], F32)
        cmg = P.op("sp", lambda e: e.dma_start(out=mixgA[:], in_=io["mixgA"][:]), inc=S["cld"])
        sgu_state = {}

        def emit_sgu(g):
            dv = None
            for tq in range(2):
                bi, pb, pfw = psr.acquire()
                full = None
                for i in range(4):
                    tc = tq * 4 + i
                    P.op("pe", lambda e, pb=pb, i=i, tc=tc, g=g: e.matmul(
                        pb[:, i * 128:(i + 1) * 128], lhsT=vn[:, tc, g * 128:(g + 1) * 128], rhs=WT[:, g, :],
                        start=True, stop=False), waits=[pfw, sgu_state["vn_done"], cpool] if i == 0 else ())
                    full = P.op("pe", lambda e, pb=pb, i=i, g=g: e.matmul(
                        pb[:, i * 128:(i + 1) * 128], lhsT=onesr[0:1, :], rhs=brow[0:1, g * 128:(g + 1) * 128],
                        start=False, stop=True), inc=S["pe"])
                dv = P.op("dve", lambda e, pb=pb, g=g, tq=tq: e.tensor_tensor(
                    out=uT[:, g, tq * 512:(tq + 1) * 512], in0=pb[:], in1=uT[:, g, tq * 512:(tq + 1) * 512],
                    op=ALU.mult), waits=[full, u_ready[g]], inc=S["dve"])
                psr.release(bi, dv)
            j, sqb, sfw = sq.acquire()
            a = P.op("act", lambda e, sqb=sqb, g=g: e.activation(out=sqb[:], in_=uT[:, g, :], func=AF.Square),
                     waits=[dv, sfw], inc=S["act"])
            sgu_state.setdefault("pend", []).append((g, j, sqb, a))

        def emit_sgu2():
            g, j, sqb, a = sgu_state["pend"].pop(0)
            ri, rb, rfw = xs.acquire()
            fulls, banks = [], []
            for th in range(2):
                bi, pb, pfw = psr.acquire()
                banks.append((bi, pb))
                fulls.append(P.op("pe", lambda e, pb=pb, sqb=sqb, th=th: e.matmul(
                    pb[:], lhsT=ones[:], rhs=sqb[:, th * 512:(th + 1) * 512], start=True, stop=True),
                    waits=[a, pfw], inc=S["pe"]))
            sq.release(j, fulls[1])
            a2 = None
            for th in range(2):
                a2 = P.op("act", lambda e, rb=rb, pb=banks[th][1], th=th: e.activation(
                    out=rb[:, th * 512:(th + 1) * 512], in_=pb[:], func=AF.Sqrt, bias=EPS, scale=1.0 / 128),
                    waits=[fulls[th], rfw], inc=S["act"])
                psr.release(banks[th][0], a2)
            P.op("dve", lambda e, rb=rb: e.reciprocal(out=rb[:], in_=rb[:]), waits=[a2, cmg])
            gd = P.op("dve", lambda e, rb=rb, g=g: e.scalar_tensor_tensor(
                out=uT[:, g, :], in0=uT[:, g, :], scalar=mixgA[:, g:g + 1], in1=rb[:], op0=ALU.mult, op1=ALU.mult),
                inc=S["dve"])
            xs.release(ri, gd)
            P.op("sp", lambda e, g=g: e.dma_start(out=io["gT"][g], in_=uT[:, g, :]), waits=[gd], inc=S["st"])

        ws = WStream(P, S, wbufs, dsems(nc, es, "A_wl", 2))
        NB = 40
        pending = []
        for b in range(min(2, NB)):
            pending.append(ws.load(io["w_in"][b], KC))
        u_ready = [None] * 16
        vn_done = None
        for b in range(NB):
            wi, wb, wld = pending.pop(0)
            blk_done = None
            if b < 16 or b >= 32:
                for j in range(2):
                    ch = 2 * b + j if b < 16 else 2 * (b - 32) + j
                    if b < 16:
                        si, stg, sfw = qst.acquire()
                    a = None
                    for th in range(2):
                        bi, pb, pfw = psr.acquire()
                        full = mm_group(P, S, pb, 512,
                                        lambda k, wb=wb, j=j: wb[:, k, j * 128:(j + 1) * 128],
                                        lambda k, th=th: hT[:, k, th * 512:(th + 1) * 512],
                                        KC, [wld, pfw, h_done])
                        if b < 16:
                            a = P.op("act", lambda e, stg=stg, pb=pb, th=th: e.activation(
                                out=stg[:, th * 512:(th + 1) * 512], in_=pb[:], func=AF.Copy),
                                waits=[full, sfw], inc=S["act"])
                        else:
                            a = P.op("act", lambda e, pb=pb, th=th, ch=ch: e.activation(
                                out=uT[:, ch, th * 512:(th + 1) * 512], in_=pb[:], func=AF.Gelu),
                                waits=[full], inc=S["act"])
                            u_ready[ch] = a
                        psr.release(bi, a)
                        blk_done = a
                    if b < 16:
                        hd = ch % 16
                        cho = (hd // 4) * 8 + (4 if ch >= 16 else 0) + hd % 4
                        st = P.op("sp", lambda e, stg=stg, cho=cho: e.dma_start(out=io["qk"][cho], in_=stg[:]),
                                  waits=[a, c1], inc=qst.sems[si])
                        qst.release(si, st)
            else:
                isv = b < 24
                cb = (b - 16) if isv else (b - 24)
                if isv:
                    si, stg, sfw = vst.acquire()
                a = None
                for tc in range(8):
                    bi, pb, pfw = psr.acquire()
                    full = mm_group(P, S, pb, 256,
                                    lambda k, tc=tc: hT[:, k, tc * 128:(tc + 1) * 128],
                                    lambda k, wb=wb: wb[:, k, :],
                                    KC, [wld, pfw, h_done])
                    if isv:
                        a = P.op("act", lambda e, stg=stg, pb=pb, tc=tc: e.activation(
                            out=stg[:, tc, :], in_=pb[:, 0:256], func=AF.Copy), waits=[full, sfw], inc=S["act"])
                    else:
                        a = P.op("act", lambda e, pb=pb, tc=tc, cb=cb: e.activation(
                            out=vn[:, tc, cb * 256:(cb + 1) * 256], in_=pb[:, 0:256], func=AF.Gelu),
                            waits=[full], inc=S["act"])
                        for gg in range(2):
                            g = cb * 2 + gg
                            P.op("act", lambda e, tc=tc, g=g: e.activation(
                                out=junk[:], in_=vn[:, tc, g * 128:(g + 1) * 128], func=AF.Square,
                                accum_out=ss[:, tc * 16 + g:tc * 16 + g + 1]))
                    psr.release(bi, a)
                    blk_done = a
                if isv:
                    vdst = io["v"](cb)
                    st = P.op("sp", lambda e, stg=stg, vdst=vdst: e.dma_start(out=vdst, in_=stg[:]),
                              waits=[a], inc=vst.sems[si])
                    vst.release(si, st)
            ws.release(wi, blk_done)
            if b + 2 < NB:
                pending.append(ws.load(io["w_in"][b + 2], KC))
            if b == 15 and "ag_qk" in io:
                prog_allgather(P, io["cc"], io["ag_qk"], qst.all_done())
            if b == 23 and "ag_v" in io:
                prog_allgather(P, io["cc"], io["ag_v"], vst.all_done())
            if b == 31:
                a = P.op("act", lambda e: e.activation(out=rs[:], in_=ss[:], func=AF.Sqrt, bias=EPS,
                                                       scale=1.0 / 128), inc=S["act"])
                P.op("dve", lambda e: e.reciprocal(out=rs[:], in_=rs[:]), waits=[a, c2])
                for tc in range(8):
                    for g in range(16):
                        col = tc * 16 + g
                        vn_done = P.op("dve", lambda e, tc=tc, g=g, col=col: e.scalar_tensor_tensor(
                            out=vn[:, tc, g * 128:(g + 1) * 128], in0=vn[:, tc, g * 128:(g + 1) * 128],
                            scalar=rs[:, col:col + 1], in1=normg[:, g * 128:(g + 1) * 128],
                            op0=ALU.mult, op1=ALU.mult), inc=S["dve"])
                sgu_state["vn_done"] = vn_done
            if b >= 32:
                while sgu_state.get("pend"):
                    emit_sgu2()
                emit_sgu(2 * (b - 32))
                emit_sgu(2 * (b - 32) + 1)

        while sgu_state.get("pend"):
            emit_sgu2()
        P.wait("sp", (S["st"], S["st"].n))
        for w in qst.all_done() + vst.all_done():
            P.wait("sp", w)
        P.run(nc, dyn)


def attn_groups(d):
    out = []
    nb = 32 // d
    if d == 16:
        for r0 in range(0, 16, 2):
            out.append(([(r0, 0), (r0, 1), (r0 + 1, 0), (r0 + 1, 1)], ("pair", r0)))
    else:
        for r in range(d):
            for n0 in range(0, nb, 4):
                out.append(([(r, n0 + i) for i in range(4)], ("run", r, n0)))
    return out


def phase_B(nc, io, dyn=None):
    P = Prog()
    _UID[0] += 1
    NS = 4096
    with ExitStack() as es:
        S = mk_sems(nc, es, "B")
        S["sfull"] = Sem(nc, es, "B_sfull", 1)
        S["ofull"] = Sem(nc, es, "B_ofull", 1)
        S["fin"] = Sem(nc, es, "B_fin", 1)
        nset = 2
        qT = [sbt(nc, es, f"qT{i}", [128, NS], BF16) for i in range(nset)]
        kT = [sbt(nc, es, f"kT{i}", [128, NS], BF16) for i in range(nset)]
        vd = [{d: sbt(nc, es, f"v{d}_{i}", [128, 32, 128], BF16) for d in (1, 4, 16)} for i in range(nset)]
        inring = Ring(list(range(nset)), dsems(nc, es, "B_hl", nset))
        acc_o = sbt(nc, es, "acc_o", [128, NS], F32)
        acc_z = sbt(nc, es, "acc_z", [128, NS], F32)
        aT = sbt(nc, es, "aT", [128, NS], BF16)
        EB = sbt(nc, es, "EB", [128, 12, 4, 256], BF16)
        sqh = sbt(nc, es, "sqh", [128, NS], BF16)
        mixgB = sbt(nc, es, "mixgB", [128, 4], F32)
        pT = Ring([sbt(nc, es, f"pT{i}", [128, 1024], BF16) for i in range(2)])
        pT2 = Ring([sbt(nc, es, f"pTm{i}", [128, 1024], BF16) for i in range(2)])
        ones = sbt(nc, es, "onesB", [128, 128], BF16)
        nsl = sbt(nc, es, "nsl", [128, 12], F32)
        it = sbt(nc, es, "it", [128, 256], I32)
        dist = sbt(nc, es, "dist", [128, 256], F32)
        etmp = Ring([sbt(nc, es, f"etmp{i}", [128, 256], F32) for i in range(2)])
        ps = [es.enter_context(nc.psum_tensor(f"psB{i}_{_UID[0]}", [128, 512], F32)) for i in range(8)]
        sp_ring = Ring([0, 1])
        o_ring = Ring([0, 1])

        P.op("sp", lambda e: e.dma_start(out=mixgB[:], in_=io["mixgB"][:]), inc=S["cld"])
        c1 = P.op("sp", lambda e: e.dma_start(out=nsl[:], in_=io["nsl"][:]), inc=S["cld"])
        P.op("pool", lambda e: e.memset(ones[:], 1.0))
        P.op("pool", lambda e: e.iota(it[:], [[1, 256]], base=0, channel_multiplier=-1))
        itd = P.op("pool", lambda e: e.memset(ones[:, 0:1], 1.0), inc=S["pool"])
        P.op("dve", lambda e: e.tensor_copy(out=dist[:], in_=it[:]), waits=[itd])
        P.op("dve", lambda e: e.tensor_scalar(out=dist[:, 0:128], in0=dist[:, 0:128], scalar1=128.0,
                                              scalar2=None, op0=ALU.add))
        dready = P.op("dve", lambda e: e.tensor_scalar(out=dist[:, 128:256], in0=dist[:, 128:256],
                                                       scalar1=-128.0, scalar2=None, op0=ALU.add), inc=S["dve"])
        eb_done = None
        for idx in range(12):
            ei, eb, efw = etmp.acquire()
            a = P.op("act", lambda e, eb=eb, idx=idx: e.activation(out=eb[:], in_=dist[:], func=AF.Exp,
                                                                   scale=nsl[:, idx:idx + 1]),
                     waits=[dready, c1, efw], inc=S["act"])
            P.op("pool", lambda e, eb=eb, idx=idx: e.affine_select(
                out=EB[:, idx, 0, 0:128], in_=eb[:, 0:128], pattern=[[-1, 128]], compare_op=ALU.is_ge, fill=0.0,
                base=0, channel_multiplier=1), waits=[a])
            eb_done = P.op("pool", lambda e, eb=eb, idx=idx: e.affine_select(
                out=EB[:, idx, 0, 128:256], in_=eb[:, 128:256], pattern=[[1, 128]], compare_op=ALU.is_ge, fill=0.0,
                base=0, channel_multiplier=-1), inc=S["pool"])
            etmp.release(ei, eb_done)
            for rep in range(1, 4):
                eb_done = P.op("pool", lambda e, idx=idx, rep=rep: e.tensor_copy(
                    out=EB[:, idx, rep, :], in_=EB[:, idx, 0, :]), inc=S["pool"])

        pre_done = None
        if "cc" in io:
            P.wait("sp", (io["cc"], io["cc"].n))
        for (dst, srcf) in io.get("pre", []):
            pre_done = P.op("sp", lambda e, dst=dst, srcf=srcf: e.dma_start(out=dst, in_=srcf()), inc=S["cld"])

        def load_head(h):
            i, si, fw = inring.acquire()
            lsem = inring.sems[i]
            P.op("sp", lambda e: e.dma_start(out=qT[si][:].rearrange("p (r t) -> p r t", r=4), in_=io["q"](h)),
                 waits=[fw, pre_done], inc=lsem)
            P.op("sp", lambda e: e.dma_start(out=kT[si][:].rearrange("p (r t) -> p r t", r=4), in_=io["k"](h)),
                 inc=lsem)
            ld = None
            for d in (1, 4, 16):
                nb = 32 // d
                for r in range(d):
                    ld = P.op("sp", lambda e, d=d, r=r, nb=nb: e.dma_start(
                        out=vd[si][d][:, r * nb:(r + 1) * nb, :], in_=io["v"](h, d, r)), inc=lsem)
            return i, si, ld

        nxt = load_head(0)
        a_st = None
        fin_prev = None
        for h in range(4):
            ri, si, ld = nxt
            if h + 1 < 4:
                nxt = load_head(h + 1)
            q_, k_ = qT[si], kT[si]
            last_pe_head = None
            for di, d in enumerate((1, 4, 16)):
                nb = 32 // d
                groups = attn_groups(d)
                qv = q_[:].rearrange("p (m r) -> p r m", r=d)
                kv = k_[:].rearrange("p (m r) -> p r m", r=d)
                ebi = h * 3 + di
                state = {}

                def p1(gi):
                    blocks, sel = groups[gi]
                    _, sset, sfw = sp_ring.acquire()
                    full = None
                    first = True
                    for b, (r, n) in enumerate(blocks):
                        bank = ps[4 * sset + b // 2]
                        off = (b % 2) * 256
                        qa = qv[:, r, n * 128:(n + 1) * 128]
                        if n > 0:
                            ka = kv[:, r, (n - 1) * 128:n * 128]
                            P.op("pe", lambda e, bank=bank, off=off, ka=ka, qa=qa: e.matmul(
                                bank[:, off:off + 128], lhsT=ka, rhs=qa, start=True, stop=True),
                                waits=[sfw, ld] if first else ())
                            first = False
                        ka = kv[:, r, n * 128:(n + 1) * 128]
                        fn = lambda e, bank=bank, off=off, ka=ka, qa=qa: e.matmul(
                            bank[:, off + 128:off + 256], lhsT=ka, rhs=qa, start=True, stop=True)
                        if b == 3:
                            full = P.op("pe", fn, waits=[sfw, ld] if first else (), inc=S["sfull"])
                        else:
                            P.op("pe", fn, waits=[sfw, ld] if first else ())
                        first = False
                    pi, pbuf, pfw = pT.acquire()
                    P.op("act", lambda e, pbuf=pbuf, sset=sset: e.activation(
                        out=pbuf[:, 0:512], in_=ps[4 * sset][:], func=AF.Exp, scale=SCALE), waits=[full, pfw])
                    a = P.op("act", lambda e, pbuf=pbuf, sset=sset: e.activation(
                        out=pbuf[:, 512:1024], in_=ps[4 * sset + 1][:], func=AF.Exp, scale=SCALE), inc=S["act"])
                    sp_ring.release(sset, a)
                    mi, mbuf, mfw = pT2.acquire()
                    meng = "dve"
                    pl = P.op(meng, lambda e, mbuf=mbuf, pbuf=pbuf, ebi=ebi: e.tensor_tensor(
                        out=mbuf[:].rearrange("p (b c) -> p b c", b=4),
                        in0=pbuf[:].rearrange("p (b c) -> p b c", b=4), in1=EB[:, ebi, :, :], op=ALU.mult),
                        waits=[a, mfw, eb_done], inc=S[meng])
                    pT.release(pi, pl)
                    state[gi] = (mi, mbuf, pl)

                def p2(gi):
                    blocks, sel = groups[gi]
                    mi, mbuf, pl = state.pop(gi)
                    _, oset, ofw = o_ring.acquire()
                    po, pz = ps[4 * oset + 2], ps[4 * oset + 3]
                    vt = vd[si][d]
                    full = None
                    first = True
                    for b, (r, n) in enumerate(blocks):
                        j = r * nb + n
                        for tgt, lfn in ((po, lambda jj: vt[:, jj, :]), (pz, lambda jj: ones[:])):
                            oa = tgt[:, b * 128:(b + 1) * 128]
                            if n > 0:
                                P.op("pe", lambda e, oa=oa, l=lfn(j - 1), b=b: e.matmul(
                                    oa, lhsT=l, rhs=mbuf[:, b * 256:b * 256 + 128], start=True, stop=False),
                                    waits=[pl, ofw] if first else ())
                                first = False
                            fn = lambda e, oa=oa, l=lfn(j), b=b, n=n: e.matmul(
                                oa, lhsT=l, rhs=mbuf[:, b * 256 + 128:b * 256 + 256], start=(n == 0), stop=True)
                            if b == 3 and tgt is pz:
                                full = P.op("pe", fn, waits=[pl, ofw] if first else (), inc=S["ofull"])
                            else:
                                P.op("pe", fn, waits=[pl, ofw] if first else ())
                            first = False
                    pT2.release(mi, full)
                    if sel[0] == "pair":
                        r0 = sel[1]
                        ao = acc_o[:].rearrange("p (m r) -> p r m", r=16)[:, r0:r0 + 2, :]
                        az = acc_z[:].rearrange("p (m r) -> p r m", r=16)[:, r0:r0 + 2, :]
                        pov = po[:].rearrange("p (a m) -> p a m", a=2)
                        pzv = pz[:].rearrange("p (a m) -> p a m", a=2)
                    else:
                        _, r, n0 = sel
                        ao = acc_o[:].rearrange("p (m r) -> p r m", r=d)[:, r, n0 * 128:n0 * 128 + 512]
                        az = acc_z[:].rearrange("p (m r) -> p r m", r=d)[:, r, n0 * 128:n0 * 128 + 512]
                        pov, pzv = po[:], pz[:]
                    if d == 1:
                        P.op("dve", lambda e, ao=ao, pov=pov: e.tensor_copy(out=ao, in_=pov),
                             waits=[full, fin_prev])
                        dv = P.op("dve", lambda e, az=az, pzv=pzv: e.tensor_copy(out=az, in_=pzv), inc=S["dve"])
                    else:
                        P.op("dve", lambda e, ao=ao, pov=pov: e.tensor_tensor(out=ao, in0=ao, in1=pov, op=ALU.add),
                             waits=[full])
                        dv = P.op("dve", lambda e, az=az, pzv=pzv: e.tensor_tensor(out=az, in0=az, in1=pzv,
                                                                                   op=ALU.add), inc=S["dve"])
                    o_ring.release(oset, dv)
                    return full

                ng = len(groups)
                p1(0)
                for gi in range(ng):
                    if gi + 1 < ng:
                        p1(gi + 1)
                    last_pe_head = p2(gi)
                    if di == 0 and gi == 5 and h > 0 and "ag_a" in io:
                        prog_allgather(P, io["cc"], [io["ag_a"][h - 1]], [a_st])
            inring.release(ri, last_pe_head)
            P.op("dve", lambda e: e.reciprocal(out=acc_z[:], in_=acc_z[:]))
            fin = P.op("dve", lambda e: e.tensor_tensor(out=aT[:], in0=acc_o[:], in1=acc_z[:], op=ALU.mult),
                       waits=[a_st], inc=S["fin"])
            asq = P.op("act", lambda e: e.activation(out=sqh[:], in_=aT[:], func=AF.Square), waits=[fin],
                       inc=S["act"])
            a2 = None
            for qq in range(4):
                _, oset, ofw = o_ring.acquire()
                po, pz = ps[4 * oset + 2], ps[4 * oset + 3]
                P.op("pe", lambda e, po=po, qq=qq: e.matmul(po[:], lhsT=ones[:], rhs=sqh[:, qq * 1024:qq * 1024 + 512],
                                                            start=True, stop=True), waits=[asq, ofw])
                full = P.op("pe", lambda e, pz=pz, qq=qq: e.matmul(
                    pz[:], lhsT=ones[:], rhs=sqh[:, qq * 1024 + 512:qq * 1024 + 1024], start=True, stop=True),
                    inc=S["ofull"])
                P.op("act", lambda e, po=po, qq=qq: e.activation(
                    out=acc_z[:, qq * 1024:qq * 1024 + 512], in_=po[:], func=AF.Sqrt, bias=EPS, scale=1.0 / 128),
                    waits=[full])
                a2 = P.op("act", lambda e, pz=pz, qq=qq: e.activation(
                    out=acc_z[:, qq * 1024 + 512:qq * 1024 + 1024], in_=pz[:], func=AF.Sqrt, bias=EPS,
                    scale=1.0 / 128), inc=S["act"])
                o_ring.release(oset, a2)
            P.op("dve", lambda e: e.reciprocal(out=acc_z[:], in_=acc_z[:]), waits=[a2, c1])
            fin2 = P.op("dve", lambda e, h=h: e.scalar_tensor_tensor(
                out=aT[:], in0=aT[:], scalar=mixgB[:, h:h + 1], in1=acc_z[:], op0=ALU.mult, op1=ALU.mult),
                inc=S["fin"])
            a_st = P.op("sp", lambda e, h=h: e.dma_start(out=io["a"](h), in_=aT[:].rearrange(
                "p (tb t) -> p tb t", tb=4)), waits=[fin2], inc=S["st"])
            fin_prev = None
        if "ag_a" in io:
            prog_allgather(P, io["cc"], [io["ag_a"][3]], [a_st])
        P.wait("sp", (S["st"], S["st"].n))
        P.run(nc, dyn)


class XPipe:
    def __init__(self, P, xs, x_src, order, st_hist):
        self.P, self.xs, self.x_src, self.order, self.st_hist = P, xs, x_src, order, st_hist
        self.nxt = 0
        self.loaded = {}

    def issue(self):
        if self.nxt >= len(self.order):
            return
        ch = self.order[self.nxt]
        self.nxt += 1
        i, xb, fw = self.xs.acquire()
        src = self.x_src(ch)
        xl = self.P.op("sp", lambda e, xb=xb, src=src: e.dma_start(out=xb[:], in_=src),
                       waits=[fw, self.st_hist.get(ch)], inc=self.xs.sems[i])
        self.loaded[ch] = (i, xb, xl)


def ssq_flush(P, S, ssq, keep):
    while len(ssq["pend"]) > keep:
        ch, j, sqb, a = ssq["pend"].pop(0)
        w0 = [a] + (list(ssq["bank_free"]) if ch == 0 else [])
        P.op("pe", lambda e, sqb=sqb, ch=ch: e.matmul(ssq["psA"][:], lhsT=ssq["ones"][:], rhs=sqb[:, 0:512],
                                                      start=(ch == 0), stop=(ch == KC - 1)), waits=w0)
        pe = P.op("pe", lambda e, sqb=sqb, ch=ch: e.matmul(ssq["psB"][:], lhsT=ssq["ones"][:], rhs=sqb[:, 512:1024],
                                                           start=(ch == 0), stop=(ch == KC - 1)), inc=S["pe"])
        ssq["sq"].release(j, pe)
        ssq["last_pe"] = pe


def ssq_finish(P, S, ssq, rstd):
    ssq_flush(P, S, ssq, 0)
    P.op("act", lambda e: e.activation(out=rstd[:, 0:512], in_=ssq["psA"][:], func=AF.Sqrt, bias=EPS,
                                       scale=1.0 / D), waits=[ssq["last_pe"]])
    a = P.op("act", lambda e: e.activation(out=rstd[:, 512:1024], in_=ssq["psB"][:], func=AF.Sqrt, bias=EPS,
                                           scale=1.0 / D), inc=S["act"])
    return P.op("dve", lambda e: e.reciprocal(out=rstd[:], in_=rstd[:]), waits=[a], inc=S["dve"])


def emit_resid_block(P, S, psr, xp, xo, wb, wld, b, nk, rhs_fn, x_dst, ready, st_hist, halo_dst=None, ssq=None):
    blk_done = None
    for j in range(2):
        ch = 2 * b + j
        xi, xb, xl = xp.loaded.pop(ch)
        oi, ob, ofw = xo.acquire()
        if ssq is not None:
            P.wait("dve", ssq["ob_read"].get(oi))
        dv = None
        for th in range(2):
            bi, pb, pfw = psr.acquire()
            full = mm_group(P, S, pb, 512, lambda k, wb=wb, j=j: wb[:, k, j * 128:(j + 1) * 128],
                            lambda k, th=th: rhs_fn(k, th), nk, [wld, pfw] + list(ready))
            dv = P.op("dve", lambda e, ob=ob, pb=pb, xb=xb, th=th: e.tensor_tensor(
                out=ob[:, th * 512:(th + 1) * 512], in0=pb[:], in1=xb[:, th * 512:(th + 1) * 512],
                op=ALU.add), waits=[full, xl, ofw], inc=S["dve"])
            psr.release(bi, dv)
            blk_done = full
        xp.xs.release(xi, dv)
        dst = x_dst(ch)
        if halo_dst is not None:
            hd = halo_dst(ch)
            P.op("sp", lambda e, ob=ob, hd=hd: e.dma_start(out=hd, in_=ob[:, T - 2:T]), waits=[dv],
                 inc=xo.sems[oi])
        st = P.op("sp", lambda e, ob=ob, dst=dst: e.dma_start(out=dst, in_=ob[:]), waits=[dv],
                  inc=xo.sems[oi])
        st_hist[ch] = st
        xo.release(oi, st)
        xp.issue()
        if ssq is not None:
            jj, sqb, sfw = ssq["sq"].acquire()
            a = P.op("act", lambda e, sqb=sqb, ob=ob: e.activation(out=sqb[:], in_=ob[:], func=AF.Square),
                     waits=[dv, sfw], inc=S["act"])
            ssq["ob_read"][oi] = a
            ssq_flush(P, S, ssq, 0)
            ssq["pend"].append((ch, jj, sqb, a))
    return blk_done


def phase_C1(nc, io, dyn=None):
    P = Prog()
    _UID[0] += 1
    with ExitStack() as es:
        S = mk_sems(nc, es, "C1")
        S["ld2"] = Sem(nc, es, "C1_ld2", 16)
        mT = sbt(nc, es, "mT", [128, KC, T], BF16)
        wbufs = [sbt(nc, es, f"wc{i}", [128, 16, 256], BF16) for i in range(4)]
        xs = Ring([sbt(nc, es, f"xc{i}", [128, T], F32) for i in range(3)], dsems(nc, es, "C1_xl", 3))
        xo = Ring([sbt(nc, es, f"xo{i}", [128, T], F32) for i in range(2)], dsems(nc, es, "C1_xs", 2))
        ps = [es.enter_context(nc.psum_tensor(f"psC{i}_{_UID[0]}", [128, 512], F32)) for i in range(8)]
        psr = Ring(ps[0:6])
        ones = sbt(nc, es, "onesC", [128, 128], BF16)
        rstd = sbt(nc, es, "rstdC", [128, T], F32)
        sq = Ring([sbt(nc, es, f"sqc{i}", [128, T], BF16) for i in range(2)])
        ones_ok = P.op("pool", lambda e: e.memset(ones[:], 1.0), inc=S["pool"])
        ssq = {"sq": sq, "psA": ps[6], "psB": ps[7], "ones": ones, "pend": [], "ob_read": {},
               "bank_free": [ones_ok]}

        ld_g = None
        for c in range(16):
            ld_g = P.op("sp", lambda e, c=c: e.dma_start(out=mT[:, 16 + c, :], in_=io["gT"][c]), inc=S["ld"])
        st_hist = {}
        ws = WStream(P, S, wbufs, dsems(nc, es, "C1_wl", 4))
        sched = [(0, b) for b in range(16)] + [(1, b) for b in range(16)]
        pending = [ws.load(io["w_out"][hf][b], 16) for (hf, b) in sched[:4]]
        nxt = 4
        xp = XPipe(P, xs, lambda ch: io["xT"][ch], list(range(KC)), st_hist)
        for _ in range(3):
            xp.issue()
        ld_a = None
        for (hf, b) in sched:
            if hf == 1 and b == 0:
                if "cc" in io:
                    P.wait("sp", (io["cc"], io["cc"].n))
                pre_done = None
                for (dst, srcf) in io.get("pre", []):
                    pre_done = P.op("sp", lambda e, dst=dst, srcf=srcf: e.dma_start(out=dst, in_=srcf()),
                                    inc=S["cld"])
                P.wait("sp", pre_done)
                for c in range(16):
                    ld_a = P.op("sp", lambda e, c=c: e.dma_start(out=mT[:, c, :], in_=io["aT"](c)), inc=S["ld2"])
                xp = XPipe(P, xs, lambda ch: io["xo"][ch], list(range(KC)), st_hist)
                for _ in range(3):
                    xp.issue()
            wi, wb, wld = pending.pop(0)
            if hf == 0:
                blk_done = emit_resid_block(P, S, psr, xp, xo, wb, wld, b, 16,
                                            lambda k, th: mT[:, 16 + k, th * 512:(th + 1) * 512],
                                            lambda ch: io["xo"][ch], [ld_g], st_hist)
            else:
                blk_done = emit_resid_block(P, S, psr, xp, xo, wb, wld, b, 16,
                                            lambda k, th: mT[:, k, th * 512:(th + 1) * 512],
                                            lambda ch: io["xo"][ch], [ld_a], st_hist,
                                            halo_dst=io.get("halo_dst"), ssq=ssq if "rstd_out" in io else None)
            ws.release(wi, blk_done)
            if nxt < len(sched):
                pending.append(ws.load(io["w_out"][sched[nxt][0]][sched[nxt][1]], 16))
                nxt += 1
        if "rstd_out" in io:
            rd = ssq_finish(P, S, ssq, rstd)
            P.op("sp", lambda e: e.dma_start(out=io["rstd_out"][:], in_=rstd[:]), waits=[rd], inc=S["st"])
            P.wait("sp", (S["st"], S["st"].n))
        for w in xo.all_done():
            P.wait("sp", w)
        if "ag_xh" in io:
            prog_allgather(P, io["cc"], io["ag_xh"], xo.all_done())
        P.run(nc, dyn)


def phase_C2(nc, io, final, dyn=None):
    P = Prog()
    _UID[0] += 1
    with ExitStack() as es:
        S = mk_sems(nc, es, "C2")
        S["act2"] = Sem(nc, es, "C2_act2", 1)
        S["dve2"] = Sem(nc, es, "C2_dve2", 1)
        hT = sbt(nc, es, "h2T", [128, KC, T], BF16)
        hh = sbt(nc, es, "hh", [128, KC, 2], BF16)
        gsl = sbt(nc, es, "gsl", [128, 22, T], BF16)
        wbufs = [sbt(nc, es, f"wf{i}", [128, KC, 256], BF16) for i in range(2)]
        xs = Ring([sbt(nc, es, f"xf{i}", [128, T], F32) for i in range(2)], dsems(nc, es, "C2_xl", 2))
        xo = Ring([sbt(nc, es, f"xg{i}", [128, T], F32) for i in range(2)], dsems(nc, es, "C2_xs", 2))
        sq = Ring([sbt(nc, es, f"sqf{i}", [128, T], BF16) for i in range(2)])
        rstd = sbt(nc, es, "rstdf", [128, T], F32)
        ones = sbt(nc, es, "onesF", [128, 128], BF16)
        g2 = sbt(nc, es, "g2s", [128, KC], F32)
        gf = sbt(nc, es, "gfs", [128, KC], F32)
        cw = sbt(nc, es, "cws", [128, 172, 3], F32)
        cb = sbt(nc, es, "cbs", [128, 172], F32)
        xh = sbt(nc, es, "xhs", [128, KC, 2], F32)
        xh2 = sbt(nc, es, "xh2", [128, KC, 2], BF16)
        rh = sbt(nc, es, "rh", [128, 2], F32)
        U = Ring([(sbt(nc, es, f"Ug{i}", [128, T + 2], F32), sbt(nc, es, f"Uv{i}", [128, T + 2], F32))
                  for i in range(2)])
        cc = Ring([(sbt(nc, es, f"cg{i}", [128, 512], F32), sbt(nc, es, f"cv{i}", [128, 512], F32))
                   for i in range(2)])
        sgr = Ring([sbt(nc, es, f"sg{i}", [128, 512], F32) for i in range(2)])
        ps = [es.enter_context(nc.psum_tensor(f"psF{i}_{_UID[0]}", [128, 512], F32)) for i in range(8)]
        psr = Ring(ps[0:7])
        ph = ps[7]

        c1 = None
        for dst, src in ((g2, "g2"), (cw, "cw"), (cb, "cb"), (xh, "xh")) + (((gf, "gf"),) if final else ()):
            if callable(io[src]):
                if "cc" in io:
                    P.wait("sp", (io["cc"], io["cc"].n))
                c1 = P.op("sp", lambda e, dst=dst, src=src: e.dma_start(
                    out=dst[:].rearrange("p k t -> p (k t)"), in_=io[src]()), inc=S["cld"])
            else:
                c1 = P.op("sp", lambda e, dst=dst, src=src: e.dma_start(out=dst[:], in_=io[src][:]), inc=S["cld"])
        ones_ok = P.op("pool", lambda e: e.memset(ones[:], 1.0), inc=S["pool"])

        a = P.op("act", lambda e: e.activation(out=xh2[:], in_=xh[:], func=AF.Square), waits=[c1], inc=S["act"])
        full = mm_group(P, S, ph, 2, lambda k: ones[:], lambda k: xh2[:, k, :], KC, [a, ones_ok])
        a = P.op("act", lambda e: e.activation(out=rh[:], in_=ph[:, 0:2], func=AF.Sqrt, bias=EPS, scale=1.0 / D),
                 waits=[full], inc=S["act"])
        ph_free = a
        P.op("dve", lambda e: e.reciprocal(out=rh[:], in_=rh[:]), waits=[a])
        hh_done = None
        for c in range(KC):
            hh_done = P.op("dve", lambda e, c=c: e.scalar_tensor_tensor(
                out=hh[:, c, :], in0=xh[:, c, :], scalar=g2[:, c:c + 1], in1=rh[:], op0=ALU.mult, op1=ALU.mult),
                inc=S["dve"])

        rr_ = None
        if "rstd_in" in io:
            rr_ = P.op("sp", lambda e: e.dma_start(out=rstd[:], in_=io["rstd_in"][:]), inc=S["ld"])
        h_done, ps_rel = emit_rmsnorm(P, S, lambda c: io["xT"][c], g2, xs, sq, ones, rstd, ps[0], ps[1],
                                      [ones_ok], lambda c: hT[:, c, :], rstd_ready=rr_)
        psr.free_at[0] = ps_rel
        psr.free_at[1] = ps_rel
        ssq = None

        ws = WStream(P, S, wbufs, dsems(nc, es, "C2_wl", 2))
        sched = []
        for s, (k0, nk) in enumerate(SLABS):
            for jj in range(nk):
                sched.append(("up", s, jj, io["w_up"][k0 + jj], KC))
            for b in range(16):
                sched.append(("dn", s, b, io[f"w_dn{s}"][b], nk))
        pending = [ws.load(sched[i][3], sched[i][4]) for i in range(2)]
        nxt_load = 2
        st_hist = {}
        stage_b = []
        g_done = None
        xp = None

        def emit_stage_b():
            (cgb, cvb, ci, jj, th, dvc) = stage_b.pop(0)
            gi_, sgb, gfw = sgr.acquire()
            a2 = P.op("act", lambda e, sgb=sgb, cgb=cgb: e.activation(out=sgb[:], in_=cgb[:], func=AF.Silu),
                      waits=[dvc, gfw], inc=S["act2"])
            d2 = P.op("dve", lambda e, sgb=sgb, cvb=cvb, jj=jj, th=th: e.tensor_tensor(
                out=gsl[:, jj, th * 512:(th + 1) * 512], in0=sgb[:], in1=cvb[:], op=ALU.mult),
                waits=[a2], inc=S["dve2"])
            sgr.release(gi_, d2)
            cc.release(ci, d2)
            return d2

        for it_, (kind, s, idx, src, nkc) in enumerate(sched):
            wi, wb, wld = pending.pop(0)
            k0, nk = SLABS[s]
            blk_done = None
            if kind == "up":
                jj = idx
                cg_, cv_ = k0 + jj, FC + k0 + jj
                ui, (Ug, Uv), ufw = U.acquire()
                mm_group(P, S, ph, 2, lambda k, wb=wb: wb[:, k, 0:128], lambda k: hh[:, k, :], KC,
                         [wld, ph_free, hh_done], out_ap=ph[:, 0:2])
                hfull = mm_group(P, S, ph, 2, lambda k, wb=wb: wb[:, k, 128:256], lambda k: hh[:, k, :], KC,
                                 [], out_ap=ph[:, 2:4])
                P.op("act", lambda e, Ug=Ug: e.activation(out=Ug[:, 0:2], in_=ph[:, 0:2], func=AF.Copy),
                     waits=[hfull, ufw])
                ph_free = P.op("act", lambda e, Uv=Uv: e.activation(out=Uv[:, 0:2], in_=ph[:, 2:4], func=AF.Copy),
                               inc=S["act"])
                for th in range(2):
                    big, pg, pgw = psr.acquire()
                    fg = mm_group(P, S, pg, 512, lambda k, wb=wb: wb[:, k, 0:128],
                                  lambda k, th=th: hT[:, k, th * 512:(th + 1) * 512], KC, [wld, pgw, h_done])
                    biv, pv, pvw = psr.acquire()
                    fv = mm_group(P, S, pv, 512, lambda k, wb=wb: wb[:, k, 128:256],
                                  lambda k, th=th: hT[:, k, th * 512:(th + 1) * 512], KC, [pvw])
                    blk_done = fv
                    ci, (cgb, cvb), cfw = cc.acquire()
                    lo = 2 + th * 512
                    P.op("act", lambda e, Ug=Ug, pg=pg, lo=lo: e.activation(out=Ug[:, lo:lo + 512], in_=pg[:],
                                                                            func=AF.Copy), waits=[fg, ufw])
                    a = P.op("act", lambda e, cgb=cgb, pg=pg, c=cg_: e.activation(
                        out=cgb[:], in_=pg[:], func=AF.Identity, bias=cb[:, c:c + 1], scale=cw[:, c, 2:3]),
                        waits=[cfw, c1], inc=S["act"])
                    psr.release(big, a)
                    P.op("act", lambda e, Uv=Uv, pv=pv, lo=lo: e.activation(out=Uv[:, lo:lo + 512], in_=pv[:],
                                                                            func=AF.Copy), waits=[fv])
                    a = P.op("act", lambda e, cvb=cvb, pv=pv, c=cv_: e.activation(
                        out=cvb[:], in_=pv[:], func=AF.Identity, bias=cb[:, c:c + 1], scale=cw[:, c, 2:3]),
                        inc=S["act"])
                    psr.release(biv, a)
                    dvc = None
                    for (ub, cbuf, c) in ((Ug, cgb, cg_), (Uv, cvb, cv_)):
                        P.op("dve", lambda e, ub=ub, cbuf=cbuf, c=c, lo=lo: e.scalar_tensor_tensor(
                            out=cbuf[:], in0=ub[:, lo - 1:lo + 511], scalar=cw[:, c, 1:2], in1=cbuf[:],
                            op0=ALU.mult, op1=ALU.add), waits=[a])
                        dvc = P.op("dve", lambda e, ub=ub, cbuf=cbuf, c=c, lo=lo: e.scalar_tensor_tensor(
                            out=cbuf[:], in0=ub[:, lo - 2:lo + 510], scalar=cw[:, c, 0:1], in1=cbuf[:],
                            op0=ALU.mult, op1=ALU.add), inc=S["dve"])
                    if th == 1:
                        U.release(ui, dvc)
                    stage_b.append((cgb, cvb, ci, jj, th, dvc))
                    if len(stage_b) > 1:
                        emit_stage_b()
                if jj == nk - 1:
                    while stage_b:
                        g_done = emit_stage_b()
                    xsrc = (lambda ch: io["xT"][ch]) if s == 0 else (lambda ch: io["xw"][ch])
                    xp = XPipe(P, xs, xsrc, list(range(KC)), st_hist)
                    xp.issue()
                    xp.issue()
            else:
                if s == len(SLABS) - 1 and idx == 0 and (final or "rstd_out" in io):
                    old = psr
                    psr = Ring(ps[0:5])
                    psr.free_at = list(old.free_at[0:5])
                    ssq = {"sq": sq, "psA": ps[5], "psB": ps[6], "ones": ones, "pend": [], "ob_read": {},
                           "bank_free": [old.free_at[5], old.free_at[6], h_done]}
                blk_done = emit_resid_block(P, S, psr, xp, xo, wb, wld, idx, nk,
                                            lambda k, th: gsl[:, k, th * 512:(th + 1) * 512],
                                            lambda ch: io["xw"][ch], [g_done], st_hist, ssq=ssq)
            ws.release(wi, blk_done)
            if nxt_load < len(sched):
                pending.append(ws.load(sched[nxt_load][3], sched[nxt_load][4]))
                nxt_load += 1
        rd = None
        if ssq is not None:
            rd = ssq_finish(P, S, ssq, rstd)
            if "rstd_out" in io:
                P.op("sp", lambda e: e.dma_start(out=io["rstd_out"][:], in_=rstd[:]), waits=[rd], inc=S["st"])
                P.wait("sp", (S["st"], S["st"].n))
        if final:
            for w in xo.all_done():
                P.wait("sp", w)
            P.wait("dve", rd)
            for c in range(KC):
                i, xb, fw = xs.acquire()
                ld = P.op("sp", lambda e, xb=xb, c=c: e.dma_start(out=xb[:], in_=io["xw"][c]), waits=[fw],
                          inc=xs.sems[i])
                oi, ob, ofw = xo.acquire()
                dv = P.op("dve", lambda e, xb=xb, ob=ob, c=c: e.scalar_tensor_tensor(
                    out=ob[:], in0=xb[:], scalar=gf[:, c:c + 1], in1=rstd[:], op0=ALU.mult, op1=ALU.mult),
                    waits=[ld, ofw, ssq["ob_read"].get(oi)], inc=S["dve"])
                xs.release(i, dv)
                st = P.op("sp", lambda e, ob=ob, c=c: e.dma_start(out=io["out"][c], in_=ob[:]), waits=[dv],
                          inc=xo.sems[oi])
                xo.release(oi, st)
        for w in xo.all_done():
            P.wait("sp", w)
        P.run(nc, dyn)


GROUPS = [[0, 1, 2, 3], [4, 5, 6, 7]]


def _blocks(w, order=None):
    K, N = w.shape
    a = w.reshape(K // 128, 128, N // 256, 256).transpose(2, 1, 0, 3)
    if order is not None:
        a = a[order]
    return np.ascontiguousarray(a)


def _fm(v):
    return np.ascontiguousarray(v.reshape(-1, 128).T)


def _dram(nc, name, shape, dt, kind):
    return nc.dram_tensor(name, list(shape), dt, kind=kind).ap()


def prog_allgather(P, cc, pairs, waits):
    r = None
    for i, (src, dst) in enumerate(pairs):
        r = P.op("pool", lambda e, src=src, dst=dst: e.collective_compute(
            "AllGather", ALU.bypass, replica_groups=GROUPS, ins=[src], outs=[dst]),
            waits=waits if i == 0 else (), inc=cc)
    return r


def emit_allgather(nc, cc, pairs):
    with nc.Block() as block:
        @block.gpsimd
        def _(g):
            for (src, dst) in pairs:
                g.collective_compute("AllGather", ALU.bypass, replica_groups=GROUPS, ins=[src], outs=[dst]
                                     ).then_inc(cc.h, 1)
                cc.n += 1
            g.wait_ge(cc.h, cc.n)


def build_fused(L=2, stop=99):
    _POOL.clear()
    nc = bass.Bass("TRN2", target_bir_lowering=False, num_devices=NCORES)
    EI, IN = "ExternalInput", "Internal"
    dyn = {}
    xT = _dram(nc, "xT", [KC, 128, T], F32, EI)
    nsl = _dram(nc, "nsl", [128, 12], F32, EI)
    gf = _dram(nc, "gf", [128, KC], F32, EI)
    out = _dram(nc, "out", [KC, 128, T], F32, "ExternalOutput")
    xA = _dram(nc, "xA", [KC, 128, T], F32, IN)
    xB = _dram(nc, "xB", [KC, 128, T], F32, IN)
    qk_send = _dram(nc, "qk_send", [32, 128, T], BF16, IN)
    v_send2 = _dram(nc, "v_send2", [4, T, 512], BF16, IN)
    gT = _dram(nc, "gTd", [16, 128, T], BF16, IN)
    qk_all = _dram(nc, "qk_all", [4 * 32 * 128, T], BF16, IN)
    v_all2 = _dram(nc, "v_all2", [4, 4 * T, 512], BF16, IN)
    a_send3 = _dram(nc, "a_send3", [4, 128, 4096], BF16, IN)
    a_all3 = _dram(nc, "a_all3", [4, 4, 128, 4096], BF16, IN)
    xh_send = _dram(nc, "xh_send", [128, 2 * KC], F32, IN)
    xh_ext = _dram(nc, "xh_ext", [5 * 128, 2 * KC], F32, IN)
    my_qk2 = _dram(nc, "my_qk2", [2, 4, 4, 128, T], BF16, IN)
    my_v = _dram(nc, "my_v", [4096, 512], BF16, IN)
    my_a = _dram(nc, "my_a", [16, 128, T], BF16, IN)
    rstd_c1 = _dram(nc, "rstd_c1", [128, T], F32, IN)
    rstd_c2 = _dram(nc, "rstd_c2", [128, T], F32, IN)
    cc = Sem(nc, None, "cc", 1)

    qk_s2 = qk_send.rearrange("c p t -> (c p) t")
    ag_qkv = [(qk_s2[j * 512:(j + 1) * 512, :], qk_all[j * 2048:(j + 1) * 2048, :]) for j in range(8)]
    ag_qkv += [(v_send2[g], v_all2[g]) for g in range(4)]
    ag_a = [(a_send3[h], a_all3[h].rearrange("rr p t -> (rr p) t")) for h in range(4)]
    xh5 = xh_ext.rearrange("(b p) c -> b p c", p=128)

    with nc.Block() as block0:
        @block0.sync
        def _(sp):
            dyn["r"] = sp.snap(sp.partition_id() % 4, min_val=0, max_val=3)
            dyn["r2"] = sp.snap(dyn["r"] * 2, min_val=0, max_val=6)

    x_in = xT
    for l in range(L):
        wl = [("g1", [128, KC]), ("w_in", [40, 128, KC, 256]), ("normg", [128, 2048]), ("wsT", [128, 16, 128]),
              ("brow", [1, 2048])]
        wl += [("mixgA", [128, 16]), ("mixgB", [128, 4])]
        if stop >= 5:
            wl += [("w_out", [2, 16, 128, 16, 256])]
        if stop >= 7:
            wl += [("g2", [128, KC]), ("w_up", [FC, 128, KC, 256]), ("cw", [128, 172, 3]), ("cb", [128, 172])]
        W = {k: _dram(nc, f"{k}_{l}", shp, F32, EI) for k, shp in wl}
        if stop >= 7:
            for s_, (k0, nk) in enumerate(SLABS):
                W[f"w_dn{s_}"] = _dram(nc, f"w_dn{s_}_{l}", [16, 128, nk, 256], F32, EI)
        ioA = {"xT": x_in, "g1": W["g1"], "w_in": W["w_in"], "normg": W["normg"], "wsT": W["wsT"],
               "brow": W["brow"], "qk": qk_send, "gT": gT, "mixgA": W["mixgA"],
               "v": lambda cb: v_send2[cb // 2].rearrange("(tc p) c -> p tc c", p=128)[
                   :, :, (cb % 2) * 256:(cb % 2) * 256 + 256]}
        if l == 0:
            ioA["zero_dst"] = xh_ext[0:128, :]
        else:
            ioA["rstd_in"] = rstd_c2
        ioA.update({"cc": cc, "ag_qk": ag_qkv[0:8], "ag_v": ag_qkv[8:12]})
        phase_A(nc, ioA, dyn)
        if stop < 2:
            break
        ioB = {
            "pre": [
                (my_qk2.rearrange("jj rr c p t -> jj (rr c p) t"),
                 lambda: qk_all.rearrange("(j x) t -> j x t", j=8)[bass.ds(dyn["r2"], 2)]),
                (my_v[:], lambda: v_all2[bass.ds(dyn["r"], 1)].rearrange("a t c -> (a t) c")),
            ],
            "q": lambda h: my_qk2[0][:, h].rearrange("rr p t -> p rr t"),
            "k": lambda h: my_qk2[1][:, h].rearrange("rr p t -> p rr t"),
            "v": lambda h, d, r: my_v[:, h * 128:(h + 1) * 128].rearrange("(n p r) e -> r p n e", p=128, r=d)[r],
            "nsl": nsl, "a": lambda h: a_send3[h].rearrange("p (tb t) -> p tb t", tb=4),
            "cc": cc, "ag_a": ag_a, "mixgB": W["mixgB"],
        }
        if stop < 3:
            break
        phase_B(nc, ioB, dyn)
        if stop < 4:
            break
        ioC1 = {"xT": x_in,
                "pre": [(my_a.rearrange("h p t -> (h p) t"),
                         lambda: a_all3.rearrange("h rr p (tb t) -> tb (h rr p) t", tb=4)[
                             bass.ds(dyn["r"], 1)].rearrange("a x t -> (a x) t"))],
                "aT": lambda c: my_a[(c % 4) * 4 + c // 4],
                "cc": cc, "ag_xh": [(xh_send[:], xh_ext[128:640, :])],
                "gT": gT, "w_out": W["w_out"], "xo": xA, "rstd_out": rstd_c1,
                "halo_dst": lambda ch: xh_send[:, 2 * ch:2 * ch + 2]}
        if stop < 5:
            break
        phase_C1(nc, ioC1, dyn)
        if stop < 6:
            break
        if stop < 7:
            break
        final = (l == L - 1)
        ioC2 = {"xT": xA, "xh": lambda: xh5[bass.ds(dyn["r"], 1)].rearrange("a p c -> p (a c)"),
                "g2": W["g2"], "w_up": W["w_up"], "cw": W["cw"], "cb": W["cb"], "xw": xB, "cc": cc,
                "rstd_in": rstd_c1}
        if not final:
            ioC2["rstd_out"] = rstd_c2
        for s_ in range(4):
            ioC2[f"w_dn{s_}"] = W[f"w_dn{s_}"]
        if final:
            ioC2["gf"] = gf
            ioC2["out"] = out
        phase_C2(nc, ioC2, final, dyn)
        x_in = xB
    return nc


def _to_xT(x):
    out = []
    for c in range(NCORES):
        b, r = divmod(c, 4)
        xs = x[b, r * T:(r + 1) * T, :]
        out.append(np.ascontiguousarray(xs.T.reshape(KC, 128, T)))
    return out


def kernel(x, attn_norm_g, w_in, sgu_norm_g, w_spatial, b_spatial, mix_norm_g, w_out,
           ffn_norm_g, w_up, conv_w, conv_b, w_down, final_norm_g):
    f = lambda a: np.asarray(a, dtype=np.float32)
    L = 2
    xT = _to_xT(f(x))
    slopes = 2.0 ** (-8.0 * np.arange(1, 17, dtype=np.float64) / 16.0)
    shared = {"gf": _fm(f(final_norm_g))}
    percore = {}
    order = list(range(24)) + list(range(32, 40)) + list(range(24, 32))
    for l in range(L):
        shared[f"g1_{l}"] = _fm(f(attn_norm_g[l]))
        shared[f"w_in_{l}"] = _blocks(f(w_in[l]), order)
        shared[f"normg_{l}"] = np.ascontiguousarray(np.broadcast_to(f(sgu_norm_g[l])[None, :], (128, 2048)))
        shared[f"wsT_{l}"] = np.ascontiguousarray(f(w_spatial[l]).transpose(2, 0, 1))
        shared[f"brow_{l}"] = np.ascontiguousarray(f(b_spatial[l]).reshape(1, 2048))
        mg = _fm(f(mix_norm_g[l]))
        shared[f"mixgA_{l}"] = np.ascontiguousarray(mg[:, 16:32])
        for hg in range(4):
            percore.setdefault(hg, {})[f"mixgB_{l}"] = np.ascontiguousarray(mg[:, 4 * hg:4 * hg + 4])
        wo = f(w_out[l])
        shared[f"w_out_{l}"] = np.stack([_blocks(wo[2048:]), _blocks(wo[:2048])], 0)
        shared[f"g2_{l}"] = _fm(f(ffn_norm_g[l]))
        wu = f(w_up[l])
        shared[f"w_up_{l}"] = np.ascontiguousarray(
            wu.reshape(KC, 128, 2, FC, 128).transpose(3, 1, 0, 2, 4).reshape(FC, 128, KC, 256))
        wd = f(w_down[l])
        for s_, (k0, nk) in enumerate(SLABS):
            shared[f"w_dn{s_}_{l}"] = np.ascontiguousarray(
                wd[k0 * 128:(k0 + nk) * 128, :].reshape(nk, 128, 16, 256).transpose(2, 1, 0, 3))
        shared[f"cw_{l}"] = np.ascontiguousarray(f(conv_w[l]).reshape(3, 172, 128).transpose(2, 1, 0))
        shared[f"cb_{l}"] = np.ascontiguousarray(f(conv_b[l]).reshape(172, 128).T)
    in_maps = []
    for c in range(NCORES):
        hg = c % 4
        nsl = np.zeros((128, 12), np.float32)
        for hl in range(4):
            for di, d in enumerate((1, 4, 16)):
                nsl[:, hl * 3 + di] = np.float32(-slopes[4 * hg + hl] * d)
        m = dict(shared)
        m.update(percore[hg])
        m["xT"] = xT[c]
        m["nsl"] = nsl
        in_maps.append(m)
    nc = build_fused(L)
    res = run_bass_kernel_spmd(nc, in_maps, core_ids=list(range(NCORES))).results
    y = np.empty((2, 4096, D), np.float32)
    for c in range(NCORES):
        b, r = divmod(c, 4)
        y[b, r * T:(r + 1) * T, :] = np.asarray(res[c]["out"]).reshape(D, T).T
    return y
```

```python
import numpy as np
import ml_dtypes
from contextlib import ExitStack
import concourse.bass as bass
import concourse.mybir as mybir
from concourse.bass_utils import run_bass_kernel_spmd

F32 = mybir.dt.float32
BF16 = mybir.dt.bfloat16
I32 = mybir.dt.int32
AF = mybir.ActivationFunctionType
ALU = mybir.AluOpType
NPBF = ml_dtypes.bfloat16

NCORES = 8
T = 1024
D = 4096
KC = 32
DFF = 11008
FC = 86
SLABS = [(0, 22), (22, 22), (44, 21), (65, 21)]
EPS = 1e-6
SCALE = 128.0 ** -0.5


_POOL = {}
_UID = [0]


class Sem:
    def __new__(cls, nc, es, name, step=1):
        key = (id(nc), name)
        if key in _POOL:
            return _POOL[key]
        o = object.__new__(cls)
        o.h = nc.alloc_semaphore(name=name)
        o.step = step
        o.n = 0
        _POOL[key] = o
        return o

    def __init__(self, nc, es, name, step=1):
        pass


class Prog:
    ENG = ("pe", "act", "dve", "pool", "sp")

    def __init__(self):
        self.q = {k: [] for k in self.ENG}
        self.waited = {k: {} for k in self.ENG}

    def wait(self, eng, w):
        if w is None:
            return
        s, v = w
        if v <= 0:
            return
        if v > self.waited[eng].get(id(s), 0):
            self.waited[eng][id(s)] = v
            self.q[eng].append(lambda e, s=s, v=v: e.wait_ge(s.h, v))

    def op(self, eng, fn, waits=(), inc=None):
        for w in waits:
            self.wait(eng, w)
        if inc is None:
            self.q[eng].append(fn)
            return None
        inc.n += inc.step
        self.q[eng].append(lambda e, fn=fn, inc=inc: fn(e).then_inc(inc.h, inc.step))
        return (inc, inc.n)

    def run(self, nc, dyn=None):
        q = self.q
        with nc.Block() as block:
            @block.tensor
            def _(e):
                for f in q["pe"]:
                    f(e)

            @block.scalar
            def _(e):
                for f in q["act"]:
                    f(e)

            @block.vector
            def _(e):
                for f in q["dve"]:
                    f(e)

            @block.gpsimd
            def _(e):
                for f in q["pool"]:
                    f(e)

            @block.sync
            def _(e):
                for f in q["sp"]:
                    f(e)


class Ring:
    def __init__(self, bufs, sems=None):
        self.bufs = bufs
        self.free_at = [None] * len(bufs)
        self.sems = sems
        self.i = 0

    def all_done(self):
        return [(sm, sm.n) for sm in self.sems]

    def acquire(self):
        i = self.i
        self.i = (i + 1) % len(self.bufs)
        return i, self.bufs[i], self.free_at[i]

    def release(self, i, w):
        self.free_at[i] = w


def sbt(nc, es, name, shape, dt):
    return es.enter_context(nc.sbuf_tensor(f"s_{name}_{_UID[0]}", shape, dt))


def dsems(nc, es, tag, n):
    return [Sem(nc, es, f"{tag}{i}", 16) for i in range(n)]


def emit_rmsnorm(P, S, x_src, g_sb, xs_ring, sq_ring, ones, rstd, psA, psB, pe_waits, dst_fn,
                 x_ready=None, rstd_ready=None):
    last_pe = None
    for c in range(KC if rstd_ready is None else 0):
        i, xb, fw = xs_ring.acquire()
        ld = P.op("sp", lambda e, xb=xb, c=c: e.dma_start(out=xb[:], in_=x_src(c)),
                  waits=[fw, x_ready], inc=xs_ring.sems[i])
        j, sqb, sfw = sq_ring.acquire()
        a = P.op("act", lambda e, xb=xb, sqb=sqb: e.activation(out=sqb[:], in_=xb[:], func=AF.Square),
                 waits=[ld, sfw], inc=S["act"])
        xs_ring.release(i, a)
        w0 = [a] + (list(pe_waits) if c == 0 else [])
        P.op("pe", lambda e, sqb=sqb, c=c: e.matmul(psA[:], lhsT=ones[:], rhs=sqb[:, 0:512],
                                                    start=(c == 0), stop=(c == KC - 1)), waits=w0)
        last_pe = P.op("pe", lambda e, sqb=sqb, c=c: e.matmul(psB[:], lhsT=ones[:], rhs=sqb[:, 512:1024],
                                                              start=(c == 0), stop=(c == KC - 1)),
                       inc=S["pe"])
        sq_ring.release(j, last_pe)
    ps_rel = None
    if rstd_ready is None:
        P.op("act", lambda e: e.activation(out=rstd[:, 0:512], in_=psA[:], func=AF.Sqrt, bias=EPS, scale=1.0 / D),
             waits=[last_pe])
        a = P.op("act", lambda e: e.activation(out=rstd[:, 512:1024], in_=psB[:], func=AF.Sqrt, bias=EPS,
                                               scale=1.0 / D), inc=S["act"])
        ps_rel = a
        rstd_ready = P.op("dve", lambda e: e.reciprocal(out=rstd[:], in_=rstd[:]), waits=[a], inc=S["dve"])
    last = None
    for c in range(KC):
        i, xb, fw = xs_ring.acquire()
        ld = P.op("sp", lambda e, xb=xb, c=c: e.dma_start(out=xb[:], in_=x_src(c)), waits=[fw, x_ready],
                  inc=xs_ring.sems[i])
        last = P.op("dve", lambda e, xb=xb, c=c: e.scalar_tensor_tensor(
            out=dst_fn(c), in0=xb[:], scalar=g_sb[:, c:c + 1], in1=rstd[:], op0=ALU.mult, op1=ALU.mult),
            waits=[ld, rstd_ready], inc=S["dve"])
        xs_ring.release(i, last)
    return last, ps_rel


class WStream:
    def __init__(self, P, S, bufs, sems):
        self.P, self.S = P, S
        self.ring = Ring(bufs, sems)

    def load(self, src_ap, nkc):
        i, wb, fw = self.ring.acquire()
        ld = self.P.op("pool", lambda e, wb=wb: e.dma_start(out=wb[:, 0:nkc, :], in_=src_ap),
                       waits=[fw], inc=self.ring.sems[i])
        return i, wb, ld

    def release(self, i, w):
        self.ring.release(i, w)


def mm_group(P, S, ps, n, lhsT_fn, rhs_fn, nk, waits, out_ap=None):
    o = out_ap if out_ap is not None else ps[:, 0:n]
    r = None
    for k in range(nk):
        w = waits if k == 0 else ()
        fn = lambda e, k=k: e.matmul(o, lhsT=lhsT_fn(k), rhs=rhs_fn(k), start=(k == 0), stop=(k == nk - 1))
        if k == nk - 1:
            r = P.op("pe", fn, waits=w, inc=S["pe"])
        else:
            P.op("pe", fn, waits=w)
    return r


def mk_sems(nc, es, tag):
    S = {}
    for nm, st in (("ld", 16), ("wld", 16), ("st", 16), ("pe", 1), ("act", 1), ("dve", 1), ("pool", 1),
                   ("cld", 16)):
        S[nm] = Sem(nc, es, f"{tag}_{nm}", st)
    return S


def phase_A(nc, io, dyn=None):
    P = Prog()
    _UID[0] += 1
    with ExitStack() as es:
        S = mk_sems(nc, es, "A")
        hT = sbt(nc, es, "hT", [128, KC, T], BF16)
        uT = sbt(nc, es, "uT", [128, 16, T], BF16)
        vn = sbt(nc, es, "vn", [128, 8, 2048], BF16)
        wbufs = [sbt(nc, es, f"w{i}", [128, KC, 256], BF16) for i in range(2)]
        xs = Ring([sbt(nc, es, f"xs{i}", [128, T], F32) for i in range(2)], dsems(nc, es, "A_xl", 2))
        sq = Ring([sbt(nc, es, f"sq{i}", [128, T], BF16) for i in range(2)])
        rstd = sbt(nc, es, "rstd", [128, T], F32)
        ones = sbt(nc, es, "ones", [128, 128], BF16)
        gsb = sbt(nc, es, "gsb", [128, KC], F32)
        normg = sbt(nc, es, "normg", [128, 2048], BF16)
        WT = sbt(nc, es, "WT", [128, 16, 128], BF16)
        brow = sbt(nc, es, "brow", [1, 2048], BF16)
        onesr = sbt(nc, es, "onesr", [1, 128], BF16)
        qst = Ring([sbt(nc, es, f"qst{i}", [128, T], BF16) for i in range(2)], dsems(nc, es, "A_qs", 2))
        vst = Ring([sbt(nc, es, f"vst{i}", [128, 8, 256], BF16) for i in range(2)], dsems(nc, es, "A_vs", 2))
        ss = sbt(nc, es, "ss", [128, 128], F32)
        rs = sbt(nc, es, "rs", [128, 128], F32)
        junk = sbt(nc, es, "junk", [128, 128], F32)
        ps = [es.enter_context(nc.psum_tensor(f"psA{i}_{_UID[0]}", [128, 512], F32)) for i in range(8)]
        psr = Ring(ps)

        c1 = P.op("sp", lambda e: e.dma_start(out=gsb[:], in_=io["g1"][:]), inc=S["cld"])
        if "zero_dst" in io:
            zt = sbt(nc, es, "zt", [128, 64], F32)
            zr = P.op("pool", lambda e: e.memset(zt[:], 0.0), inc=S["pool"])
            c1 = P.op("sp", lambda e: e.dma_start(out=io["zero_dst"], in_=zt[:]), waits=[zr], inc=S["cld"])
        P.op("pool", lambda e: e.dma_start(out=normg[:], in_=io["normg"][:]), inc=S["wld"])
        P.op("pool", lambda e: e.dma_start(out=WT[:], in_=io["wsT"][:]), inc=S["wld"])
        c2 = P.op("pool", lambda e: e.dma_start(out=brow[:], in_=io["brow"][:]), inc=S["wld"])
        P.op("pool", lambda e: e.memset(ones[:], 1.0))
        P.op("pool", lambda e: e.memset(onesr[:], 1.0))
        P.op("pool", lambda e: e.memset(ss[:], 0.0))
        cpool = P.op("pool", lambda e: e.affine_select(out=WT[:], in_=WT[:], pattern=[[0, 16], [1, 128]],
                                                       compare_op=ALU.is_ge, fill=0.0, base=0,
                                                       channel_multiplier=-1), waits=[c2], inc=S["pool"])

        rr_ = None
        if "rstd_in" in io:
            rr_ = P.op("sp", lambda e: e.dma_start(out=rstd[:], in_=io["rstd_in"][:]), inc=S["ld"])
        h_done, ps_rel = emit_rmsnorm(P, S, lambda c: io["xT"][c], gsb, xs, sq, ones, rstd, ps[0], ps[1],
                                      [cpool], lambda c: hT[:, c, :], rstd_ready=rr_)
        psr.free_at[0] = ps_rel
        psr.free_at[1] = ps_rel

        mixgA = sbt(nc, es, "mixgA", [128, 16], F32)
        cmg = P.op("sp", lambda e: e.dma_start(out=mixgA[:], in_=io["mixgA"][:]), inc=S["cld"])
        sgu_state = {}

        def emit_sgu(g):
            dv = None
            for tq in range(2):
                bi, pb, pfw = psr.acquire()
                full = None
                for i in range(4):
                    tc = tq * 4 + i
                    P.op("pe", lambda e, pb=pb, i=i, tc=tc, g=g: e.matmul(
                        pb[:, i * 128:(i + 1) * 128], lhsT=vn[:, tc, g * 128:(g + 1) * 128], rhs=WT[:, g, :],
                        start=True, stop=False), waits=[pfw, sgu_state["vn_done"], cpool] if i == 0 else ())
                    full = P.op("pe", lambda e, pb=pb, i=i, g=g: e.matmul(
                        pb[:, i * 128:(i + 1) * 128], lhsT=onesr[0:1, :], rhs=brow[0:1, g * 128:(g + 1) * 128],
                        start=False, stop=True), inc=S["pe"])
                dv = P.op("dve", lambda e, pb=pb, g=g, tq=tq: e.tensor_tensor(
                    out=uT[:, g, tq * 512:(tq + 1) * 512], in0=pb[:], in1=uT[:, g, tq * 512:(tq + 1) * 512],
                    op=ALU.mult), waits=[full, u_ready[g]], inc=S["dve"])
                psr.release(bi, dv)
            j, sqb, sfw = sq.acquire()
            a = P.op("act", lambda e, sqb=sqb, g=g: e.activation(out=sqb[:], in_=uT[:, g, :], func=AF.Square),
                     waits=[dv, sfw], inc=S["act"])
            sgu_state.setdefault("pend", []).append((g, j, sqb, a))

        def emit_sgu2():
            g, j, sqb, a = sgu_state["pend"].pop(0)
            ri, rb, rfw = xs.acquire()
            fulls, banks = [], []
            for th in range(2):
                bi, pb, pfw = psr.acquire()
                banks.append((bi, pb))
                fulls.append(P.op("pe", lambda e, pb=pb, sqb=sqb, th=th: e.matmul(
                    pb[:], lhsT=ones[:], rhs=sqb[:, th * 512:(th + 1) * 512], start=True, stop=True),
                    waits=[a, pfw], inc=S["pe"]))
            sq.release(j, fulls[1])
            a2 = None
            for th in range(2):
                a2 = P.op("act", lambda e, rb=rb, pb=banks[th][1], th=th: e.activation(
                    out=rb[:, th * 512:(th + 1) * 512], in_=pb[:], func=AF.Sqrt, bias=EPS, scale=1.0 / 128),
                    waits=[fulls[th], rfw], inc=S["act"])
                psr.release(banks[th][0], a2)
            P.op("dve", lambda e, rb=rb: e.reciprocal(out=rb[:], in_=rb[:]), waits=[a2, cmg])
            gd = P.op("dve", lambda e, rb=rb, g=g: e.scalar_tensor_tensor(
                out=uT[:, g, :], in0=uT[:, g, :], scalar=mixgA[:, g:g + 1], in1=rb[:], op0=ALU.mult, op1=ALU.mult),
                inc=S["dve"])
            xs.release(ri, gd)
            P.op("sp", lambda e, g=g: e.dma_start(out=io["gT"][g], in_=uT[:, g, :]), waits=[gd], inc=S["st"])

        ws = WStream(P, S, wbufs, dsems(nc, es, "A_wl", 2))
        NB = 40
        pending = []
        for b in range(min(2, NB)):
            pending.append(ws.load(io["w_in"][b], KC))
        u_ready = [None] * 16
        vn_done = None
        for b in range(NB):
            wi, wb, wld = pending.pop(0)
            blk_done = None
            if b < 16 or b >= 32:
                for j in range(2):
                    ch = 2 * b + j if b < 16 else 2 * (b - 32) + j
                    if b < 16:
                        si, stg, sfw = qst.acquire()
                    a = None
                    for th in range(2):
                        bi, pb, pfw = psr.acquire()
                        full = mm_group(P, S, pb, 512,
                                        lambda k, wb=wb, j=j: wb[:, k, j * 128:(j + 1) * 128],
                                        lambda k, th=th: hT[:, k, th * 512:(th + 1) * 512],
                                        KC, [wld, pfw, h_done])
                        if b < 16:
                            a = P.op("act", lambda e, stg=stg, pb=pb, th=th: e.activation(
                                out=stg[:, th * 512:(th + 1) * 512], in_=pb[:], func=AF.Copy),
                                waits=[full, sfw], inc=S["act"])
                        else:
                            a = P.op("act", lambda e, pb=pb, th=th, ch=ch: e.activation(
                                out=uT[:, ch, th * 512:(th + 1) * 512], in_=pb[:], func=AF.Gelu),
                                waits=[full], inc=S["act"])
                            u_ready[ch] = a
                        psr.release(bi, a)
                        blk_done = a
                    if b < 16:
                        hd = ch % 16
                        cho = (hd // 4) * 8 + (4 if ch >= 16 else 0) + hd % 4
                        st = P.op("sp", lambda e, stg=stg, cho=cho: e.dma_start(out=io["qk"][cho], in_=stg[:]),
                                  waits=[a, c1], inc=qst.sems[si])
                        qst.release(si, st)
            else:
                isv = b < 24
                cb = (b - 16) if isv else (b - 24)
                if isv:
                    si, stg, sfw = vst.acquire()
                a = None
                for tc in range(8):
                    bi, pb, pfw = psr.acquire()
                    full = mm_group(P, S, pb, 256,
                                    lambda k, tc=tc: hT[:, k, tc * 128:(tc + 1) * 128],
                                    lambda k, wb=wb: wb[:, k, :],
                                    KC, [wld, pfw, h_done])
                    if isv:
                        a = P.op("act", lambda e, stg=stg, pb=pb, tc=tc: e.activation(
                            out=stg[:, tc, :], in_=pb[:, 0:256], func=AF.Copy), waits=[full, sfw], inc=S["act"])
                    else:
                        a = P.op("act", lambda e, pb=pb, tc=tc, cb=cb: e.activation(
                            out=vn[:, tc, cb * 256:(cb + 1) * 256], in_=pb[:, 0:256], func=AF.Gelu),
                            waits=[full], inc=S["act"])
                        for gg in range(2):
                            g = cb * 2 + gg
                            P.op("act", lambda e, tc=tc, g=g: e.activation(
                                out=junk[:], in_=vn[:, tc, g * 128:(g + 1) * 128], func=AF.Square,
                                accum_out=ss[:, tc * 16 + g:tc * 16 + g + 1]))
                    psr.release(bi, a)
                    blk_done = a
                if isv:
                    vdst = io["v"](cb)
                    st = P.op("sp", lambda e, stg=stg, vdst=vdst: e.dma_start(out=vdst, in_=stg[:]),
                              waits=[a], inc=vst.sems[si])
                    vst.release(si, st)
            ws.release(wi, blk_done)
            if b + 2 < NB:
                pending.append(ws.load(io["w_in"][b + 2], KC))
            if b == 15 and "ag_qk" in io:
                prog_allgather(P, io["cc"], io["ag_qk"], qst.all_done())
            if b == 23 and "ag_v" in io:
                prog_allgather(P, io["cc"], io["ag_v"], vst.all_done())
            if b == 31:
                a = P.op("act", lambda e: e.activation(out=rs[:], in_=ss[:], func=AF.Sqrt, bias=EPS,
                                                       scale=1.0 / 128), inc=S["act"])
                P.op("dve", lambda e: e.reciprocal(out=rs[:], in_=rs[:]), waits=[a, c2])
                for tc in range(8):
                    for g in range(16):
                        col = tc * 16 + g
                        vn_done = P.op("dve", lambda e, tc=tc, g=g, col=col: e.scalar_tensor_tensor(
                            out=vn[:, tc, g * 128:(g + 1) * 128], in0=vn[:, tc, g * 128:(g + 1) * 128],
                            scalar=rs[:, col:col + 1], in1=normg[:, g * 128:(g + 1) * 128],
                            op0=ALU.mult, op1=ALU.mult), inc=S["dve"])
                sgu_state["vn_done"] = vn_done
            if b >= 32:
                while sgu_state.get("pend"):
                    emit_sgu2()
                emit_sgu(2 * (b - 32))
                emit_sgu(2 * (b - 32) + 1)

        while sgu_state.get("pend"):
            emit_sgu2()
        P.wait("sp", (S["st"], S["st"].n))
        for w in qst.all_done() + vst.all_done():
            P.wait("sp", w)
        P.run(nc, dyn)


def attn_groups(d):
    out = []
    nb = 32 // d
    if d == 16:
        for r0 in range(0, 16, 2):
            out.append(([(r0, 0), (r0, 1), (r0 + 1, 0), (r0 + 1, 1)], ("pair", r0)))
    else:
        for r in range(d):
            for n0 in range(0, nb, 4):
                out.append(([(r, n0 + i) for i in range(4)], ("run", r, n0)))
    return out


def phase_B(nc, io, dyn=None):
    P = Prog()
    _UID[0] += 1
    NS = 4096
    with ExitStack() as es:
        S = mk_sems(nc, es, "B")
        S["sfull"] = Sem(nc, es, "B_sfull", 1)
        S["ofull"] = Sem(nc, es, "B_ofull", 1)
        S["fin"] = Sem(nc, es, "B_fin", 1)
        nset = 2
        qT = [sbt(nc, es, f"qT{i}", [128, NS], BF16) for i in range(nset)]
        kT = [sbt(nc, es, f"kT{i}", [128, NS], BF16) for i in range(nset)]
        vd = [{d: sbt(nc, es, f"v{d}_{i}", [128, 32, 128], BF16) for d in (1, 4, 16)} for i in range(nset)]
        inring = Ring(list(range(nset)), dsems(nc, es, "B_hl", nset))
        acc_o = sbt(nc, es, "acc_o", [128, NS], F32)
        acc_z = sbt(nc, es, "acc_z", [128, NS], F32)
        aT = sbt(nc, es, "aT", [128, NS], BF16)
        EB = sbt(nc, es, "EB", [128, 12, 4, 256], BF16)
        sqh = sbt(nc, es, "sqh", [128, NS], BF16)
        mixgB = sbt(nc, es, "mixgB", [128, 4], F32)
        pT = Ring([sbt(nc, es, f"pT{i}", [128, 1024], BF16) for i in range(2)])
        pT2 = Ring([sbt(nc, es, f"pTm{i}", [128, 1024], BF16) for i in range(2)])
        ones = sbt(nc, es, "onesB", [128, 128], BF16)
        nsl = sbt(nc, es, "nsl", [128, 12], F32)
        it = sbt(nc, es, "it", [128, 256], I32)
        dist = sbt(nc, es, "dist", [128, 256], F32)
        etmp = Ring([sbt(nc, es, f"etmp{i}", [128, 256], F32) for i in range(2)])
        ps = [es.enter_context(nc.psum_tensor(f"psB{i}_{_UID[0]}", [128, 512], F32)) for i in range(8)]
        sp_ring = Ring([0, 1])
        o_ring = Ring([0, 1])

        P.op("sp", lambda e: e.dma_start(out=mixgB[:], in_=io["mixgB"][:]), inc=S["cld"])
        c1 = P.op("sp", lambda e: e.dma_start(out=nsl[:], in_=io["nsl"][:]), inc=S["cld"])
        P.op("pool", lambda e: e.memset(ones[:], 1.0))
        P.op("pool", lambda e: e.iota(it[:], [[1, 256]], base=0, channel_multiplier=-1))
        itd = P.op("pool", lambda e: e.memset(ones[:, 0:1], 1.0), inc=S["pool"])
        P.op("dve", lambda e: e.tensor_copy(out=dist[:], in_=it[:]), waits=[itd])
        P.op("dve", lambda e: e.tensor_scalar(out=dist[:, 0:128], in0=dist[:, 0:128], scalar1=128.0,
                                              scalar2=None, op0=ALU.add))
        dready = P.op("dve", lambda e: e.tensor_scalar(out=dist[:, 128:256], in0=dist[:, 128:256],
                                                       scalar1=-128.0, scalar2=None, op0=ALU.add), inc=S["dve"])
        eb_done = None
        for idx in range(12):
            ei, eb, efw = etmp.acquire()
            a = P.op("act", lambda e, eb=eb, idx=idx: e.activation(out=eb[:], in_=dist[:], func=AF.Exp,
                                                                   scale=nsl[:, idx:idx + 1]),
                     waits=[dready, c1, efw], inc=S["act"])
            P.op("pool", lambda e, eb=eb, idx=idx: e.affine_select(
                out=EB[:, idx, 0, 0:128], in_=eb[:, 0:128], pattern=[[-1, 128]], compare_op=ALU.is_ge, fill=0.0,
                base=0, channel_multiplier=1), waits=[a])
            eb_done = P.op("pool", lambda e, eb=eb, idx=idx: e.affine_select(
                out=EB[:, idx, 0, 128:256], in_=eb[:, 128:256], pattern=[[1, 128]], compare_op=ALU.is_ge, fill=0.0,
                base=0, channel_multiplier=-1), inc=S["pool"])
            etmp.release(ei, eb_done)
            for rep in range(1, 4):
                eb_done = P.op("pool", lambda e, idx=idx, rep=rep: e.tensor_copy(
                    out=EB[:, idx, rep, :], in_=EB[:, idx, 0, :]), inc=S["pool"])

        pre_done = None
        if "cc" in io:
            P.wait("sp", (io["cc"], io["cc"].n))
        for (dst, srcf) in io.get("pre", []):
            pre_done = P.op("sp", lambda e, dst=dst, srcf=srcf: e.dma_start(out=dst, in_=srcf()), inc=S["cld"])

        def load_head(h):
            i, si, fw = inring.acquire()
            lsem = inring.sems[i]
            P.op("sp", lambda e: e.dma_start(out=qT[si][:].rearrange("p (r t) -> p r t", r=4), in_=io["q"](h)),
                 waits=[fw, pre_done], inc=lsem)
            P.op("sp", lambda e: e.dma_start(out=kT[si][:].rearrange("p (r t) -> p r t", r=4), in_=io["k"](h)),
                 inc=lsem)
            ld = None
            for d in (1, 4, 16):
                nb = 32 // d
                for r in range(d):
                    ld = P.op("sp", lambda e, d=d, r=r, nb=nb: e.dma_start(
                        out=vd[si][d][:, r * nb:(r + 1) * nb, :], in_=io["v"](h, d, r)), inc=lsem)
            return i, si, ld

        nxt = load_head(0)
        a_st = None
        fin_prev = None
        for h in range(4):
            ri, si, ld = nxt
            if h + 1 < 4:
                nxt = load_head(h + 1)
            q_, k_ = qT[si], kT[si]
            last_pe_head = None
            for di, d in enumerate((1, 4, 16)):
                nb = 32 // d
                groups = attn_groups(d)
                qv = q_[:].rearrange("p (m r) -> p r m", r=d)
                kv = k_[:].rearrange("p (m r) -> p r m", r=d)
                ebi = h * 3 + di
                state = {}

                def p1(gi):
                    blocks, sel = groups[gi]
                    _, sset, sfw = sp_ring.acquire()
                    full = None
                    first = True
                    for b, (r, n) in enumerate(blocks):
                        bank = ps[4 * sset + b // 2]
                        off = (b % 2) * 256
                        qa = qv[:, r, n * 128:(n + 1) * 128]
                        if n > 0:
                            ka = kv[:, r, (n - 1) * 128:n * 128]
                            P.op("pe", lambda e, bank=bank, off=off, ka=ka, qa=qa: e.matmul(
                                bank[:, off:off + 128], lhsT=ka, rhs=qa, start=True, stop=True),
                                waits=[sfw, ld] if first else ())
                            first = False
                        ka = kv[:, r, n * 128:(n + 1) * 128]
                        fn = lambda e, bank=bank, off=off, ka=ka, qa=qa: e.matmul(
                            bank[:, off + 128:off + 256], lhsT=ka, rhs=qa, start=True, stop=True)
                        if b == 3:
                            full = P.op("pe", fn, waits=[sfw, ld] if first else (), inc=S["sfull"])
                        else:
                            P.op("pe", fn, waits=[sfw, ld] if first else ())
                        first = False
                    pi, pbuf, pfw = pT.acquire()
                    P.op("act", lambda e, pbuf=pbuf, sset=sset: e.activation(
                        out=pbuf[:, 0:512], in_=ps[4 * sset][:], func=AF.Exp, scale=SCALE), waits=[full, pfw])
                    a = P.op("act", lambda e, pbuf=pbuf, sset=sset: e.activation(
                        out=pbuf[:, 512:1024], in_=ps[4 * sset + 1][:], func=AF.Exp, scale=SCALE), inc=S["act"])
                    sp_ring.release(sset, a)
                    mi, mbuf, mfw = pT2.acquire()
                    meng = "dve"
                    pl = P.op(meng, lambda e, mbuf=mbuf, pbuf=pbuf, ebi=ebi: e.tensor_tensor(
                        out=mbuf[:].rearrange("p (b c) -> p b c", b=4),
                        in0=pbuf[:].rearrange("p (b c) -> p b c", b=4), in1=EB[:, ebi, :, :], op=ALU.mult),
                        waits=[a, mfw, eb_done], inc=S[meng])
                    pT.release(pi, pl)
                    state[gi] = (mi, mbuf, pl)

                def p2(gi):
                    blocks, sel = groups[gi]
                    mi, mbuf, pl = state.pop(gi)
                    _, oset, ofw = o_ring.acquire()
                    po, pz = ps[4 * oset + 2], ps[4 * oset + 3]
                    vt = vd[si][d]
                    full = None
                    first = True
                    for b, (r, n) in enumerate(blocks):
                        j = r * nb + n
                        for tgt, lfn in ((po, lambda jj: vt[:, jj, :]), (pz, lambda jj: ones[:])):
                            oa = tgt[:, b * 128:(b + 1) * 128]
                            if n > 0:
                                P.op("pe", lambda e, oa=oa, l=lfn(j - 1), b=b: e.matmul(
                                    oa, lhsT=l, rhs=mbuf[:, b * 256:b * 256 + 128], start=True, stop=False),
                                    waits=[pl, ofw] if first else ())
                                first = False
                            fn = lambda e, oa=oa, l=lfn(j), b=b, n=n: e.matmul(
                                oa, lhsT=l, rhs=mbuf[:, b * 256 + 128:b * 256 + 256], start=(n == 0), stop=True)
                            if b == 3 and tgt is pz:
                                full = P.op("pe", fn, waits=[pl, ofw] if first else (), inc=S["ofull"])
                            else:
                                P.op("pe", fn, waits=[pl, ofw] if first else ())
                            first = False
                    pT2.release(mi, full)
                    if sel[0] == "pair":
                        r0 = sel[1]
                        ao = acc_o[:].rearrange("p (m r) -> p r m", r=16)[:, r0:r0 + 2, :]
                        az = acc_z[:].rearrange("p (m r) -> p r m", r=16)[:, r0:r0 + 2, :]
                        pov = po[:].rearrange("p (a m) -> p a m", a=2)
                        pzv = pz[:].rearrange("p (a m) -> p a m", a=2)
                    else:
                        _, r, n0 = sel
                        ao = acc_o[:].rearrange("p (m r) -> p r m", r=d)[:, r, n0 * 128:n0 * 128 + 512]
                        az = acc_z[:].rearrange("p (m r) -> p r m", r=d)[:, r, n0 * 128:n0 * 128 + 512]
                        pov, pzv = po[:], pz[:]
                    if d == 1:
                        P.op("dve", lambda e, ao=ao, pov=pov: e.tensor_copy(out=ao, in_=pov),
                             waits=[full, fin_prev])
                        dv = P.op("dve", lambda e, az=az, pzv=pzv: e.tensor_copy(out=az, in_=pzv), inc=S["dve"])
                    else:
                        P.op("dve", lambda e, ao=ao, pov=pov: e.tensor_tensor(out=ao, in0=ao, in1=pov, op=ALU.add),
                             waits=[full])
                        dv = P.op("dve", lambda e, az=az, pzv=pzv: e.tensor_tensor(out=az, in0=az, in1=pzv,
                                                                                   op=ALU.add), inc=S["dve"])
                    o_ring.release(oset, dv)
                    return full

                ng = len(groups)
                p1(0)
                for gi in range(ng):
                    if gi + 1 < ng:
                        p1(gi + 1)
                    last_pe_head = p2(gi)
                    if di == 0 and gi == 5 and h > 0 and "ag_a" in io:
                        prog_allgather(P, io["cc"], [io["ag_a"][h - 1]], [a_st])
            inring.release(ri, last_pe_head)
            P.op("dve", lambda e: e.reciprocal(out=acc_z[:], in_=acc_z[:]))
            fin = P.op("dve", lambda e: e.tensor_tensor(out=aT[:], in0=acc_o[:], in1=acc_z[:], op=ALU.mult),
                       waits=[a_st], inc=S["fin"])
            asq = P.op("act", lambda e: e.activation(out=sqh[:], in_=aT[:], func=AF.Square), waits=[fin],
                       inc=S["act"])
            a2 = None
            for qq in range(4):
                _, oset, ofw = o_ring.acquire()
                po, pz = ps[4 * oset + 2], ps[4 * oset + 3]
                P.op("pe", lambda e, po=po, qq=qq: e.matmul(po[:], lhsT=ones[:], rhs=sqh[:, qq * 1024:qq * 1024 + 512],
                                                            start=True, stop=True), waits=[asq, ofw])
                full = P.op("pe", lambda e, pz=pz, qq=qq: e.matmul(
                    pz[:], lhsT=ones[:], rhs=sqh[:, qq * 1024 + 512:qq * 1024 + 1024], start=True, stop=True),
                    inc=S["ofull"])
                P.op("act", lambda e, po=po, qq=qq: e.activation(
                    out=acc_z[:, qq * 1024:qq * 1024 + 512], in_=po[:], func=AF.Sqrt, bias=EPS, scale=1.0 / 128),
                    waits=[full])
                a2 = P.op("act", lambda e, pz=pz, qq=qq: e.activation(
                    out=acc_z[:, qq * 1024 + 512:qq * 1024 + 1024], in_=pz[:], func=AF.Sqrt, bias=EPS,
                    scale=1.0 / 128), inc=S["act"])
                o_ring.release(oset, a2)
            P.op("dve", lambda e: e.reciprocal(out=acc_z[:], in_=acc_z[:]), waits=[a2, c1])
            fin2 = P.op("dve", lambda e, h=h: e.scalar_tensor_tensor(
                out=aT[:], in0=aT[:], scalar=mixgB[:, h:h + 1], in1=acc_z[:], op0=ALU.mult, op1=ALU.mult),
                inc=S["fin"])
            a_st = P.op("sp", lambda e, h=h: e.dma_start(out=io["a"](h), in_=aT[:].rearrange(
                "p (tb t) -> p tb t", tb=4)), waits=[fin2], inc=S["st"])
            fin_prev = None
        if "ag_a" in io:
            prog_allgather(P, io["cc"], [io["ag_a"][3]], [a_st])
        P.wait("sp", (S["st"], S["st"].n))
        P.run(nc, dyn)


class XPipe:
    def __init__(self, P, xs, x_src, order, st_hist):
        self.P, self.xs, self.x_src, self.order, self.st_hist = P, xs, x_src, order, st_hist
        self.nxt = 0
        self.loaded = {}

    def issue(self):
        if self.nxt >= len(self.order):
            return
        ch = self.order[self.nxt]
        self.nxt += 1
        i, xb, fw = self.xs.acquire()
        src = self.x_src(ch)
        xl = self.P.op("sp", lambda e, xb=xb, src=src: e.dma_start(out=xb[:], in_=src),
                       waits=[fw, self.st_hist.get(ch)], inc=self.xs.sems[i])
        self.loaded[ch] = (i, xb, xl)


def ssq_flush(P, S, ssq, keep):
    while len(ssq["pend"]) > keep:
        ch, j, sqb, a = ssq["pend"].pop(0)
        w0 = [a] + (list(ssq["bank_free"]) if ch == 0 else [])
        P.op("pe", lambda e, sqb=sqb, ch=ch: e.matmul(ssq["psA"][:], lhsT=ssq["ones"][:], rhs=sqb[:, 0:512],
                                                      start=(ch == 0), stop=(ch == KC - 1)), waits=w0)
        pe = P.op("pe", lambda e, sqb=sqb, ch=ch: e.matmul(ssq["psB"][:], lhsT=ssq["ones"][:], rhs=sqb[:, 512:1024],
                                                           start=(ch == 0), stop=(ch == KC - 1)), inc=S["pe"])
        ssq["sq"].release(j, pe)
        ssq["last_pe"] = pe


def ssq_finish(P, S, ssq, rstd):
    ssq_flush(P, S, ssq, 0)
    P.op("act", lambda e: e.activation(out=rstd[:, 0:512], in_=ssq["psA"][:], func=AF.Sqrt, bias=EPS,
                                       scale=1.0 / D), waits=[ssq["last_pe"]])
    a = P.op("act", lambda e: e.activation(out=rstd[:, 512:1024], in_=ssq["psB"][:], func=AF.Sqrt, bias=EPS,
                                           scale=1.0 / D), inc=S["act"])
    return P.op("dve", lambda e: e.reciprocal(out=rstd[:], in_=rstd[:]), waits=[a], inc=S["dve"])


def emit_resid_block(P, S, psr, xp, xo, wb, wld, b, nk, rhs_fn, x_dst, ready, st_hist, halo_dst=None, ssq=None):
    blk_done = None
    for j in range(2):
        ch = 2 * b + j
        xi, xb, xl = xp.loaded.pop(ch)
        oi, ob, ofw = xo.acquire()
        if ssq is not None:
            P.wait("dve", ssq["ob_read"].get(oi))
        dv = None
        for th in range(2):
            bi, pb, pfw = psr.acquire()
            full = mm_group(P, S, pb, 512, lambda k, wb=wb, j=j: wb[:, k, j * 128:(j + 1) * 128],
                            lambda k, th=th: rhs_fn(k, th), nk, [wld, pfw] + list(ready))
            dv = P.op("dve", lambda e, ob=ob, pb=pb, xb=xb, th=th: e.tensor_tensor(
                out=ob[:, th * 512:(th + 1) * 512], in0=pb[:], in1=xb[:, th * 512:(th + 1) * 512],
                op=ALU.add), waits=[full, xl, ofw], inc=S["dve"])
            psr.release(bi, dv)
            blk_done = full
        xp.xs.release(xi, dv)
        dst = x_dst(ch)
        if halo_dst is not None:
            hd = halo_dst(ch)
            P.op("sp", lambda e, ob=ob, hd=hd: e.dma_start(out=hd, in_=ob[:, T - 2:T]), waits=[dv],
                 inc=xo.sems[oi])
        st = P.op("sp", lambda e, ob=ob, dst=dst: e.dma_start(out=dst, in_=ob[:]), waits=[dv],
                  inc=xo.sems[oi])
        st_hist[ch] = st
        xo.release(oi, st)
        xp.issue()
        if ssq is not None:
            jj, sqb, sfw = ssq["sq"].acquire()
            a = P.op("act", lambda e, sqb=sqb, ob=ob: e.activation(out=sqb[:], in_=ob[:], func=AF.Square),
                     waits=[dv, sfw], inc=S["act"])
            ssq["ob_read"][oi] = a
            ssq_flush(P, S, ssq, 0)
            ssq["pend"].append((ch, jj, sqb, a))
    return blk_done


def phase_C1(nc, io, dyn=None):
    P = Prog()
    _UID[0] += 1
    with ExitStack() as es:
        S = mk_sems(nc, es, "C1")
        S["ld2"] = Sem(nc, es, "C1_ld2", 16)
        mT = sbt(nc, es, "mT", [128, KC, T], BF16)
        wbufs = [sbt(nc, es, f"wc{i}", [128, 16, 256], BF16) for i in range(4)]
        xs = Ring([sbt(nc, es, f"xc{i}", [128, T], F32) for i in range(3)], dsems(nc, es, "C1_xl", 3))
        xo = Ring([sbt(nc, es, f"xo{i}", [128, T], F32) for i in range(2)], dsems(nc, es, "C1_xs", 2))
        ps = [es.enter_context(nc.psum_tensor(f"psC{i}_{_UID[0]}", [128, 512], F32)) for i in range(8)]
        psr = Ring(ps[0:6])
        ones = sbt(nc, es, "onesC", [128, 128], BF16)
        rstd = sbt(nc, es, "rstdC", [128, T], F32)
        sq = Ring([sbt(nc, es, f"sqc{i}", [128, T], BF16) for i in range(2)])
        ones_ok = P.op("pool", lambda e: e.memset(ones[:], 1.0), inc=S["pool"])
        ssq = {"sq": sq, "psA": ps[6], "psB": ps[7], "ones": ones, "pend": [], "ob_read": {},
               "bank_free": [ones_ok]}

        ld_g = None
        for c in range(16):
            ld_g = P.op("sp", lambda e, c=c: e.dma_start(out=mT[:, 16 + c, :], in_=io["gT"][c]), inc=S["ld"])
        st_hist = {}
        ws = WStream(P, S, wbufs, dsems(nc, es, "C1_wl", 4))
        sched = [(0, b) for b in range(16)] + [(1, b) for b in range(16)]
        pending = [ws.load(io["w_out"][hf][b], 16) for (hf, b) in sched[:4]]
        nxt = 4
        xp = XPipe(P, xs, lambda ch: io["xT"][ch], list(range(KC)), st_hist)
        for _ in range(3):
            xp.issue()
        ld_a = None
        for (hf, b) in sched:
            if hf == 1 and b == 0:
                if "cc" in io:
                    P.wait("sp", (io["cc"], io["cc"].n))
                pre_done = None
                for (dst, srcf) in io.get("pre", []):
                    pre_done = P.op("sp", lambda e, dst=dst, srcf=srcf: e.dma_start(out=dst, in_=srcf()),
                                    inc=S["cld"])
                P.wait("sp", pre_done)
                for c in range(16):
                    ld_a = P.op("sp", lambda e, c=c: e.dma_start(out=mT[:, c, :], in_=io["aT"](c)), inc=S["ld2"])
                xp = XPipe(P, xs, lambda ch: io["xo"][ch], list(range(KC)), st_hist)
                for _ in range(3):
                    xp.issue()
            wi, wb, wld = pending.pop(0)
            if hf == 0:
                blk_done = emit_resid_block(P, S, psr, xp, xo, wb, wld, b, 16,
                                            lambda k, th: mT[:, 16 + k, th * 512:(th + 1) * 512],
                                            lambda ch: io["xo"][ch], [ld_g], st_hist)
            else:
                blk_done = emit_resid_block(P, S, psr, xp, xo, wb, wld, b, 16,
                                            lambda k, th: mT[:, k, th * 512:(th + 1) * 512],
                                            lambda ch: io["xo"][ch], [ld_a], st_hist,
                                            halo_dst=io.get("halo_dst"), ssq=ssq if "rstd_out" in io else None)
            ws.release(wi, blk_done)
            if nxt < len(sched):
                pending.append(ws.load(io["w_out"][sched[nxt][0]][sched[nxt][1]], 16))
                nxt += 1
        if "rstd_out" in io:
            rd = ssq_finish(P, S, ssq, rstd)
            P.op("sp", lambda e: e.dma_start(out=io["rstd_out"][:], in_=rstd[:]), waits=[rd], inc=S["st"])
            P.wait("sp", (S["st"], S["st"].n))
        for w in xo.all_done():
            P.wait("sp", w)
        if "ag_xh" in io:
            prog_allgather(P, io["cc"], io["ag_xh"], xo.all_done())
        P.run(nc, dyn)


def phase_C2(nc, io, final, dyn=None):
    P = Prog()
    _UID[0] += 1
    with ExitStack() as es:
        S = mk_sems(nc, es, "C2")
        S["act2"] = Sem(nc, es, "C2_act2", 1)
        S["dve2"] = Sem(nc, es, "C2_dve2", 1)
        hT = sbt(nc, es, "h2T", [128, KC, T + 2], BF16)
        gsl = sbt(nc, es, "gsl", [128, 22, T], BF16)
        wbufs = [sbt(nc, es, f"wf{i}", [128, KC, 256], BF16) for i in range(2)]
        xs = Ring([sbt(nc, es, f"xf{i}", [128, T], F32) for i in range(2)], dsems(nc, es, "C2_xl", 2))
        xo = Ring([sbt(nc, es, f"xg{i}", [128, T], F32) for i in range(2)], dsems(nc, es, "C2_xs", 2))
        sq = Ring([sbt(nc, es, f"sqf{i}", [128, T], BF16) for i in range(2)])
        rstd = sbt(nc, es, "rstdf", [128, T], F32)
        ones = sbt(nc, es, "onesF", [128, 128], BF16)
        g2 = sbt(nc, es, "g2s", [128, KC], F32)
        gf = sbt(nc, es, "gfs", [128, KC], F32)
        cw = sbt(nc, es, "cws", [128, 172, 3], F32)
        cb = sbt(nc, es, "cbs", [128, 172], F32)
        xh = sbt(nc, es, "xhs", [128, KC, 2], F32)
        xh2 = sbt(nc, es, "xh2", [128, KC, 2], BF16)
        rh = sbt(nc, es, "rh", [128, 2], F32)
        U = Ring([(sbt(nc, es, f"Ug{i}", [128, T + 2], F32), sbt(nc, es, f"Uv{i}", [128, T + 2], F32))
                  for i in range(2)])
        cc = Ring([(sbt(nc, es, f"cg{i}", [128, 512], F32), sbt(nc, es, f"cv{i}", [128, 512], F32))
                   for i in range(2)])
        sgr = Ring([sbt(nc, es, f"sg{i}", [128, 512], F32) for i in range(2)])
        ps = [es.enter_context(nc.psum_tensor(f"psF{i}_{_UID[0]}", [128, 512], F32)) for i in range(8)]
        psr = Ring(ps)
        ph = ps[7]

        c1 = None
        for dst, src in ((g2, "g2"), (cw, "cw"), (cb, "cb"), (xh, "xh")) + (((gf, "gf"),) if final else ()):
            if callable(io[src]):
                if "cc" in io:
                    P.wait("sp", (io["cc"], io["cc"].n))
                c1 = P.op("sp", lambda e, dst=dst, src=src: e.dma_start(
                    out=dst[:].rearrange("p k t -> p (k t)"), in_=io[src]()), inc=S["cld"])
            else:
                c1 = P.op("sp", lambda e, dst=dst, src=src: e.dma_start(out=dst[:], in_=io[src][:]), inc=S["cld"])
        ones_ok = P.op("pool", lambda e: e.memset(ones[:], 1.0), inc=S["pool"])

        a = P.op("act", lambda e: e.activation(out=xh2[:], in_=xh[:], func=AF.Square), waits=[c1], inc=S["act"])
        full = mm_group(P, S, ph, 2, lambda k: ones[:], lambda k: xh2[:, k, :], KC, [a, ones_ok])
        a = P.op("act", lambda e: e.activation(out=rh[:], in_=ph[:, 0:2], func=AF.Sqrt, bias=EPS, scale=1.0 / D),
                 waits=[full], inc=S["act"])
        ph_free = a
        P.op("dve", lambda e: e.reciprocal(out=rh[:], in_=rh[:]), waits=[a])
        hh_done = None
        for c in range(KC):
            hh_done = P.op("dve", lambda e, c=c: e.scalar_tensor_tensor(
                out=hT[:, c, 0:2], in0=xh[:, c, :], scalar=g2[:, c:c + 1], in1=rh[:], op0=ALU.mult, op1=ALU.mult),
                inc=S["dve"])

        rr_ = None
        if "rstd_in" in io:
            rr_ = P.op("sp", lambda e: e.dma_start(out=rstd[:], in_=io["rstd_in"][:]), inc=S["ld"])
        h_done, ps_rel = emit_rmsnorm(P, S, lambda c: io["xT"][c], g2, xs, sq, ones, rstd, ps[0], ps[1],
                                      [ones_ok], lambda c: hT[:, c, 2:T + 2], rstd_ready=rr_)
        psr.free_at[0] = ps_rel
        psr.free_at[1] = ps_rel
        psr.free_at[7] = ph_free
        ssq = None

        ws = WStream(P, S, wbufs, dsems(nc, es, "C2_wl", 2))
        sched = []
        for s, (k0, nk) in enumerate(SLABS):
            for jj in range(nk):
                sched.append(("up", s, jj, io["w_up"][k0 + jj], KC))
            for b in range(16):
                sched.append(("dn", s, b, io[f"w_dn{s}"][b], nk))
        pending = [ws.load(sched[i][3], sched[i][4]) for i in range(2)]
        nxt_load = 2
        st_hist = {}
        stage_b = []
        g_done = None
        xp = None

        def emit_stage_b():
            (cgb, cvb, ci, jj, th, dvc) = stage_b.pop(0)
            gi_, sgb, gfw = sgr.acquire()
            a2 = P.op("act", lambda e, sgb=sgb, cgb=cgb: e.activation(out=sgb[:], in_=cgb[:], func=AF.Silu),
                      waits=[dvc, gfw], inc=S["act2"])
            d2 = P.op("dve", lambda e, sgb=sgb, cvb=cvb, jj=jj, th=th: e.tensor_tensor(
                out=gsl[:, jj, th * 512:(th + 1) * 512], in0=sgb[:], in1=cvb[:], op=ALU.mult),
                waits=[a2], inc=S["dve2"])
            sgr.release(gi_, d2)
            cc.release(ci, d2)
            return d2

        for it_, (kind, s, idx, src, nkc) in enumerate(sched):
            wi, wb, wld = pending.pop(0)
            k0, nk = SLABS[s]
            blk_done = None
            if kind == "up":
                jj = idx
                cg_, cv_ = k0 + jj, FC + k0 + jj
                ui, (Ug, Uv), ufw = U.acquire()
                TW = (T + 2) // 3
                a = None
                for t3 in range(3):
                    for (ub, c0_) in ((Ug, 0), (Uv, 128)):
                        bi_, pb_, pw_ = psr.acquire()
                        f_ = mm_group(P, S, pb_, TW, lambda k, wb=wb, c0_=c0_: wb[:, k, c0_:c0_ + 128],
                                      lambda k, t3=t3: hT[:, k, t3 * TW:(t3 + 1) * TW], KC,
                                      [wld, pw_, h_done, hh_done])
                        blk_done = f_
                        a = P.op("act", lambda e, ub=ub, pb_=pb_, t3=t3: e.activation(
                            out=ub[:, t3 * TW:(t3 + 1) * TW], in_=pb_[:, 0:TW], func=AF.Copy),
                            waits=[f_, ufw], inc=S["act"])
                        psr.release(bi_, a)
                for th in range(2):
                    ci, (cgb, cvb), cfw = cc.acquire()
                    lo = 2 + th * 512
                    P.op("act", lambda e, cgb=cgb, Ug=Ug, c=cg_, lo=lo: e.activation(
                        out=cgb[:], in_=Ug[:, lo:lo + 512], func=AF.Identity, bias=cb[:, c:c + 1],
                        scale=cw[:, c, 2:3]), waits=[cfw, c1])
                    a = P.op("act", lambda e, cvb=cvb, Uv=Uv, c=cv_, lo=lo: e.activation(
                        out=cvb[:], in_=Uv[:, lo:lo + 512], func=AF.Identity, bias=cb[:, c:c + 1],
                        scale=cw[:, c, 2:3]), inc=S["act"])
                    dvc = None
                    for (ub, cbuf, c) in ((Ug, cgb, cg_), (Uv, cvb, cv_)):
                        P.op("dve", lambda e, ub=ub, cbuf=cbuf, c=c, lo=lo: e.scalar_tensor_tensor(
                            out=cbuf[:], in0=ub[:, lo - 1:lo + 511], scalar=cw[:, c, 1:2], in1=cbuf[:],
                            op0=ALU.mult, op1=ALU.add), waits=[a])
                        dvc = P.op("dve", lambda e, ub=ub, cbuf=cbuf, c=c, lo=lo: e.scalar_tensor_tensor(
                            out=cbuf[:], in0=ub[:, lo - 2:lo + 510], scalar=cw[:, c, 0:1], in1=cbuf[:],
                            op0=ALU.mult, op1=ALU.add), inc=S["dve"])
                    if th == 1:
                        U.release(ui, dvc)
                    stage_b.append((cgb, cvb, ci, jj, th, dvc))
                    if len(stage_b) > 1:
                        emit_stage_b()
                if jj == nk - 1:
                    while stage_b:
                        g_done = emit_stage_b()
                    xsrc = (lambda ch: io["xT"][ch]) if s == 0 else (lambda ch: io["xw"][ch])
                    xp = XPipe(P, xs, xsrc, list(range(KC)), st_hist)
                    xp.issue()
                    xp.issue()
            else:
                if s == len(SLABS) - 1 and idx == 0 and (final or "rstd_out" in io):
                    old = psr
                    psr = Ring(ps[0:6])
                    psr.free_at = list(old.free_at[0:6])
                    ssq = {"sq": sq, "psA": ps[6], "psB": ps[7], "ones": ones, "pend": [], "ob_read": {},
                           "bank_free": [old.free_at[6], old.free_at[7], h_done]}
                blk_done = emit_resid_block(P, S, psr, xp, xo, wb, wld, idx, nk,
                                            lambda k, th: gsl[:, k, th * 512:(th + 1) * 512],
                                            lambda ch: io["xw"][ch], [g_done], st_hist, ssq=ssq)
            ws.release(wi, blk_done)
            if nxt_load < len(sched):
                pending.append(ws.load(sched[nxt_load][3], sched[nxt_load][4]))
                nxt_load += 1
        rd = None
        if ssq is not None:
            rd = ssq_finish(P, S, ssq, rstd)
            if "rstd_out" in io:
                P.op("sp", lambda e: e.dma_start(out=io["rstd_out"][:], in_=rstd[:]), waits=[rd], inc=S["st"])
                P.wait("sp", (S["st"], S["st"].n))
        if final:
            for w in xo.all_done():
                P.wait("sp", w)
            P.wait("dve", rd)
            for c in range(KC):
                i, xb, fw = xs.acquire()
                ld = P.op("sp", lambda e, xb=xb, c=c: e.dma_start(out=xb[:], in_=io["xw"][c]), waits=[fw],
                          inc=xs.sems[i])
                oi, ob, ofw = xo.acquire()
                dv = P.op("dve", lambda e, xb=xb, ob=ob, c=c: e.scalar_tensor_tensor(
                    out=ob[:], in0=xb[:], scalar=gf[:, c:c + 1], in1=rstd[:], op0=ALU.mult, op1=ALU.mult),
                    waits=[ld, ofw, ssq["ob_read"].get(oi)], inc=S["dve"])
                xs.release(i, dv)
                st = P.op("sp", lambda e, ob=ob, c=c: e.dma_start(out=io["out"][c], in_=ob[:]), waits=[dv],
                          inc=xo.sems[oi])
                xo.release(oi, st)
        for w in xo.all_done():
            P.wait("sp", w)
        P.run(nc, dyn)


GROUPS = [[0, 1, 2, 3], [4, 5, 6, 7]]


def _blocks(w, order=None):
    K, N = w.shape
    a = w.reshape(K // 128, 128, N // 256, 256).transpose(2, 1, 0, 3)
    if order is not None:
        a = a[order]
    return np.ascontiguousarray(a)


def _fm(v):
    return np.ascontiguousarray(v.reshape(-1, 128).T)


def _dram(nc, name, shape, dt, kind):
    return nc.dram_tensor(name, list(shape), dt, kind=kind).ap()


def prog_allgather(P, cc, pairs, waits):
    r = None
    for i, (src, dst) in enumerate(pairs):
        r = P.op("pool", lambda e, src=src, dst=dst: e.collective_compute(
            "AllGather", ALU.bypass, replica_groups=GROUPS, ins=[src], outs=[dst]),
            waits=waits if i == 0 else (), inc=cc)
    return r


def emit_allgather(nc, cc, pairs):
    with nc.Block() as block:
        @block.gpsimd
        def _(g):
            for (src, dst) in pairs:
                g.collective_compute("AllGather", ALU.bypass, replica_groups=GROUPS, ins=[src], outs=[dst]
                                     ).then_inc(cc.h, 1)
                cc.n += 1
            g.wait_ge(cc.h, cc.n)


def build_fused(L=2, stop=99):
    _POOL.clear()
    nc = bass.Bass("TRN2", target_bir_lowering=False, num_devices=NCORES)
    EI, IN = "ExternalInput", "Internal"
    dyn = {}
    xT = _dram(nc, "xT", [KC, 128, T], F32, EI)
    nsl = _dram(nc, "nsl", [128, 12], F32, EI)
    gf = _dram(nc, "gf", [128, KC], F32, EI)
    out = _dram(nc, "out", [KC, 128, T], F32, "ExternalOutput")
    xA = _dram(nc, "xA", [KC, 128, T], F32, IN)
    xB = _dram(nc, "xB", [KC, 128, T], F32, IN)
    qk_send = _dram(nc, "qk_send", [32, 128, T], BF16, IN)
    v_send2 = _dram(nc, "v_send2", [4, T, 512], BF16, IN)
    gT = _dram(nc, "gTd", [16, 128, T], BF16, IN)
    qk_all = _dram(nc, "qk_all", [4 * 32 * 128, T], BF16, IN)
    v_all2 = _dram(nc, "v_all2", [4, 4 * T, 512], BF16, IN)
    a_send3 = _dram(nc, "a_send3", [4, 128, 4096], BF16, IN)
    a_all3 = _dram(nc, "a_all3", [4, 4, 128, 4096], BF16, IN)
    xh_send = _dram(nc, "xh_send", [128, 2 * KC], F32, IN)
    xh_ext = _dram(nc, "xh_ext", [5 * 128, 2 * KC], F32, IN)
    my_qk2 = _dram(nc, "my_qk2", [2, 4, 4, 128, T], BF16, IN)
    my_v = _dram(nc, "my_v", [4096, 512], BF16, IN)
    my_a = _dram(nc, "my_a", [16, 128, T], BF16, IN)
    rstd_c1 = _dram(nc, "rstd_c1", [128, T], F32, IN)
    rstd_c2 = _dram(nc, "rstd_c2", [128, T], F32, IN)
    cc = Sem(nc, None, "cc", 1)

    qk_s2 = qk_send.rearrange("c p t -> (c p) t")
    ag_qkv = [(qk_s2[j * 512:(j + 1) * 512, :], qk_all[j * 2048:(j + 1) * 2048, :]) for j in range(8)]
    ag_qkv += [(v_send2[g], v_all2[g]) for g in range(4)]
    ag_a = [(a_send3[h], a_all3[h].rearrange("rr p t -> (rr p) t")) for h in range(4)]
    xh5 = xh_ext.rearrange("(b p) c -> b p c", p=128)

    with nc.Block() as block0:
        @block0.sync
        def _(sp):
            dyn["r"] = sp.snap(sp.partition_id() % 4, min_val=0, max_val=3)
            dyn["r2"] = sp.snap(dyn["r"] * 2, min_val=0, max_val=6)

    x_in = xT
    for l in range(L):
        wl = [("g1", [128, KC]), ("w_in", [40, 128, KC, 256]), ("normg", [128, 2048]), ("wsT", [128, 16, 128]),
              ("brow", [1, 2048])]
        wl += [("mixgA", [128, 16]), ("mixgB", [128, 4])]
        if stop >= 5:
            wl += [("w_out", [2, 16, 128, 16, 256])]
        if stop >= 7:
            wl += [("g2", [128, KC]), ("w_up", [FC, 128, KC, 256]), ("cw", [128, 172, 3]), ("cb", [128, 172])]
        W = {k: _dram(nc, f"{k}_{l}", shp, F32, EI) for k, shp in wl}
        if stop >= 7:
            for s_, (k0, nk) in enumerate(SLABS):
                W[f"w_dn{s_}"] = _dram(nc, f"w_dn{s_}_{l}", [16, 128, nk, 256], F32, EI)
        ioA = {"xT": x_in, "g1": W["g1"], "w_in": W["w_in"], "normg": W["normg"], "wsT": W["wsT"],
               "brow": W["brow"], "qk": qk_send, "gT": gT, "mixgA": W["mixgA"],
               "v": lambda cb: v_send2[cb // 2].rearrange("(tc p) c -> p tc c", p=128)[
                   :, :, (cb % 2) * 256:(cb % 2) * 256 + 256]}
        if l == 0:
            ioA["zero_dst"] = xh_ext[0:128, :]
        else:
            ioA["rstd_in"] = rstd_c2
        ioA.update({"cc": cc, "ag_qk": ag_qkv[0:8], "ag_v": ag_qkv[8:12]})
        phase_A(nc, ioA, dyn)
        if stop < 2:
            break
        ioB = {
            "pre": [
                (my_qk2.rearrange("jj rr c p t -> jj (rr c p) t"),
                 lambda: qk_all.rearrange("(j x) t -> j x t", j=8)[bass.ds(dyn["r2"], 2)]),
                (my_v[:], lambda: v_all2[bass.ds(dyn["r"], 1)].rearrange("a t c -> (a t) c")),
            ],
            "q": lambda h: my_qk2[0][:, h].rearrange("rr p t -> p rr t"),
            "k": lambda h: my_qk2[1][:, h].rearrange("rr p t -> p rr t"),
            "v": lambda h, d, r: my_v[:, h * 128:(h + 1) * 128].rearrange("(n p r) e -> r p n e", p=128, r=d)[r],
            "nsl": nsl, "a": lambda h: a_send3[h].rearrange("p (tb t) -> p tb t", tb=4),
            "cc": cc, "ag_a": ag_a, "mixgB": W["mixgB"],
        }
        if stop < 3:
            break
        phase_B(nc, ioB, dyn)
        if stop < 4:
            break
        ioC1 = {"xT": x_in,
                "pre": [(my_a.rearrange("h p t -> (h p) t"),
                         lambda: a_all3.rearrange("h rr p (tb t) -> tb (h rr p) t", tb=4)[
                             bass.ds(dyn["r"], 1)].rearrange("a x t -> (a x) t"))],
                "aT": lambda c: my_a[(c % 4) * 4 + c // 4],
                "cc": cc, "ag_xh": [(xh_send[:], xh_ext[128:640, :])],
                "gT": gT, "w_out": W["w_out"], "xo": xA, "rstd_out": rstd_c1,
                "halo_dst": lambda ch: xh_send[:, 2 * ch:2 * ch + 2]}
        if stop < 5:
            break
        phase_C1(nc, ioC1, dyn)
        if stop < 6:
            break
        if stop < 7:
            break
        final = (l == L - 1)
        ioC2 = {"xT": xA, "xh": lambda: xh5[bass.ds(dyn["r"], 1)].rearrange("a p c -> p (a c)"),
                "g2": W["g2"], "w_up": W["w_up"], "cw": W["cw"], "cb": W["cb"], "xw": xB, "cc": cc,
                "rstd_in": rstd_c1}
        if not final:
            ioC2["rstd_out"] = rstd_c2
        for s_ in range(4):
            ioC2[f"w_dn{s_}"] = W[f"w_dn{s_}"]
        if final:
            ioC2["gf"] = gf
            ioC2["out"] = out
        phase_C2(nc, ioC2, final, dyn)
        x_in = xB
    return nc


def _to_xT(x):
    out = []
    for c in range(NCORES):
        b, r = divmod(c, 4)
        xs = x[b, r * T:(r + 1) * T, :]
        out.append(np.ascontiguousarray(xs.T.reshape(KC, 128, T)))
    return out


def kernel(x, attn_norm_g, w_in, sgu_norm_g, w_spatial, b_spatial, mix_norm_g, w_out,
           ffn_norm_g, w_up, conv_w, conv_b, w_down, final_norm_g):
    f = lambda a: np.asarray(a, dtype=np.float32)
    L = 2
    xT = _to_xT(f(x))
    slopes = 2.0 ** (-8.0 * np.arange(1, 17, dtype=np.float64) / 16.0)
    shared = {"gf": _fm(f(final_norm_g))}
    percore = {}
    order = list(range(24)) + list(range(32, 40)) + list(range(24, 32))
    for l in range(L):
        shared[f"g1_{l}"] = _fm(f(attn_norm_g[l]))
        shared[f"w_in_{l}"] = _blocks(f(w_in[l]), order)
        shared[f"normg_{l}"] = np.ascontiguousarray(np.broadcast_to(f(sgu_norm_g[l])[None, :], (128, 2048)))
        shared[f"wsT_{l}"] = np.ascontiguousarray(f(w_spatial[l]).transpose(2, 0, 1))
        shared[f"brow_{l}"] = np.ascontiguousarray(f(b_spatial[l]).reshape(1, 2048))
        mg = _fm(f(mix_norm_g[l]))
        shared[f"mixgA_{l}"] = np.ascontiguousarray(mg[:, 16:32])
        for hg in range(4):
            percore.setdefault(hg, {})[f"mixgB_{l}"] = np.ascontiguousarray(mg[:, 4 * hg:4 * hg + 4])
        wo = f(w_out[l])
        shared[f"w_out_{l}"] = np.stack([_blocks(wo[2048:]), _blocks(wo[:2048])], 0)
        shared[f"g2_{l}"] = _fm(f(ffn_norm_g[l]))
        wu = f(w_up[l])
        shared[f"w_up_{l}"] = np.ascontiguousarray(
            wu.reshape(KC, 128, 2, FC, 128).transpose(3, 1, 0, 2, 4).reshape(FC, 128, KC, 256))
        wd = f(w_down[l])
        for s_, (k0, nk) in enumerate(SLABS):
            shared[f"w_dn{s_}_{l}"] = np.ascontiguousarray(
                wd[k0 * 128:(k0 + nk) * 128, :].reshape(nk, 128, 16, 256).transpose(2, 1, 0, 3))
        shared[f"cw_{l}"] = np.ascontiguousarray(f(conv_w[l]).reshape(3, 172, 128).transpose(2, 1, 0))
        shared[f"cb_{l}"] = np.ascontiguousarray(f(conv_b[l]).reshape(172, 128).T)
    in_maps = []
    for c in range(NCORES):
        hg = c % 4
        nsl = np.zeros((128, 12), np.float32)
        for hl in range(4):
            for di, d in enumerate((1, 4, 16)):
                nsl[:, hl * 3 + di] = np.float32(-slopes[4 * hg + hl] * d)
        m = dict(shared)
        m.update(percore[hg])
        m["xT"] = xT[c]
        m["nsl"] = nsl
        in_maps.append(m)
    nc = build_fused(L)
    res = run_bass_kernel_spmd(nc, in_maps, core_ids=list(range(NCORES))).results
    y = np.empty((2, 4096, D), np.float32)
    for c in range(NCORES):
        b, r = divmod(c, 4)
        y[b, r * T:(r + 1) * T, :] = np.asarray(res[c]["out"]).reshape(D, T).T
    return y
```

```python
import numpy as np
import ml_dtypes
from contextlib import ExitStack
import concourse.bass as bass
import concourse.mybir as mybir
from concourse.bass_utils import run_bass_kernel_spmd

F32 = mybir.dt.float32
BF16 = mybir.dt.bfloat16
I32 = mybir.dt.int32
AF = mybir.ActivationFunctionType
ALU = mybir.AluOpType
NPBF = ml_dtypes.bfloat16

NCORES = 8
T = 1024
D = 4096
KC = 32
DFF = 11008
FC = 86
SLABS = [(0, 22), (22, 22), (44, 21), (65, 21)]
EPS = 1e-6
SCALE = 128.0 ** -0.5


_POOL = {}
_UID = [0]


class Sem:
    def __new__(cls, nc, es, name, step=1):
        key = (id(nc), name)
        if key in _POOL:
            return _POOL[key]
        o = object.__new__(cls)
        o.h = nc.alloc_semaphore(name=name)
        o.step = step
        o.n = 0
        _POOL[key] = o
        return o

    def __init__(self, nc, es, name, step=1):
        pass


class Prog:
    ENG = ("pe", "act", "dve", "pool", "sp")

    def __init__(self):
        self.q = {k: [] for k in self.ENG}
        self.waited = {k: {} for k in self.ENG}

    def wait(self, eng, w):
        if w is None:
            return
        s, v = w
        if v <= 0:
            return
        if v > self.waited[eng].get(id(s), 0):
            self.waited[eng][id(s)] = v
            self.q[eng].append(lambda e, s=s, v=v: e.wait_ge(s.h, v))

    def op(self, eng, fn, waits=(), inc=None):
        for w in waits:
            self.wait(eng, w)
        if inc is None:
            self.q[eng].append(fn)
            return None
        inc.n += inc.step
        self.q[eng].append(lambda e, fn=fn, inc=inc: fn(e).then_inc(inc.h, inc.step))
        return (inc, inc.n)

    def run(self, nc, dyn=None):
        q = self.q
        with nc.Block() as block:
            @block.tensor
            def _(e):
                for f in q["pe"]:
                    f(e)

            @block.scalar
            def _(e):
                for f in q["act"]:
                    f(e)

            @block.vector
            def _(e):
                for f in q["dve"]:
                    f(e)

            @block.gpsimd
            def _(e):
                for f in q["pool"]:
                    f(e)

            @block.sync
            def _(e):
                for f in q["sp"]:
                    f(e)


class Ring:
    def __init__(self, bufs, sems=None):
        self.bufs = bufs
        self.free_at = [None] * len(bufs)
        self.sems = sems
        self.i = 0

    def all_done(self):
        return [(sm, sm.n) for sm in self.sems]

    def acquire(self):
        i = self.i
        self.i = (i + 1) % len(self.bufs)
        return i, self.bufs[i], self.free_at[i]

    def release(self, i, w):
        self.free_at[i] = w


def sbt(nc, es, name, shape, dt):
    return es.enter_context(nc.sbuf_tensor(f"s_{name}_{_UID[0]}", shape, dt))


def dsems(nc, es, tag, n):
    return [Sem(nc, es, f"{tag}{i}", 16) for i in range(n)]


def emit_rmsnorm(P, S, x_src, g_sb, xs_ring, sq_ring, ones, rstd, psA, psB, pe_waits, dst_fn,
                 x_ready=None, rstd_ready=None):
    last_pe = None
    for c in range(KC if rstd_ready is None else 0):
        i, xb, fw = xs_ring.acquire()
        ld = P.op("sp", lambda e, xb=xb, c=c: e.dma_start(out=xb[:], in_=x_src(c)),
                  waits=[fw, x_ready], inc=xs_ring.sems[i])
        j, sqb, sfw = sq_ring.acquire()
        a = P.op("act", lambda e, xb=xb, sqb=sqb: e.activation(out=sqb[:], in_=xb[:], func=AF.Square),
                 waits=[ld, sfw], inc=S["act"])
        xs_ring.release(i, a)
        w0 = [a] + (list(pe_waits) if c == 0 else [])
        P.op("pe", lambda e, sqb=sqb, c=c: e.matmul(psA[:], lhsT=ones[:], rhs=sqb[:, 0:512],
                                                    start=(c == 0), stop=(c == KC - 1)), waits=w0)
        last_pe = P.op("pe", lambda e, sqb=sqb, c=c: e.matmul(psB[:], lhsT=ones[:], rhs=sqb[:, 512:1024],
                                                              start=(c == 0), stop=(c == KC - 1)),
                       inc=S["pe"])
        sq_ring.release(j, last_pe)
    ps_rel = None
    if rstd_ready is None:
        P.op("act", lambda e: e.activation(out=rstd[:, 0:512], in_=psA[:], func=AF.Sqrt, bias=EPS, scale=1.0 / D),
             waits=[last_pe])
        a = P.op("act", lambda e: e.activation(out=rstd[:, 512:1024], in_=psB[:], func=AF.Sqrt, bias=EPS,
                                               scale=1.0 / D), inc=S["act"])
        ps_rel = a
        rstd_ready = P.op("dve", lambda e: e.reciprocal(out=rstd[:], in_=rstd[:]), waits=[a], inc=S["dve"])
    last = None
    for c in range(KC):
        i, xb, fw = xs_ring.acquire()
        ld = P.op("sp", lambda e, xb=xb, c=c: e.dma_start(out=xb[:], in_=x_src(c)), waits=[fw, x_ready],
                  inc=xs_ring.sems[i])
        last = P.op("dve", lambda e, xb=xb, c=c: e.scalar_tensor_tensor(
            out=dst_fn(c), in0=xb[:], scalar=g_sb[:, c:c + 1], in1=rstd[:], op0=ALU.mult, op1=ALU.mult),
            waits=[ld, rstd_ready], inc=S["dve"])
        xs_ring.release(i, last)
    return last, ps_rel


class WStream:
    def __init__(self, P, S, bufs, sems):
        self.P, self.S = P, S
        self.ring = Ring(bufs, sems)

    def load(self, src_ap, nkc):
        i, wb, fw = self.ring.acquire()
        ld = self.P.op("pool", lambda e, wb=wb: e.dma_start(out=wb[:, 0:nkc, :], in_=src_ap),
                       waits=[fw], inc=self.ring.sems[i])
        return i, wb, ld

    def release(self, i, w):
        self.ring.release(i, w)


def mm_group(P, S, ps, n, lhsT_fn, rhs_fn, nk, waits, out_ap=None):
    o = out_ap if out_ap is not None else ps[:, 0:n]
    r = None
    for k in range(nk):
        w = waits if k == 0 else ()
        fn = lambda e, k=k: e.matmul(o, lhsT=lhsT_fn(k), rhs=rhs_fn(k), start=(k == 0), stop=(k == nk - 1))
        if k == nk - 1:
            r = P.op("pe", fn, waits=w, inc=S["pe"])
        else:
            P.op("pe", fn, waits=w)
    return r


def mk_sems(nc, es, tag):
    S = {}
    for nm, st in (("ld", 16), ("wld", 16), ("st", 16), ("pe", 1), ("act", 1), ("dve", 1), ("pool", 1),
                   ("cld", 16)):
        S[nm] = Sem(nc, es, f"{tag}_{nm}", st)
    return S


def phase_A(nc, io, dyn=None):
    P = Prog()
    _UID[0] += 1
    with ExitStack() as es:
        S = mk_sems(nc, es, "A")
        hT = sbt(nc, es, "hT", [128, KC, T], BF16)
        uT = sbt(nc, es, "uT", [128, 16, T], BF16)
        vn = sbt(nc, es, "vn", [128, 8, 2048], BF16)
        wbufs = [sbt(nc, es, f"w{i}", [128, KC, 256], BF16) for i in range(2)]
        xs = Ring([sbt(nc, es, f"xs{i}", [128, T], F32) for i in range(2)], dsems(nc, es, "A_xl", 2))
        sq = Ring([sbt(nc, es, f"sq{i}", [128, T], BF16) for i in range(2)])
        rstd = sbt(nc, es, "rstd", [128, T], F32)
        ones = sbt(nc, es, "ones", [128, 128], BF16)
        gsb = sbt(nc, es, "gsb", [128, KC], F32)
        normg = sbt(nc, es, "normg", [128, 2048], BF16)
        WT = sbt(nc, es, "WT", [128, 16, 128], BF16)
        brow = sbt(nc, es, "brow", [1, 2048], BF16)
        onesr = sbt(nc, es, "onesr", [1, 128], BF16)
        qst = Ring([sbt(nc, es, f"qst{i}", [128, T], BF16) for i in range(2)], dsems(nc, es, "A_qs", 2))
        vst = Ring([sbt(nc, es, f"vst{i}", [128, 8, 256], BF16) for i in range(2)], dsems(nc, es, "A_vs", 2))
        ss = sbt(nc, es, "ss", [128, 128], F32)
        rs = sbt(nc, es, "rs", [128, 128], F32)
        junk = sbt(nc, es, "junk", [128, 128], F32)
        ps = [es.enter_context(nc.psum_tensor(f"psA{i}_{_UID[0]}", [128, 512], F32)) for i in range(8)]
        psr = Ring(ps)

        c1 = P.op("sp", lambda e: e.dma_start(out=gsb[:], in_=io["g1"][:]), inc=S["cld"])
        if "zero_dst" in io:
            zt = sbt(nc, es, "zt", [128, 64], F32)
            zr = P.op("pool", lambda e: e.memset(zt[:], 0.0), inc=S["pool"])
            c1 = P.op("sp", lambda e: e.dma_start(out=io["zero_dst"], in_=zt[:]), waits=[zr], inc=S["cld"])
        P.op("pool", lambda e: e.dma_start(out=normg[:], in_=io["normg"][:]), inc=S["wld"])
        P.op("pool", lambda e: e.dma_start(out=WT[:], in_=io["wsT"][:]), inc=S["wld"])
        c2 = P.op("pool", lambda e: e.dma_start(out=brow[:], in_=io["brow"][:]), inc=S["wld"])
        P.op("pool", lambda e: e.memset(ones[:], 1.0))
        P.op("pool", lambda e: e.memset(onesr[:], 1.0))
        P.op("pool", lambda e: e.memset(ss[:], 0.0))
        cpool = P.op("pool", lambda e: e.affine_select(out=WT[:], in_=WT[:], pattern=[[0, 16], [1, 128]],
                                                       compare_op=ALU.is_ge, fill=0.0, base=0,
                                                       channel_multiplier=-1), waits=[c2], inc=S["pool"])

        rr_ = None
        if "rstd_in" in io:
            rr_ = P.op("sp", lambda e: e.dma_start(out=rstd[:], in_=io["rstd_in"][:]), inc=S["ld"])
        h_done, ps_rel = emit_rmsnorm(P, S, lambda c: io["xT"][c], gsb, xs, sq, ones, rstd, ps[0], ps[1],
                                      [cpool], lambda c: hT[:, c, :], rstd_ready=rr_)
        psr.free_at[0] = ps_rel
        psr.free_at[1] = ps_rel

        mixgA = sbt(nc, es, "mixgA", [128, 16], F32)
        cmg = P.op("sp", lambda e: e.dma_start(out=mixgA[:], in_=io["mixgA"][:]), inc=S["cld"])
        sgu_state = {}

        def emit_sgu(g):
            dv = None
            for tq in range(2):
                bi, pb, pfw = psr.acquire()
                full = None
                for i in range(4):
                    tc = tq * 4 + i
                    P.op("pe", lambda e, pb=pb, i=i, tc=tc, g=g: e.matmul(
                        pb[:, i * 128:(i + 1) * 128], lhsT=vn[:, tc, g * 128:(g + 1) * 128], rhs=WT[:, g, :],
                        start=True, stop=False), waits=[pfw, sgu_state["vn_done"], cpool] if i == 0 else ())
                    full = P.op("pe", lambda e, pb=pb, i=i, g=g: e.matmul(
                        pb[:, i * 128:(i + 1) * 128], lhsT=onesr[0:1, :], rhs=brow[0:1, g * 128:(g + 1) * 128],
                        start=False, stop=True), inc=S["pe"])
                dv = P.op("dve", lambda e, pb=pb, g=g, tq=tq: e.tensor_tensor(
                    out=uT[:, g, tq * 512:(tq + 1) * 512], in0=pb[:], in1=uT[:, g, tq * 512:(tq + 1) * 512],
                    op=ALU.mult), waits=[full, u_ready[g]], inc=S["dve"])
                psr.release(bi, dv)
            j, sqb, sfw = sq.acquire()
            a = P.op("act", lambda e, sqb=sqb, g=g: e.activation(out=sqb[:], in_=uT[:, g, :], func=AF.Square),
                     waits=[dv, sfw], inc=S["act"])
            sgu_state.setdefault("pend", []).append((g, j, sqb, a))

        def emit_sgu2():
            g, j, sqb, a = sgu_state["pend"].pop(0)
            ri, rb, rfw = xs.acquire()
            fulls, banks = [], []
            for th in range(2):
                bi, pb, pfw = psr.acquire()
                banks.append((bi, pb))
                fulls.append(P.op("pe", lambda e, pb=pb, sqb=sqb, th=th: e.matmul(
                    pb[:], lhsT=ones[:], rhs=sqb[:, th * 512:(th + 1) * 512], start=True, stop=True),
                    waits=[a, pfw], inc=S["pe"]))
            sq.release(j, fulls[1])
            a2 = None
            for th in range(2):
                a2 = P.op("act", lambda e, rb=rb, pb=banks[th][1], th=th: e.activation(
                    out=rb[:, th * 512:(th + 1) * 512], in_=pb[:], func=AF.Sqrt, bias=EPS, scale=1.0 / 128),
                    waits=[fulls[th], rfw], inc=S["act"])
                psr.release(banks[th][0], a2)
            P.op("dve", lambda e, rb=rb: e.reciprocal(out=rb[:], in_=rb[:]), waits=[a2, cmg])
            gd = P.op("dve", lambda e, rb=rb, g=g: e.scalar_tensor_tensor(
                out=uT[:, g, :], in0=uT[:, g, :], scalar=mixgA[:, g:g + 1], in1=rb[:], op0=ALU.mult, op1=ALU.mult),
                inc=S["dve"])
            xs.release(ri, gd)
            P.op("sp", lambda e, g=g: e.dma_start(out=io["gT"][g], in_=uT[:, g, :]), waits=[gd], inc=S["st"])

        ws = WStream(P, S, wbufs, dsems(nc, es, "A_wl", 2))
        NB = 40
        pending = []
        for b in range(min(2, NB)):
            pending.append(ws.load(io["w_in"][b], KC))
        u_ready = [None] * 16
        vn_done = None
        for b in range(NB):
            wi, wb, wld = pending.pop(0)
            blk_done = None
            if b < 16 or b >= 32:
                for j in range(2):
                    ch = 2 * b + j if b < 16 else 2 * (b - 32) + j
                    if b < 16:
                        si, stg, sfw = qst.acquire()
                    a = None
                    for th in range(2):
                        bi, pb, pfw = psr.acquire()
                        full = mm_group(P, S, pb, 512,
                                        lambda k, wb=wb, j=j: wb[:, k, j * 128:(j + 1) * 128],
                                        lambda k, th=th: hT[:, k, th * 512:(th + 1) * 512],
                                        KC, [wld, pfw, h_done])
                        if b < 16:
                            a = P.op("act", lambda e, stg=stg, pb=pb, th=th: e.activation(
                                out=stg[:, th * 512:(th + 1) * 512], in_=pb[:], func=AF.Copy),
                                waits=[full, sfw], inc=S["act"])
                        else:
                            a = P.op("act", lambda e, pb=pb, th=th, ch=ch: e.activation(
                                out=uT[:, ch, th * 512:(th + 1) * 512], in_=pb[:], func=AF.Gelu),
                                waits=[full], inc=S["act"])
                            u_ready[ch] = a
                        psr.release(bi, a)
                        blk_done = a
                    if b < 16:
                        hd = ch % 16
                        cho = (hd // 4) * 8 + (4 if ch >= 16 else 0) + hd % 4
                        st = P.op("sp", lambda e, stg=stg, cho=cho: e.dma_start(out=io["qk"][cho], in_=stg[:]),
                                  waits=[a, c1], inc=qst.sems[si])
                        qst.release(si, st)
            else:
                isv = b < 24
                cb = (b - 16) if isv else (b - 24)
                if isv:
                    si, stg, sfw = vst.acquire()
                a = None
                for tc in range(8):
                    bi, pb, pfw = psr.acquire()
                    full = mm_group(P, S, pb, 256,
                                    lambda k, tc=tc: hT[:, k, tc * 128:(tc + 1) * 128],
                                    lambda k, wb=wb: wb[:, k, :],
                                    KC, [wld, pfw, h_done])
                    if isv:
                        a = P.op("act", lambda e, stg=stg, pb=pb, tc=tc: e.activation(
                            out=stg[:, tc, :], in_=pb[:, 0:256], func=AF.Copy), waits=[full, sfw], inc=S["act"])
                    else:
                        a = P.op("act", lambda e, pb=pb, tc=tc, cb=cb: e.activation(
                            out=vn[:, tc, cb * 256:(cb + 1) * 256], in_=pb[:, 0:256], func=AF.Gelu),
                            waits=[full], inc=S["act"])
                        for gg in range(2):
                            g = cb * 2 + gg
                            P.op("act", lambda e, tc=tc, g=g: e.activation(
                                out=junk[:], in_=vn[:, tc, g * 128:(g + 1) * 128], func=AF.Square,
                                accum_out=ss[:, tc * 16 + g:tc * 16 + g + 1]))
                    psr.release(bi, a)
                    blk_done = a
                if isv:
                    vdst = io["v"](cb)
                    st = P.op("sp", lambda e, stg=stg, vdst=vdst: e.dma_start(out=vdst, in_=stg[:]),
                              waits=[a], inc=vst.sems[si])
                    vst.release(si, st)
            ws.release(wi, blk_done)
            if b + 2 < NB:
                pending.append(ws.load(io["w_in"][b + 2], KC))
            if b == 15 and "ag_qk" in io:
                prog_allgather(P, io["cc"], io["ag_qk"], qst.all_done())
            if b == 23 and "ag_v" in io:
                prog_allgather(P, io["cc"], io["ag_v"], vst.all_done())
            if b == 31:
                a = P.op("act", lambda e: e.activation(out=rs[:], in_=ss[:], func=AF.Sqrt, bias=EPS,
                                                       scale=1.0 / 128), inc=S["act"])
                P.op("dve", lambda e: e.reciprocal(out=rs[:], in_=rs[:]), waits=[a, c2])
                for tc in range(8):
                    for g in range(16):
                        col = tc * 16 + g
                        vn_done = P.op("dve", lambda e, tc=tc, g=g, col=col: e.scalar_tensor_tensor(
                            out=vn[:, tc, g * 128:(g + 1) * 128], in0=vn[:, tc, g * 128:(g + 1) * 128],
                            scalar=rs[:, col:col + 1], in1=normg[:, g * 128:(g + 1) * 128],
                            op0=ALU.mult, op1=ALU.mult), inc=S["dve"])
                sgu_state["vn_done"] = vn_done
            if b >= 32:
                while sgu_state.get("pend"):
                    emit_sgu2()
                emit_sgu(2 * (b - 32))
                emit_sgu(2 * (b - 32) + 1)

        while sgu_state.get("pend"):
            emit_sgu2()
        P.wait("sp", (S["st"], S["st"].n))
        for w in qst.all_done() + vst.all_done():
            P.wait("sp", w)
        P.run(nc, dyn)


def attn_groups(d):
    out = []
    nb = 32 // d
    if d == 16:
        for r0 in range(0, 16, 2):
            out.append(([(r0, 0), (r0, 1), (r0 + 1, 0), (r0 + 1, 1)], ("pair", r0)))
    else:
        for r in range(d):
            for n0 in range(0, nb, 4):
                out.append(([(r, n0 + i) for i in range(4)], ("run", r, n0)))
    return out


def phase_B(nc, io, dyn=None):
    P = Prog()
    _UID[0] += 1
    NS = 4096
    with ExitStack() as es:
        S = mk_sems(nc, es, "B")
        S["sfull"] = Sem(nc, es, "B_sfull", 1)
        S["ofull"] = Sem(nc, es, "B_ofull", 1)
        S["fin"] = Sem(nc, es, "B_fin", 1)
        nset = 2
        qT = [sbt(nc, es, f"qT{i}", [128, NS], BF16) for i in range(nset)]
        kT = [sbt(nc, es, f"kT{i}", [128, NS], BF16) for i in range(nset)]
        vd = [{d: sbt(nc, es, f"v{d}_{i}", [128, 32, 128], BF16) for d in (1, 4, 16)} for i in range(nset)]
        inring = Ring(list(range(nset)), dsems(nc, es, "B_hl", nset))
        acc_o = sbt(nc, es, "acc_o", [128, NS], F32)
        acc_z = sbt(nc, es, "acc_z", [128, NS], F32)
        aT = sbt(nc, es, "aT", [128, NS], BF16)
        EB = sbt(nc, es, "EB", [128, 12, 4, 256], BF16)
        sqh = sbt(nc, es, "sqh", [128, NS], BF16)
        mixgB = sbt(nc, es, "mixgB", [128, 4], F32)
        pT = Ring([sbt(nc, es, f"pT{i}", [128, 1024], BF16) for i in range(2)])
        pT2 = Ring([sbt(nc, es, f"pTm{i}", [128, 1024], BF16) for i in range(2)])
        ones = sbt(nc, es, "onesB", [128, 128], BF16)
        nsl = sbt(nc, es, "nsl", [128, 12], F32)
        it = sbt(nc, es, "it", [128, 256], I32)
        dist = sbt(nc, es, "dist", [128, 256], F32)
        etmp = Ring([sbt(nc, es, f"etmp{i}", [128, 256], F32) for i in range(2)])
        ps = [es.enter_context(nc.psum_tensor(f"psB{i}_{_UID[0]}", [128, 512], F32)) for i in range(8)]
        sp_ring = Ring([0, 1])
        o_ring = Ring([0, 1])

        P.op("sp", lambda e: e.dma_start(out=mixgB[:], in_=io["mixgB"][:]), inc=S["cld"])
        c1 = P.op("sp", lambda e: e.dma_start(out=nsl[:], in_=io["nsl"][:]), inc=S["cld"])
        P.op("pool", lambda e: e.memset(ones[:], 1.0))
        P.op("pool", lambda e: e.iota(it[:], [[1, 256]], base=0, channel_multiplier=-1))
        itd = P.op("pool", lambda e: e.memset(ones[:, 0:1], 1.0), inc=S["pool"])
        P.op("dve", lambda e: e.tensor_copy(out=dist[:], in_=it[:]), waits=[itd])
        P.op("dve", lambda e: e.tensor_scalar(out=dist[:, 0:128], in0=dist[:, 0:128], scalar1=128.0,
                                              scalar2=None, op0=ALU.add))
        dready = P.op("dve", lambda e: e.tensor_scalar(out=dist[:, 128:256], in0=dist[:, 128:256],
                                                       scalar1=-128.0, scalar2=None, op0=ALU.add), inc=S["dve"])
        eb_done = None
        for idx in range(12):
            ei, eb, efw = etmp.acquire()
            a = P.op("act", lambda e, eb=eb, idx=idx: e.activation(out=eb[:], in_=dist[:], func=AF.Exp,
                                                                   scale=nsl[:, idx:idx + 1]),
                     waits=[dready, c1, efw], inc=S["act"])
            P.op("pool", lambda e, eb=eb, idx=idx: e.affine_select(
                out=EB[:, idx, 0, 0:128], in_=eb[:, 0:128], pattern=[[-1, 128]], compare_op=ALU.is_ge, fill=0.0,
                base=0, channel_multiplier=1), waits=[a])
            eb_done = P.op("pool", lambda e, eb=eb, idx=idx: e.affine_select(
                out=EB[:, idx, 0, 128:256], in_=eb[:, 128:256], pattern=[[1, 128]], compare_op=ALU.is_ge, fill=0.0,
                base=0, channel_multiplier=-1), inc=S["pool"])
            etmp.release(ei, eb_done)
            for rep in range(1, 4):
                eb_done = P.op("pool", lambda e, idx=idx, rep=rep: e.tensor_copy(
                    out=EB[:, idx, rep, :], in_=EB[:, idx, 0, :]), inc=S["pool"])

        pre_done = None
        if "cc" in io:
            P.wait("sp", (io["cc"], io["cc"].n))
        for (dst, srcf) in io.get("pre", []):
            pre_done = P.op("sp", lambda e, dst=dst, srcf=srcf: e.dma_start(out=dst, in_=srcf()), inc=S["cld"])

        def load_head(h):
            i, si, fw = inring.acquire()
            lsem = inring.sems[i]
            P.op("sp", lambda e: e.dma_start(out=qT[si][:].rearrange("p (r t) -> p r t", r=4), in_=io["q"](h)),
                 waits=[fw, pre_done], inc=lsem)
            P.op("sp", lambda e: e.dma_start(out=kT[si][:].rearrange("p (r t) -> p r t", r=4), in_=io["k"](h)),
                 inc=lsem)
            ld = None
            for d in (1, 4, 16):
                nb = 32 // d
                for r in range(d):
                    ld = P.op("sp", lambda e, d=d, r=r, nb=nb: e.dma_start(
                        out=vd[si][d][:, r * nb:(r + 1) * nb, :], in_=io["v"](h, d, r)), inc=lsem)
            return i, si, ld

        nxt = load_head(0)
        a_st = None
        fin_prev = None
        for h in range(4):
            ri, si, ld = nxt
            if h + 1 < 4:
                nxt = load_head(h + 1)
            q_, k_ = qT[si], kT[si]
            last_pe_head = None
            for di, d in enumerate((1, 4, 16)):
                nb = 32 // d
                groups = attn_groups(d)
                qv = q_[:].rearrange("p (m r) -> p r m", r=d)
                kv = k_[:].rearrange("p (m r) -> p r m", r=d)
                ebi = h * 3 + di
                state = {}

                def p1(gi):
                    blocks, sel = groups[gi]
                    _, sset, sfw = sp_ring.acquire()
                    full = None
                    first = True
                    for b, (r, n) in enumerate(blocks):
                        bank = ps[4 * sset + b // 2]
                        off = (b % 2) * 256
                        qa = qv[:, r, n * 128:(n + 1) * 128]
                        if n > 0:
                            ka = kv[:, r, (n - 1) * 128:n * 128]
                            P.op("pe", lambda e, bank=bank, off=off, ka=ka, qa=qa: e.matmul(
                                bank[:, off:off + 128], lhsT=ka, rhs=qa, start=True, stop=True),
                                waits=[sfw, ld] if first else ())
                            first = False
                        ka = kv[:, r, n * 128:(n + 1) * 128]
                        fn = lambda e, bank=bank, off=off, ka=ka, qa=qa: e.matmul(
                            bank[:, off + 128:off + 256], lhsT=ka, rhs=qa, start=True, stop=True)
                        if b == 3:
                            full = P.op("pe", fn, waits=[sfw, ld] if first else (), inc=S["sfull"])
                        else:
                            P.op("pe", fn, waits=[sfw, ld] if first else ())
                        first = False
                    pi, pbuf, pfw = pT.acquire()
                    P.op("act", lambda e, pbuf=pbuf, sset=sset: e.activation(
                        out=pbuf[:, 0:512], in_=ps[4 * sset][:], func=AF.Exp, scale=SCALE), waits=[full, pfw])
                    a = P.op("act", lambda e, pbuf=pbuf, sset=sset: e.activation(
                        out=pbuf[:, 512:1024], in_=ps[4 * sset + 1][:], func=AF.Exp, scale=SCALE), inc=S["act"])
                    sp_ring.release(sset, a)
                    mi, mbuf, mfw = pT2.acquire()
                    meng = "dve"
                    pl = P.op(meng, lambda e, mbuf=mbuf, pbuf=pbuf, ebi=ebi: e.tensor_tensor(
                        out=mbuf[:].rearrange("p (b c) -> p b c", b=4),
                        in0=pbuf[:].rearrange("p (b c) -> p b c", b=4), in1=EB[:, ebi, :, :], op=ALU.mult),
                        waits=[a, mfw, eb_done], inc=S[meng])
                    pT.release(pi, pl)
                    state[gi] = (mi, mbuf, pl)

                def p2(gi):
                    blocks, sel = groups[gi]
                    mi, mbuf, pl = state.pop(gi)
                    _, oset, ofw = o_ring.acquire()
                    po, pz = ps[4 * oset + 2], ps[4 * oset + 3]
                    vt = vd[si][d]
                    full = None
                    first = True
                    for b, (r, n) in enumerate(blocks):
                        j = r * nb + n
                        for tgt, lfn in ((po, lambda jj: vt[:, jj, :]), (pz, lambda jj: ones[:])):
                            oa = tgt[:, b * 128:(b + 1) * 128]
                            if n > 0:
                                P.op("pe", lambda e, oa=oa, l=lfn(j - 1), b=b: e.matmul(
                                    oa, lhsT=l, rhs=mbuf[:, b * 256:b * 256 + 128], start=True, stop=False),
                                    waits=[pl, ofw] if first else ())
                                first = False
                            fn = lambda e, oa=oa, l=lfn(j), b=b, n=n: e.matmul(
                                oa, lhsT=l, rhs=mbuf[:, b * 256 + 128:b * 256 + 256], start=(n == 0), stop=True)
                            if b == 3 and tgt is pz:
                                full = P.op("pe", fn, waits=[pl, ofw] if first else (), inc=S["ofull"])
                            else:
                                P.op("pe", fn, waits=[pl, ofw] if first else ())
                            first = False
                    pT2.release(mi, full)
                    if sel[0] == "pair":
                        r0 = sel[1]
                        ao = acc_o[:].rearrange("p (m r) -> p r m", r=16)[:, r0:r0 + 2, :]
                        az = acc_z[:].rearrange("p (m r) -> p r m", r=16)[:, r0:r0 + 2, :]
                        pov = po[:].rearrange("p (a m) -> p a m", a=2)
                        pzv = pz[:].rearrange("p (a m) -> p a m", a=2)
                    else:
                        _, r, n0 = sel
                        ao = acc_o[:].rearrange("p (m r) -> p r m", r=d)[:, r, n0 * 128:n0 * 128 + 512]
                        az = acc_z[:].rearrange("p (m r) -> p r m", r=d)[:, r, n0 * 128:n0 * 128 + 512]
                        pov, pzv = po[:], pz[:]
                    if d == 1:
                        P.op("dve", lambda e, ao=ao, pov=pov: e.tensor_copy(out=ao, in_=pov),
                             waits=[full, fin_prev])
                        dv = P.op("dve", lambda e, az=az, pzv=pzv: e.tensor_copy(out=az, in_=pzv), inc=S["dve"])
                    else:
                        P.op("dve", lambda e, ao=ao, pov=pov: e.tensor_tensor(out=ao, in0=ao, in1=pov, op=ALU.add),
                             waits=[full])
                        dv = P.op("dve", lambda e, az=az, pzv=pzv: e.tensor_tensor(out=az, in0=az, in1=pzv,
                                                                                   op=ALU.add), inc=S["dve"])
                    o_ring.release(oset, dv)
                    return full

                ng = len(groups)
                p1(0)
                for gi in range(ng):
                    if gi + 1 < ng:
                        p1(gi + 1)
                    last_pe_head = p2(gi)
                    if di == 0 and gi == 5 and h > 0 and "ag_a" in io:
                        prog_allgather(P, io["cc"], [io["ag_a"][h - 1]], [a_st])
            inring.release(ri, last_pe_head)
            P.op("dve", lambda e: e.reciprocal(out=acc_z[:], in_=acc_z[:]))
            fin = P.op("dve", lambda e: e.tensor_tensor(out=aT[:], in0=acc_o[:], in1=acc_z[:], op=ALU.mult),
                       waits=[a_st], inc=S["fin"])
            asq = P.op("act", lambda e: e.activation(out=sqh[:], in_=aT[:], func=AF.Square), waits=[fin],
                       inc=S["act"])
            a2 = None
            for qq in range(4):
                _, oset, ofw = o_ring.acquire()
                po, pz = ps[4 * oset + 2], ps[4 * oset + 3]
                P.op("pe", lambda e, po=po, qq=qq: e.matmul(po[:], lhsT=ones[:], rhs=sqh[:, qq * 1024:qq * 1024 + 512],
                                                            start=True, stop=True), waits=[asq, ofw])
                full = P.op("pe", lambda e, pz=pz, qq=qq: e.matmul(
                    pz[:], lhsT=ones[:], rhs=sqh[:, qq * 1024 + 512:qq * 1024 + 1024], start=True, stop=True),
                    inc=S["ofull"])
                P.op("act", lambda e, po=po, qq=qq: e.activation(
                    out=acc_z[:, qq * 1024:qq * 1024 + 512], in_=po[:], func=AF.Sqrt, bias=EPS, scale=1.0 / 128),
                    waits=[full])
                a2 = P.op("act", lambda e, pz=pz, qq=qq: e.activation(
                    out=acc_z[:, qq * 1024 + 512:qq * 1024 + 1024], in_=pz[:], func=AF.Sqrt, bias=EPS,
                    scale=1.0 / 128), inc=S["act"])
                o_ring.release(oset, a2)
            P.op("dve", lambda e: e.reciprocal(out=acc_z[:], in_=acc_z[:]), waits=[a2, c1])
            fin2 = P.op("dve", lambda e, h=h: e.scalar_tensor_tensor(
                out=aT[:], in0=aT[:], scalar=mixgB[:, h:h + 1], in1=acc_z[:], op0=ALU.mult, op1=ALU.mult),
                inc=S["fin"])
            a_st = P.op("sp", lambda e, h=h: e.dma_start(out=io["a"](h), in_=aT[:].rearrange(
                "p (tb t) -> p tb t", tb=4)), waits=[fin2], inc=S["st"])
            fin_prev = None
        if "ag_a" in io:
            prog_allgather(P, io["cc"], [io["ag_a"][3]], [a_st])
        P.wait("sp", (S["st"], S["st"].n))
        P.run(nc, dyn)


class XPipe:
    def __init__(self, P, xs, x_src, order, st_hist):
        self.P, self.xs, self.x_src, self.order, self.st_hist = P, xs, x_src, order, st_hist
        self.nxt = 0
        self.loaded = {}

    def issue(self):
        if self.nxt >= len(self.order):
            return
        ch = self.order[self.nxt]
        self.nxt += 1
        i, xb, fw = self.xs.acquire()
        src = self.x_src(ch)
        xl = self.P.op("sp", lambda e, xb=xb, src=src: e.dma_start(out=xb[:], in_=src),
                       waits=[fw, self.st_hist.get(ch)], inc=self.xs.sems[i])
        self.loaded[ch] = (i, xb, xl)


def ssq_flush(P, S, ssq, keep):
    while len(ssq["pend"]) > keep:
        ch, j, sqb, a = ssq["pend"].pop(0)
        w0 = [a] + (list(ssq["bank_free"]) if ch == 0 else [])
        P.op("pe", lambda e, sqb=sqb, ch=ch: e.matmul(ssq["psA"][:], lhsT=ssq["ones"][:], rhs=sqb[:, 0:512],
                                                      start=(ch == 0), stop=(ch == KC - 1)), waits=w0)
        pe = P.op("pe", lambda e, sqb=sqb, ch=ch: e.matmul(ssq["psB"][:], lhsT=ssq["ones"][:], rhs=sqb[:, 512:1024],
                                                           start=(ch == 0), stop=(ch == KC - 1)), inc=S["pe"])
        ssq["sq"].release(j, pe)
        ssq["last_pe"] = pe


def ssq_finish(P, S, ssq, rstd):
    ssq_flush(P, S, ssq, 0)
    P.op("act", lambda e: e.activation(out=rstd[:, 0:512], in_=ssq["psA"][:], func=AF.Sqrt, bias=EPS,
                                       scale=1.0 / D), waits=[ssq["last_pe"]])
    a = P.op("act", lambda e: e.activation(out=rstd[:, 512:1024], in_=ssq["psB"][:], func=AF.Sqrt, bias=EPS,
                                           scale=1.0 / D), inc=S["act"])
    return P.op("dve", lambda e: e.reciprocal(out=rstd[:], in_=rstd[:]), waits=[a], inc=S["dve"])


def emit_resid_block(P, S, psr, xp, xo, wb, wld, b, nk, rhs_fn, x_dst, ready, st_hist, halo_dst=None, ssq=None):
    blk_done = None
    for j in range(2):
        ch = 2 * b + j
        xi, xb, xl = xp.loaded.pop(ch)
        oi, ob, ofw = xo.acquire()
        if ssq is not None:
            P.wait("dve", ssq["ob_read"].get(oi))
        dv = None
        for th in range(2):
            bi, pb, pfw = psr.acquire()
            full = mm_group(P, S, pb, 512, lambda k, wb=wb, j=j: wb[:, k, j * 128:(j + 1) * 128],
                            lambda k, th=th: rhs_fn(k, th), nk, [wld, pfw] + list(ready))
            dv = P.op("dve", lambda e, ob=ob, pb=pb, xb=xb, th=th: e.tensor_tensor(
                out=ob[:, th * 512:(th + 1) * 512], in0=pb[:], in1=xb[:, th * 512:(th + 1) * 512],
                op=ALU.add), waits=[full, xl, ofw], inc=S["dve"])
            psr.release(bi, dv)
            blk_done = full
        xp.xs.release(xi, dv)
        dst = x_dst(ch)
        if halo_dst is not None:
            hd = halo_dst(ch)
            P.op("sp", lambda e, ob=ob, hd=hd: e.dma_start(out=hd, in_=ob[:, T - 2:T]), waits=[dv],
                 inc=xo.sems[oi])
        st = P.op("sp", lambda e, ob=ob, dst=dst: e.dma_start(out=dst, in_=ob[:]), waits=[dv],
                  inc=xo.sems[oi])
        st_hist[ch] = st
        xo.release(oi, st)
        xp.issue()
        if ssq is not None:
            jj, sqb, sfw = ssq["sq"].acquire()
            a = P.op("act", lambda e, sqb=sqb, ob=ob: e.activation(out=sqb[:], in_=ob[:], func=AF.Square),
                     waits=[dv, sfw], inc=S["act"])
            ssq["ob_read"][oi] = a
            ssq_flush(P, S, ssq, 0)
            ssq["pend"].append((ch, jj, sqb, a))
    return blk_done


def phase_C1(nc, io, dyn=None):
    P = Prog()
    _UID[0] += 1
    with ExitStack() as es:
        S = mk_sems(nc, es, "C1")
        S["ld2"] = Sem(nc, es, "C1_ld2", 16)
        mT = sbt(nc, es, "mT", [128, KC, T], BF16)
        wbufs = [sbt(nc, es, f"wc{i}", [128, 16, 256], BF16) for i in range(4)]
        xs = Ring([sbt(nc, es, f"xc{i}", [128, T], F32) for i in range(3)], dsems(nc, es, "C1_xl", 3))
        xo = Ring([sbt(nc, es, f"xo{i}", [128, T], F32) for i in range(2)], dsems(nc, es, "C1_xs", 2))
        ps = [es.enter_context(nc.psum_tensor(f"psC{i}_{_UID[0]}", [128, 512], F32)) for i in range(8)]
        psr = Ring(ps[0:6])
        ones = sbt(nc, es, "onesC", [128, 128], BF16)
        rstd = sbt(nc, es, "rstdC", [128, T], F32)
        sq = Ring([sbt(nc, es, f"sqc{i}", [128, T], BF16) for i in range(2)])
        ones_ok = P.op("pool", lambda e: e.memset(ones[:], 1.0), inc=S["pool"])
        ssq = {"sq": sq, "psA": ps[6], "psB": ps[7], "ones": ones, "pend": [], "ob_read": {},
               "bank_free": [ones_ok]}

        ld_g = None
        for c in range(16):
            ld_g = P.op("sp", lambda e, c=c: e.dma_start(out=mT[:, 16 + c, :], in_=io["gT"][c]), inc=S["ld"])
        st_hist = {}
        ws = WStream(P, S, wbufs, dsems(nc, es, "C1_wl", 4))
        sched = [(0, b) for b in range(16)] + [(1, b) for b in range(16)]
        pending = [ws.load(io["w_out"][hf][b], 16) for (hf, b) in sched[:4]]
        nxt = 4
        xp = XPipe(P, xs, lambda ch: io["xT"][ch], list(range(KC)), st_hist)
        for _ in range(3):
            xp.issue()
        ld_a = None
        for (hf, b) in sched:
            if hf == 1 and b == 0:
                if "cc" in io:
                    P.wait("sp", (io["cc"], io["cc"].n))
                pre_done = None
                for (dst, srcf) in io.get("pre", []):
                    pre_done = P.op("sp", lambda e, dst=dst, srcf=srcf: e.dma_start(out=dst, in_=srcf()),
                                    inc=S["cld"])
                P.wait("sp", pre_done)
                for c in range(16):
                    ld_a = P.op("sp", lambda e, c=c: e.dma_start(out=mT[:, c, :], in_=io["aT"](c)), inc=S["ld2"])
                xp = XPipe(P, xs, lambda ch: io["xo"][ch], list(range(KC)), st_hist)
                for _ in range(3):
                    xp.issue()
            wi, wb, wld = pending.pop(0)
            if hf == 0:
                blk_done = emit_resid_block(P, S, psr, xp, xo, wb, wld, b, 16,
                                            lambda k, th: mT[:, 16 + k, th * 512:(th + 1) * 512],
                                            lambda ch: io["xo"][ch], [ld_g], st_hist)
            else:
                blk_done = emit_resid_block(P, S, psr, xp, xo, wb, wld, b, 16,
                                            lambda k, th: mT[:, k, th * 512:(th + 1) * 512],
                                            lambda ch: io["xo"][ch], [ld_a], st_hist,
                                            halo_dst=io.get("halo_dst"), ssq=ssq if "rstd_out" in io else None)
            ws.release(wi, blk_done)
            if nxt < len(sched):
                pending.append(ws.load(io["w_out"][sched[nxt][0]][sched[nxt][1]], 16))
                nxt += 1
        if "rstd_out" in io:
            rd = ssq_finish(P, S, ssq, rstd)
            P.op("sp", lambda e: e.dma_start(out=io["rstd_out"][:], in_=rstd[:]), waits=[rd], inc=S["st"])
            P.wait("sp", (S["st"], S["st"].n))
        for w in xo.all_done():
            P.wait("sp", w)
        if "ag_xh" in io:
            prog_allgather(P, io["cc"], io["ag_xh"], xo.all_done())
        P.run(nc, dyn)


def phase_C2(nc, io, final, dyn=None):
    P = Prog()
    _UID[0] += 1
    with ExitStack() as es:
        S = mk_sems(nc, es, "C2")
        S["act2"] = Sem(nc, es, "C2_act2", 1)
        S["dve2"] = Sem(nc, es, "C2_dve2", 1)
        hT = sbt(nc, es, "h2T", [128, KC, T + 2], BF16)
        gsl = sbt(nc, es, "gsl", [128, 22, T], BF16)
        wbufs = [sbt(nc, es, f"wf{i}", [128, KC, 256], BF16) for i in range(2)]
        xs = Ring([sbt(nc, es, f"xf{i}", [128, T], F32) for i in range(3)], dsems(nc, es, "C2_xl", 3))
        xo = Ring([sbt(nc, es, f"xg{i}", [128, T], F32) for i in range(3)], dsems(nc, es, "C2_xs", 3))
        sq = Ring([sbt(nc, es, f"sqf{i}", [128, T], BF16) for i in range(2)])
        rstd = sbt(nc, es, "rstdf", [128, T], F32)
        ones = sbt(nc, es, "onesF", [128, 128], BF16)
        g2 = sbt(nc, es, "g2s", [128, KC], F32)
        gf = sbt(nc, es, "gfs", [128, KC], F32)
        cw = sbt(nc, es, "cws", [128, 172, 3], F32)
        cb = sbt(nc, es, "cbs", [128, 172], F32)
        xh = sbt(nc, es, "xhs", [128, KC, 2], F32)
        xh2 = sbt(nc, es, "xh2", [128, KC, 2], BF16)
        rh = sbt(nc, es, "rh", [128, 2], F32)
        U = Ring([(sbt(nc, es, f"Ug{i}", [128, T + 2], F32), sbt(nc, es, f"Uv{i}", [128, T + 2], F32))
                  for i in range(2)])
        cc = Ring([(sbt(nc, es, f"cg{i}", [128, 512], F32), sbt(nc, es, f"cv{i}", [128, 512], F32))
                   for i in range(2)])
        sgr = Ring([sbt(nc, es, f"sg{i}", [128, 512], F32) for i in range(2)])
        ps = [es.enter_context(nc.psum_tensor(f"psF{i}_{_UID[0]}", [128, 512], F32)) for i in range(8)]
        psr = Ring(ps)
        ph = ps[7]

        c1 = None
        for dst, src in ((g2, "g2"), (cw, "cw"), (cb, "cb"), (xh, "xh")) + (((gf, "gf"),) if final else ()):
            if callable(io[src]):
                if "cc" in io:
                    P.wait("sp", (io["cc"], io["cc"].n))
                c1 = P.op("sp", lambda e, dst=dst, src=src: e.dma_start(
                    out=dst[:].rearrange("p k t -> p (k t)"), in_=io[src]()), inc=S["cld"])
            else:
                c1 = P.op("sp", lambda e, dst=dst, src=src: e.dma_start(out=dst[:], in_=io[src][:]), inc=S["cld"])
        ones_ok = P.op("pool", lambda e: e.memset(ones[:], 1.0), inc=S["pool"])

        a = P.op("act", lambda e: e.activation(out=xh2[:], in_=xh[:], func=AF.Square), waits=[c1], inc=S["act"])
        full = mm_group(P, S, ph, 2, lambda k: ones[:], lambda k: xh2[:, k, :], KC, [a, ones_ok])
        a = P.op("act", lambda e: e.activation(out=rh[:], in_=ph[:, 0:2], func=AF.Sqrt, bias=EPS, scale=1.0 / D),
                 waits=[full], inc=S["act"])
        ph_free = a
        P.op("dve", lambda e: e.reciprocal(out=rh[:], in_=rh[:]), waits=[a])
        hh_done = None
        for c in range(KC):
            hh_done = P.op("dve", lambda e, c=c: e.scalar_tensor_tensor(
                out=hT[:, c, 0:2], in0=xh[:, c, :], scalar=g2[:, c:c + 1], in1=rh[:], op0=ALU.mult, op1=ALU.mult),
                inc=S["dve"])

        rr_ = None
        if "rstd_in" in io:
            rr_ = P.op("sp", lambda e: e.dma_start(out=rstd[:], in_=io["rstd_in"][:]), inc=S["ld"])
        h_done, ps_rel = emit_rmsnorm(P, S, lambda c: io["xT"][c], g2, xs, sq, ones, rstd, ps[0], ps[1],
                                      [ones_ok], lambda c: hT[:, c, 2:T + 2], rstd_ready=rr_)
        psr.free_at[0] = ps_rel
        psr.free_at[1] = ps_rel
        psr.free_at[7] = ph_free
        ssq = None

        ws = WStream(P, S, wbufs, dsems(nc, es, "C2_wl", 2))
        sched = []
        for s, (k0, nk) in enumerate(SLABS):
            for jj in range(nk):
                sched.append(("up", s, jj, io["w_up"][k0 + jj], KC))
            for b in range(16):
                sched.append(("dn", s, b, io[f"w_dn{s}"][b], nk))
        pending = [ws.load(sched[i][3], sched[i][4]) for i in range(2)]
        nxt_load = 2
        st_hist = {}
        stage_b = []
        g_done = None
        xp = None

        def emit_stage_b():
            (cgb, cvb, ci, jj, th, dvc) = stage_b.pop(0)
            gi_, sgb, gfw = sgr.acquire()
            a2 = P.op("act", lambda e, sgb=sgb, cgb=cgb: e.activation(out=sgb[:], in_=cgb[:], func=AF.Silu),
                      waits=[dvc, gfw], inc=S["act2"])
            d2 = P.op("dve", lambda e, sgb=sgb, cvb=cvb, jj=jj, th=th: e.tensor_tensor(
                out=gsl[:, jj, th * 512:(th + 1) * 512], in0=sgb[:], in1=cvb[:], op=ALU.mult),
                waits=[a2], inc=S["dve2"])
            sgr.release(gi_, d2)
            cc.release(ci, d2)
            return d2

        for it_, (kind, s, idx, src, nkc) in enumerate(sched):
            wi, wb, wld = pending.pop(0)
            k0, nk = SLABS[s]
            blk_done = None
            if kind == "up":
                jj = idx
                cg_, cv_ = k0 + jj, FC + k0 + jj
                ui, (Ug, Uv), ufw = U.acquire()
                TW = (T + 2) // 3
                a = None
                for t3 in range(3):
                    for (ub, c0_) in ((Ug, 0), (Uv, 128)):
                        bi_, pb_, pw_ = psr.acquire()
                        f_ = mm_group(P, S, pb_, TW, lambda k, wb=wb, c0_=c0_: wb[:, k, c0_:c0_ + 128],
                                      lambda k, t3=t3: hT[:, k, t3 * TW:(t3 + 1) * TW], KC,
                                      [wld, pw_, h_done, hh_done])
                        blk_done = f_
                        a = P.op("act", lambda e, ub=ub, pb_=pb_, t3=t3: e.activation(
                            out=ub[:, t3 * TW:(t3 + 1) * TW], in_=pb_[:, 0:TW], func=AF.Copy),
                            waits=[f_, ufw], inc=S["act"])
                        psr.release(bi_, a)
                for th in range(2):
                    ci, (cgb, cvb), cfw = cc.acquire()
                    lo = 2 + th * 512
                    P.op("act", lambda e, cgb=cgb, Ug=Ug, c=cg_, lo=lo: e.activation(
                        out=cgb[:], in_=Ug[:, lo:lo + 512], func=AF.Identity, bias=cb[:, c:c + 1],
                        scale=cw[:, c, 2:3]), waits=[cfw, c1])
                    a = P.op("act", lambda e, cvb=cvb, Uv=Uv, c=cv_, lo=lo: e.activation(
                        out=cvb[:], in_=Uv[:, lo:lo + 512], func=AF.Identity, bias=cb[:, c:c + 1],
                        scale=cw[:, c, 2:3]), inc=S["act"])
                    dvc = None
                    for (ub, cbuf, c) in ((Ug, cgb, cg_), (Uv, cvb, cv_)):
                        P.op("dve", lambda e, ub=ub, cbuf=cbuf, c=c, lo=lo: e.scalar_tensor_tensor(
                            out=cbuf[:], in0=ub[:, lo - 1:lo + 511], scalar=cw[:, c, 1:2], in1=cbuf[:],
                            op0=ALU.mult, op1=ALU.add), waits=[a])
                        dvc = P.op("dve", lambda e, ub=ub, cbuf=cbuf, c=c, lo=lo: e.scalar_tensor_tensor(
                            out=cbuf[:], in0=ub[:, lo - 2:lo + 510], scalar=cw[:, c, 0:1], in1=cbuf[:],
                            op0=ALU.mult, op1=ALU.add), inc=S["dve"])
                    if th == 1:
                        U.release(ui, dvc)
                    stage_b.append((cgb, cvb, ci, jj, th, dvc))
                    if len(stage_b) > 1:
                        emit_stage_b()
                if jj == nk - 1:
                    while stage_b:
                        g_done = emit_stage_b()
                    xsrc = (lambda ch: io["xT"][ch]) if s == 0 else (lambda ch: io["xw"][ch])
                    xp = XPipe(P, xs, xsrc, list(range(KC)), st_hist)
                    xp.issue()
                    xp.issue()
                    xp.issue()
            else:
                if s == len(SLABS) - 1 and idx == 0 and (final or "rstd_out" in io):
                    old = psr
                    psr = Ring(ps[0:6])
                    psr.free_at = list(old.free_at[0:6])
                    ssq = {"sq": sq, "psA": ps[6], "psB": ps[7], "ones": ones, "pend": [], "ob_read": {},
                           "bank_free": [old.free_at[6], old.free_at[7], h_done]}
                blk_done = emit_resid_block(P, S, psr, xp, xo, wb, wld, idx, nk,
                                            lambda k, th: gsl[:, k, th * 512:(th + 1) * 512],
                                            lambda ch: io["xw"][ch], [g_done], st_hist, ssq=ssq)
            ws.release(wi, blk_done)
            if nxt_load < len(sched):
                pending.append(ws.load(sched[nxt_load][3], sched[nxt_load][4]))
                nxt_load += 1
        rd = None
        if ssq is not None:
            rd = ssq_finish(P, S, ssq, rstd)
            if "rstd_out" in io:
                P.op("sp", lambda e: e.dma_start(out=io["rstd_out"][:], in_=rstd[:]), waits=[rd], inc=S["st"])
                P.wait("sp", (S["st"], S["st"].n))
        if final:
            for w in xo.all_done():
                P.wait("sp", w)
            P.wait("dve", rd)
            for c in range(KC):
                i, xb, fw = xs.acquire()
                ld = P.op("sp", lambda e, xb=xb, c=c: e.dma_start(out=xb[:], in_=io["xw"][c]), waits=[fw],
                          inc=xs.sems[i])
                oi, ob, ofw = xo.acquire()
                dv = P.op("dve", lambda e, xb=xb, ob=ob, c=c: e.scalar_tensor_tensor(
                    out=ob[:], in0=xb[:], scalar=gf[:, c:c + 1], in1=rstd[:], op0=ALU.mult, op1=ALU.mult),
                    waits=[ld, ofw, ssq["ob_read"].get(oi)], inc=S["dve"])
                xs.release(i, dv)
                st = P.op("sp", lambda e, ob=ob, c=c: e.dma_start(out=io["out"][c], in_=ob[:]), waits=[dv],
                          inc=xo.sems[oi])
                xo.release(oi, st)
        for w in xo.all_done():
            P.wait("sp", w)
        P.run(nc, dyn)


GROUPS = [[0, 1, 2, 3], [4, 5, 6, 7]]


def _blocks(w, order=None):
    K, N = w.shape
    a = w.reshape(K // 128, 128, N // 256, 256).transpose(2, 1, 0, 3)
    if order is not None:
        a = a[order]
    return np.ascontiguousarray(a)


def _fm(v):
    return np.ascontiguousarray(v.reshape(-1, 128).T)


def _dram(nc, name, shape, dt, kind):
    return nc.dram_tensor(name, list(shape), dt, kind=kind).ap()


def prog_allgather(P, cc, pairs, waits):
    r = None
    for i, (src, dst) in enumerate(pairs):
        r = P.op("pool", lambda e, src=src, dst=dst: e.collective_compute(
            "AllGather", ALU.bypass, replica_groups=GROUPS, ins=[src], outs=[dst]),
            waits=waits if i == 0 else (), inc=cc)
    return r


def emit_allgather(nc, cc, pairs):
    with nc.Block() as block:
        @block.gpsimd
        def _(g):
            for (src, dst) in pairs:
                g.collective_compute("AllGather", ALU.bypass, replica_groups=GROUPS, ins=[src], outs=[dst]
                                     ).then_inc(cc.h, 1)
                cc.n += 1
            g.wait_ge(cc.h, cc.n)


def build_fused(L=2, stop=99):
    _POOL.clear()
    nc = bass.Bass("TRN2", target_bir_lowering=False, num_devices=NCORES)
    EI, IN = "ExternalInput", "Internal"
    dyn = {}
    xT = _dram(nc, "xT", [KC, 128, T], F32, EI)
    nsl = _dram(nc, "nsl", [128, 12], F32, EI)
    gf = _dram(nc, "gf", [128, KC], F32, EI)
    out = _dram(nc, "out", [KC, 128, T], F32, "ExternalOutput")
    xA = _dram(nc, "xA", [KC, 128, T], F32, IN)
    xB = _dram(nc, "xB", [KC, 128, T], F32, IN)
    qk_send = _dram(nc, "qk_send", [32, 128, T], BF16, IN)
    v_send2 = _dram(nc, "v_send2", [4, T, 512], BF16, IN)
    gT = _dram(nc, "gTd", [16, 128, T], BF16, IN)
    qk_all = _dram(nc, "qk_all", [4 * 32 * 128, T], BF16, IN)
    v_all2 = _dram(nc, "v_all2", [4, 4 * T, 512], BF16, IN)
    a_send3 = _dram(nc, "a_send3", [4, 128, 4096], BF16, IN)
    a_all3 = _dram(nc, "a_all3", [4, 4, 128, 4096], BF16, IN)
    xh_send = _dram(nc, "xh_send", [128, 2 * KC], F32, IN)
    xh_ext = _dram(nc, "xh_ext", [5 * 128, 2 * KC], F32, IN)
    my_qk2 = _dram(nc, "my_qk2", [2, 4, 4, 128, T], BF16, IN)
    my_v = _dram(nc, "my_v", [4096, 512], BF16, IN)
    my_a = _dram(nc, "my_a", [16, 128, T], BF16, IN)
    rstd_c1 = _dram(nc, "rstd_c1", [128, T], F32, IN)
    rstd_c2 = _dram(nc, "rstd_c2", [128, T], F32, IN)
    cc = Sem(nc, None, "cc", 1)

    qk_s2 = qk_send.rearrange("c p t -> (c p) t")
    ag_qkv = [(qk_s2[j * 512:(j + 1) * 512, :], qk_all[j * 2048:(j + 1) * 2048, :]) for j in range(8)]
    ag_qkv += [(v_send2[g], v_all2[g]) for g in range(4)]
    ag_a = [(a_send3[h], a_all3[h].rearrange("rr p t -> (rr p) t")) for h in range(4)]
    xh5 = xh_ext.rearrange("(b p) c -> b p c", p=128)

    with nc.Block() as block0:
        @block0.sync
        def _(sp):
            dyn["r"] = sp.snap(sp.partition_id() % 4, min_val=0, max_val=3)
            dyn["r2"] = sp.snap(dyn["r"] * 2, min_val=0, max_val=6)

    x_in = xT
    for l in range(L):
        wl = [("g1", [128, KC]), ("w_in", [40, 128, KC, 256]), ("normg", [128, 2048]), ("wsT", [128, 16, 128]),
              ("brow", [1, 2048])]
        wl += [("mixgA", [128, 16]), ("mixgB", [128, 4])]
        if stop >= 5:
            wl += [("w_out", [2, 16, 128, 16, 256])]
        if stop >= 7:
            wl += [("g2", [128, KC]), ("w_up", [FC, 128, KC, 256]), ("cw", [128, 172, 3]), ("cb", [128, 172])]
        W = {k: _dram(nc, f"{k}_{l}", shp, F32, EI) for k, shp in wl}
        if stop >= 7:
            for s_, (k0, nk) in enumerate(SLABS):
                W[f"w_dn{s_}"] = _dram(nc, f"w_dn{s_}_{l}", [16, 128, nk, 256], F32, EI)
        ioA = {"xT": x_in, "g1": W["g1"], "w_in": W["w_in"], "normg": W["normg"], "wsT": W["wsT"],
               "brow": W["brow"], "qk": qk_send, "gT": gT, "mixgA": W["mixgA"],
               "v": lambda cb: v_send2[cb // 2].rearrange("(tc p) c -> p tc c", p=128)[
                   :, :, (cb % 2) * 256:(cb % 2) * 256 + 256]}
        if l == 0:
            ioA["zero_dst"] = xh_ext[0:128, :]
        else:
            ioA["rstd_in"] = rstd_c2
        ioA.update({"cc": cc, "ag_qk": ag_qkv[0:8], "ag_v": ag_qkv[8:12]})
        phase_A(nc, ioA, dyn)
        if stop < 2:
            break
        ioB = {
            "pre": [
                (my_qk2.rearrange("jj rr c p t -> jj (rr c p) t"),
                 lambda: qk_all.rearrange("(j x) t -> j x t", j=8)[bass.ds(dyn["r2"], 2)]),
                (my_v[:], lambda: v_all2[bass.ds(dyn["r"], 1)].rearrange("a t c -> (a t) c")),
            ],
            "q": lambda h: my_qk2[0][:, h].rearrange("rr p t -> p rr t"),
            "k": lambda h: my_qk2[1][:, h].rearrange("rr p t -> p rr t"),
            "v": lambda h, d, r: my_v[:, h * 128:(h + 1) * 128].rearrange("(n p r) e -> r p n e", p=128, r=d)[r],
            "nsl": nsl, "a": lambda h: a_send3[h].rearrange("p (tb t) -> p tb t", tb=4),
            "cc": cc, "ag_a": ag_a, "mixgB": W["mixgB"],
        }
        if stop < 3:
            break
        phase_B(nc, ioB, dyn)
        if stop < 4:
            break
        ioC1 = {"xT": x_in,
                "pre": [(my_a.rearrange("h p t -> (h p) t"),
                         lambda: a_all3.rearrange("h rr p (tb t) -> tb (h rr p) t", tb=4)[
                             bass.ds(dyn["r"], 1)].rearrange("a x t -> (a x) t"))],
                "aT": lambda c: my_a[(c % 4) * 4 + c // 4],
                "cc": cc, "ag_xh": [(xh_send[:], xh_ext[128:640, :])],
                "gT": gT, "w_out": W["w_out"], "xo": xA, "rstd_out": rstd_c1,
                "halo_dst": lambda ch: xh_send[:, 2 * ch:2 * ch + 2]}
        if stop < 5:
            break
        phase_C1(nc, ioC1, dyn)
        if stop < 6:
            break
        if stop < 7:
            break
        final = (l == L - 1)
        ioC2 = {"xT": xA, "xh": lambda: xh5[bass.ds(dyn["r"], 1)].rearrange("a p c -> p (a c)"),
                "g2": W["g2"], "w_up": W["w_up"], "cw": W["cw"], "cb": W["cb"], "xw": xB, "cc": cc,
                "rstd_in": rstd_c1}
        if not final:
            ioC2["rstd_out"] = rstd_c2
        for s_ in range(4):
            ioC2[f"w_dn{s_}"] = W[f"w_dn{s_}"]
        if final:
            ioC2["gf"] = gf
            ioC2["out"] = out
        phase_C2(nc, ioC2, final, dyn)
        x_in = xB
    return nc


def _to_xT(x):
    out = []
    for c in range(NCORES):
        b, r = divmod(c, 4)
        xs = x[b, r * T:(r + 1) * T, :]
        out.append(np.ascontiguousarray(xs.T.reshape(KC, 128, T)))
    return out


def kernel(x, attn_norm_g, w_in, sgu_norm_g, w_spatial, b_spatial, mix_norm_g, w_out,
           ffn_norm_g, w_up, conv_w, conv_b, w_down, final_norm_g):
    f = lambda a: np.asarray(a, dtype=np.float32)
    L = 2
    xT = _to_xT(f(x))
    slopes = 2.0 ** (-8.0 * np.arange(1, 17, dtype=np.float64) / 16.0)
    shared = {"gf": _fm(f(final_norm_g))}
    percore = {}
    order = list(range(24)) + list(range(32, 40)) + list(range(24, 32))
    for l in range(L):
        shared[f"g1_{l}"] = _fm(f(attn_norm_g[l]))
        shared[f"w_in_{l}"] = _blocks(f(w_in[l]), order)
        shared[f"normg_{l}"] = np.ascontiguousarray(np.broadcast_to(f(sgu_norm_g[l])[None, :], (128, 2048)))
        shared[f"wsT_{l}"] = np.ascontiguousarray(f(w_spatial[l]).transpose(2, 0, 1))
        shared[f"brow_{l}"] = np.ascontiguousarray(f(b_spatial[l]).reshape(1, 2048))
        mg = _fm(f(mix_norm_g[l]))
        shared[f"mixgA_{l}"] = np.ascontiguousarray(mg[:, 16:32])
        for hg in range(4):
            percore.setdefault(hg, {})[f"mixgB_{l}"] = np.ascontiguousarray(mg[:, 4 * hg:4 * hg + 4])
        wo = f(w_out[l])
        shared[f"w_out_{l}"] = np.stack([_blocks(wo[2048:]), _blocks(wo[:2048])], 0)
        shared[f"g2_{l}"] = _fm(f(ffn_norm_g[l]))
        wu = f(w_up[l])
        shared[f"w_up_{l}"] = np.ascontiguousarray(
            wu.reshape(KC, 128, 2, FC, 128).transpose(3, 1, 0, 2, 4).reshape(FC, 128, KC, 256))
        wd = f(w_down[l])
        for s_, (k0, nk) in enumerate(SLABS):
            shared[f"w_dn{s_}_{l}"] = np.ascontiguousarray(
                wd[k0 * 128:(k0 + nk) * 128, :].reshape(nk, 128, 16, 256).transpose(2, 1, 0, 3))
        shared[f"cw_{l}"] = np.ascontiguousarray(f(conv_w[l]).reshape(3, 172, 128).transpose(2, 1, 0))
        shared[f"cb_{l}"] = np.ascontiguousarray(f(conv_b[l]).reshape(172, 128).T)
    in_maps = []
    for c in range(NCORES):
        hg = c % 4
        nsl = np.zeros((128, 12), np.float32)
        for hl in range(4):
            for di, d in enumerate((1, 4, 16)):
                nsl[:, hl * 3 + di] = np.float32(-slopes[4 * hg + hl] * d)
        m = dict(shared)
        m.update(percore[hg])
        m["xT"] = xT[c]
        m["nsl"] = nsl
        in_maps.append(m)
    nc = build_fused(L)
    res = run_bass_kernel_spmd(nc, in_maps, core_ids=list(range(NCORES))).results
    y = np.empty((2, 4096, D), np.float32)
    for c in range(NCORES):
        b, r = divmod(c, 4)
        y[b, r * T:(r + 1) * T, :] = np.asarray(res[c]["out"]).reshape(D, T).T
    return y
```
